# Optimizing a Trainium2 kernel written in Bass

```python
import math
import numpy as np
import jax
import jax.numpy as jnp
from jax import lax

D_MODEL = 1024
BATCH = 2
SEQ = 8192
DEPTH = 2
DEC_BATCH = 32
DEC_SEQ = 1
PAST_LEN = 16384
PAGE_SIZE = 128

EPS = 1e-6
ROPE_THETA = 500000.0
ROPE_FRACTION = 4
NH_A = 4
DK_A = 128
DV_A = 128
CHUNK_A = 128
NH_B = 4
KVH_B = 2
HD_B = 128
MOBA_BLOCK = 256
MOBA_TOPK = 3
MOBA_QB = 64
NH_C = 8
HD_C = 64
DI_C = NH_C * HD_C
NG_C = 2
DS_C = 128
CONV_C = 4
CONV_DIM_C = DI_C + 2 * NG_C * DS_C
CHUNK_C = 128
SWA_GROUPS = ((128, 1), (512, 4), (2048, 16))
NG_D = 3
HPG_D = 4
HD_D = 64
NH_D = NG_D * HPG_D
SWA_QB = 128
D_FF = 2816
FFN_CONV = 3
L0_SIZES = (NH_A * DK_A, NH_A * DK_A, NH_A * DV_A, NH_A * DV_A, NH_A, NH_A, NH_B * HD_B, KVH_B * HD_B, KVH_B * HD_B)
L0_OUT = NH_A * DV_A + NH_B * HD_B
L1_SIZES = (DI_C, CONV_DIM_C, NH_C, NH_D * HD_D, NH_D * HD_D, NH_D * HD_D)
L1_OUT = DI_C + HPG_D * HD_D

kernel_name = 'hybrid_mlstm_moba_ssd_dilated_step'


def rmsnorm(x, w):
    xf = x.astype(jnp.float32)
    y = xf * lax.rsqrt(jnp.mean(xf * xf, axis=-1, keepdims=True) + EPS)
    return (y * w.astype(jnp.float32)).astype(x.dtype)


def group_rmsnorm(y, w, groups):
    shp = y.shape
    yg = y.astype(jnp.float32).reshape(shp[:-1] + (groups, shp[-1] // groups))
    yg = yg * lax.rsqrt(jnp.mean(yg * yg, axis=-1, keepdims=True) + EPS)
    return yg.reshape(shp) * w.astype(jnp.float32)


def split_cols(p, sizes):
    return jnp.split(p, np.cumsum(sizes)[:-1].tolist(), axis=-1)


def rope_partial(x, pos):
    rd = x.shape[-1] // ROPE_FRACTION
    half = rd // 2
    inv = ROPE_THETA ** (-jnp.arange(half, dtype=jnp.float32) / half)
    ang = pos.astype(jnp.float32)[:, None] * inv[None, :]
    cos = jnp.cos(ang)[None, :, None, :]
    sin = jnp.sin(ang)[None, :, None, :]
    x1 = x[..., :half].astype(jnp.float32)
    x2 = x[..., half:rd].astype(jnp.float32)
    return jnp.concatenate([(x1 * cos - x2 * sin).astype(x.dtype), (x2 * cos + x1 * sin).astype(x.dtype), x[..., rd:]], axis=-1)


def causal_dwconv(u, buf, w, b):
    k = w.shape[0]
    ext = jnp.concatenate([buf.astype(u.dtype), u], axis=1)
    y = lax.conv_general_dilated(ext, w.astype(u.dtype)[:, None, :], window_strides=(1,), padding='VALID',
                                 dimension_numbers=('NWC', 'WIO', 'NWC'), feature_group_count=u.shape[-1])
    return y + b.astype(u.dtype), ext[:, ext.shape[1] - (k - 1):]


def to_chunks(a, size):
    b, s = a.shape[:2]
    a = a.reshape((b, s // size, size) + a.shape[2:])
    return a.transpose((1, 0, 3, 2) + tuple(range(4, a.ndim)))


def from_chunks(a):
    a = a.transpose((1, 0, 3, 2) + tuple(range(4, a.ndim)))
    return a.reshape((a.shape[0], a.shape[1] * a.shape[2]) + a.shape[3:])


def q_blocks(a, qb):
    b, q = a.shape[:2]
    return jnp.moveaxis(a.reshape((b, q // qb, qb) + a.shape[2:]), 1, 0)


def q_unblocks(a):
    a = jnp.moveaxis(a, 0, 1)
    return a.reshape((a.shape[0], a.shape[1] * a.shape[2]) + a.shape[3:])


def mlstm_chunked(q, k, v, ig, fg, c0, n0, m0):
    f32 = jnp.float32
    s, dk = q.shape[1], q.shape[-1]
    size = CHUNK_A if s % CHUNK_A == 0 else s
    causal = jnp.tril(jnp.ones((size, size), dtype=bool))
    xs = (to_chunks(q.astype(f32), size), to_chunks(k.astype(f32) * dk ** -0.5, size), to_chunks(v.astype(f32), size),
          to_chunks(ig.astype(f32), size), to_chunks(jax.nn.log_sigmoid(fg.astype(f32)), size))

    def step(carry, blk):
        c, n, m = carry
        qc, kc, vc, ic, lf = blk
        fcum = jnp.cumsum(lf, axis=-1)
        dmat = jnp.where(causal, fcum[..., :, None] - fcum[..., None, :] + ic[..., None, :], -jnp.inf)
        inter = fcum + m[..., None]
        mt = jnp.maximum(inter, jnp.max(dmat, axis=-1))
        wmat = jnp.einsum('bhtd,bhsd->bhts', qc, kc) * jnp.exp(dmat - mt[..., None])
        a_inter = jnp.exp(inter - mt)
        num = jnp.einsum('bhts,bhsv->bhtv', wmat, vc) + a_inter[..., None] * jnp.einsum('bhtd,bhdv->bhtv', qc, c)
        den = jnp.sum(wmat, axis=-1) + a_inter * jnp.einsum('bhtd,bhd->bht', qc, n)
        h = num / jnp.maximum(jnp.abs(den), jnp.exp(-mt))[..., None]
        f_end = fcum[..., -1]
        w_end = f_end[..., None] - fcum + ic
        m_new = jnp.maximum(f_end + m, jnp.max(w_end, axis=-1))
        w_end = jnp.exp(w_end - m_new[..., None])
        decay = jnp.exp(f_end + m - m_new)
        c_new = decay[..., None, None] * c + jnp.einsum('bhs,bhsd,bhsv->bhdv', w_end, kc, vc)
        n_new = decay[..., None] * n + jnp.einsum('bhs,bhsd->bhd', w_end, kc)
        return (c_new, n_new, m_new), h

    (c1, n1, m1), h = lax.scan(step, (c0.astype(f32), n0.astype(f32), m0.astype(f32)), xs)
    return from_chunks(h), c1, n1, m1


def ssd_chunked(x, dt, a, bm, cm, h0):
    f32 = jnp.float32
    s, nh = x.shape[1], x.shape[2]
    size = CHUNK_C if s % CHUNK_C == 0 else s
    head_group = jnp.arange(nh) // (nh // bm.shape[2])
    causal = jnp.tril(jnp.ones((size, size), dtype=bool))
    xs = (to_chunks(x.astype(f32), size), to_chunks(dt * a, size), to_chunks(dt, size),
          to_chunks(bm.astype(f32)[:, :, head_group], size), to_chunks(cm.astype(f32)[:, :, head_group], size))

    def step(h, blk):
        xc, ac, dc, bc, cc = blk
        acum = jnp.cumsum(ac, axis=-1)
        decay = jnp.exp(jnp.where(causal, acum[..., :, None] - acum[..., None, :], -jnp.inf))
        mmat = jnp.einsum('bhtn,bhsn->bhts', cc, bc) * decay * dc[..., None, :]
        y = jnp.einsum('bhts,bhsp->bhtp', mmat, xc) + jnp.exp(acum)[..., None] * jnp.einsum('bhtn,bhpn->bhtp', cc, h)
        w_end = jnp.exp(acum[..., -1:] - acum) * dc
        h_new = jnp.exp(acum[..., -1])[..., None, None] * h + jnp.einsum('bhs,bhsp,bhsn->bhpn', w_end, xc, bc)
        return h_new, y

    h1, y = lax.scan(step, h0.astype(f32), xs)
    return from_chunks(y), h1


def moba_attend(q, k, v, q_pos):
    f32 = jnp.float32
    b, nq_tot = q.shape[:2]
    lk = k.shape[1]
    grp = NH_B // KVH_B
    nb = -(-lk // MOBA_BLOCK)
    n_full = lk // MOBA_BLOCK
    topk = min(MOBA_TOPK, n_full)
    pad = ((0, 0), (0, nb * MOBA_BLOCK - lk), (0, 0), (0, 0))
    kb = jnp.pad(k, pad).reshape(b, nb, MOBA_BLOCK, KVH_B, HD_B).transpose(0, 3, 1, 2, 4)
    vb = jnp.pad(v, pad).reshape(b, nb, MOBA_BLOCK, KVH_B, HD_B).transpose(0, 3, 1, 2, 4)
    kmean = jnp.mean(kb[:, :, :n_full].astype(f32), axis=3)
    bi = jnp.arange(b)[:, None, None, None]
    hi = (jnp.arange(NH_B) // grp)[None, None, :, None]
    offs = jnp.arange(MOBA_BLOCK, dtype=jnp.int32)
    qb = MOBA_QB if nq_tot % MOBA_QB == 0 else nq_tot
    scale = HD_B ** -0.5

    def one_block(args):
        qc, pc = args
        qf = qc.astype(f32)
        nqb = qc.shape[1]
        own = pc // MOBA_BLOCK
        own_sel = jnp.broadcast_to(own[None, :, None, None], (b, nqb, NH_B, 1))
        if topk > 0:
            bscore = jnp.einsum('bqjgd,bjnd->bqjgn', qf.reshape(b, nqb, KVH_B, grp, HD_B), kmean).reshape(b, nqb, NH_B, n_full)
            past = jnp.arange(n_full)[None, :] < own[:, None]
            bscore = jnp.where(past[None, :, None, :], bscore, -jnp.inf)
            sel = lax.top_k(bscore, topk)[1].astype(jnp.int32)
            blocks = jnp.concatenate([sel, own_sel], axis=-1)
            slot_ok = jnp.concatenate([jnp.arange(topk)[None, :] < own[:, None], jnp.ones((nqb, 1), bool)], axis=-1)
        else:
            blocks = own_sel
            slot_ok = jnp.ones((nqb, 1), bool)
        kg = kb[bi, hi, blocks]
        vg = vb[bi, hi, blocks]
        kpos = blocks[..., None] * MOBA_BLOCK + offs
        ok = (kpos <= pc[None, :, None, None, None]) & slot_ok[None, :, None, :, None]
        s = jnp.einsum('bqhd,bqhtkd->bqhtk', qf, kg.astype(f32)) * scale
        s = jnp.where(ok, s, -jnp.inf)
        p = jax.nn.softmax(s.reshape(s.shape[:3] + (-1,)), axis=-1).reshape(s.shape)
        return jnp.einsum('bqhtk,bqhtkd->bqhd', p, vg.astype(f32))

    out = lax.map(one_block, (q_blocks(q, qb), q_pos.reshape(-1, qb)))
    return q_unblocks(out)


def dilated_attend(q, k, v, q_pos, k_start, window, dil):
    f32 = jnp.float32
    nq_tot, hd = q.shape[1], q.shape[-1]
    dist = jnp.arange(window // dil + 1, dtype=jnp.int32) * dil
    qb = SWA_QB if nq_tot % SWA_QB == 0 else nq_tot
    scale = hd ** -0.5

    def one_block(args):
        qc, pc = args
        idx = pc[:, None] - dist[None, :] - k_start
        ok = idx >= 0
        idx = jnp.maximum(idx, 0)
        kg = k[:, idx].astype(f32)
        vg = v[:, idx].astype(f32)
        s = jnp.einsum('bqhd,bqnhd->bqhn', qc.astype(f32), kg) * scale
        s = jnp.where(ok[None, :, None, :], s, -jnp.inf)
        lse = jax.nn.logsumexp(s, axis=-1)
        o = jnp.einsum('bqhn,bqnhd->bqhd', jnp.exp(s - lse[..., None]), vg)
        return o, lse

    o, lse = lax.map(one_block, (q_blocks(q, qb), q_pos.reshape(-1, qb)))
    return q_unblocks(o), q_unblocks(lse)


def mixer_even(h, pos, state, w_in, b_gates, norm_mlstm, w_out):
    f32 = jnp.float32
    b, s, _ = h.shape
    qa, ka, va, oa, ia, fa, qb, kb, vb = split_cols(h @ w_in, L0_SIZES)
    if state is None:
        c0 = jnp.zeros((b, NH_A, DK_A, DV_A), f32)
        n0 = jnp.zeros((b, NH_A, DK_A), f32)
        m0 = jnp.zeros((b, NH_A), f32)
        k_past = None
        v_past = None
    else:
        c0, n0, m0, k_past, v_past = state
    ig = ia.astype(f32) + b_gates[:NH_A].astype(f32)
    fg = fa.astype(f32) + b_gates[NH_A:].astype(f32)
    ha, c1, n1, m1 = mlstm_chunked(qa.reshape(b, s, NH_A, DK_A), ka.reshape(b, s, NH_A, DK_A),
                                   va.reshape(b, s, NH_A, DV_A), ig, fg, c0, n0, m0)
    ha = jax.nn.sigmoid(oa.reshape(b, s, NH_A, DV_A).astype(f32)) * ha
    ha = group_rmsnorm(ha.reshape(b, s, NH_A * DV_A), norm_mlstm, NH_A)
    qr = rope_partial(qb.reshape(b, s, NH_B, HD_B), pos)
    kr = rope_partial(kb.reshape(b, s, KVH_B, HD_B), pos)
    vr = vb.reshape(b, s, KVH_B, HD_B)
    if k_past is None:
        k_all, v_all = kr, vr
    else:
        k_all = jnp.concatenate([k_past.astype(kr.dtype), kr], axis=1)
        v_all = jnp.concatenate([v_past.astype(vr.dtype), vr], axis=1)
    hb = moba_attend(qr, k_all, v_all, pos).reshape(b, s, NH_B * HD_B)
    y = jnp.concatenate([ha, hb], axis=-1).astype(h.dtype) @ w_out
    dt = h.dtype
    return y, (c1.astype(dt), n1.astype(dt), m1.astype(dt), kr, vr)


def mixer_odd(h, pos, state, w_in, conv_w, conv_b, dt_bias, a_log, d_skip, norm_ssd, w_out):
    f32 = jnp.float32
    b, s, _ = h.shape
    z, xbc, dtr, qd, kd, vd = split_cols(h @ w_in, L1_SIZES)
    if state is None:
        h0 = jnp.zeros((b, NH_C, HD_C, DS_C), f32)
        conv0 = jnp.zeros((b, CONV_C - 1, CONV_DIM_C), h.dtype)
        bufs = (None,) * NG_D
    else:
        h0, conv0, bufs = state
    xbc, conv1 = causal_dwconv(xbc, conv0, conv_w, conv_b)
    xbc = jax.nn.silu(xbc.astype(f32))
    xs, bm, cm = split_cols(xbc, (DI_C, NG_C * DS_C, NG_C * DS_C))
    xs = xs.reshape(b, s, NH_C, HD_C)
    dt = jax.nn.softplus(dtr.astype(f32) + dt_bias.astype(f32))
    a = -jnp.exp(a_log.astype(f32))
    ys, h1 = ssd_chunked(xs, dt, a, bm.reshape(b, s, NG_C, DS_C), cm.reshape(b, s, NG_C, DS_C), h0)
    ys = ys + d_skip.astype(f32)[:, None] * xs
    yc = group_rmsnorm(ys.reshape(b, s, DI_C) * jax.nn.silu(z.astype(f32)), norm_ssd, NG_C)
    qr = rope_partial(qd.reshape(b, s, NH_D, HD_D), pos)
    kr = rope_partial(kd.reshape(b, s, NH_D, HD_D), pos)
    vr = vd.reshape(b, s, NH_D, HD_D)
    outs, lses, rows = [], [], []
    for g, (win, dil) in enumerate(SWA_GROUPS):
        hs = slice(g * HPG_D, (g + 1) * HPG_D)
        kg, vg = kr[:, :, hs], vr[:, :, hs]
        kv_new = jnp.stack([kg, vg], axis=2)
        if bufs[g] is None:
            k_seq, v_seq, k_start = kg, vg, 0
            rows.append(kv_new[:, s - min(win, s):])
        else:
            buf = bufs[g].astype(kg.dtype)
            k_seq = jnp.concatenate([buf[:, :, 0], kg], axis=1)
            v_seq = jnp.concatenate([buf[:, :, 1], vg], axis=1)
            k_start = pos[0] - buf.shape[1]
            rows.append(kv_new)
        o, lse = dilated_attend(qr[:, :, hs], k_seq, v_seq, pos, k_start, win, dil)
        outs.append(o)
        lses.append(lse)
    wts = jax.nn.softmax(jnp.stack(lses), axis=0)
    od = jnp.sum(wts[..., None] * jnp.stack(outs), axis=0).reshape(b, s, HPG_D * HD_D)
    y = jnp.concatenate([yc, od], axis=-1).astype(h.dtype) @ w_out
    return y, (h1.astype(h.dtype), conv1, rows[0], rows[1], rows[2])


def conv_ffn(h, buf, w_up, conv_w, conv_b, w_down):
    u, new_buf = causal_dwconv(h @ w_up, buf, conv_w, conv_b)
    gate, val = jnp.split(u, 2, axis=-1)
    return (jax.nn.silu(gate) * val) @ w_down, new_buf


def run_trunk(x, pos0, even_state, odd_state, ffn_state, even_w, odd_w, ffn_w, norm_mix, norm_ffn, norm_final):
    b, s, _ = x.shape
    pos = pos0 + jnp.arange(s, dtype=jnp.int32)
    ffn_up, ffn_conv_w, ffn_conv_b, ffn_down = ffn_w
    mixer_states = []
    ffn_bufs = []
    for layer in range(DEPTH):
        hn = rmsnorm(x, norm_mix[layer])
        if layer % 2 == 0:
            mix, st = mixer_even(hn, pos, even_state, *even_w)
        else:
            mix, st = mixer_odd(hn, pos, odd_state, *odd_w)
        mixer_states.append(st)
        x = x + mix
        buf = jnp.zeros((b, FFN_CONV - 1, 2 * D_FF), x.dtype) if ffn_state is None else ffn_state[layer]
        f, fbuf = conv_ffn(rmsnorm(x, norm_ffn[layer]), buf, ffn_up[layer], ffn_conv_w[layer], ffn_conv_b[layer], ffn_down[layer])
        x = x + f
        ffn_bufs.append(fbuf)
    return rmsnorm(x, norm_final), mixer_states[0], mixer_states[1], jnp.stack(ffn_bufs)


def setup_inputs(seed: int = 0) -> dict:
    key = jax.random.key(seed)
    keys = iter(jax.random.split(key, 48))
    f32 = jnp.float32
    n_pages = PAST_LEN // PAGE_SIZE
    n_used = DEC_BATCH * n_pages
    n_pool = n_used + max(1, n_used // 4)

    def normal(shape, scale):
        return scale * jax.random.normal(next(keys), shape, f32)

    def gain(shape):
        return 1.0 + 0.01 * jax.random.normal(next(keys), shape, f32)

    def uniform(shape, lo, hi):
        return jax.random.uniform(next(keys), shape, f32, lo, hi)

    x_prompt = normal((BATCH, SEQ, D_MODEL), 1.0)
    x_sample = normal((DEC_BATCH, DEC_SEQ, D_MODEL), 1.0)
    state_l0_mlstm_c = normal((DEC_BATCH, NH_A, DK_A, DV_A), 0.5)
    state_l0_mlstm_n = normal((DEC_BATCH, NH_A, DK_A), 0.5)
    state_l0_mlstm_m = normal((DEC_BATCH, NH_A), 1.0)
    cache_l0_moba_k = normal((n_pool, PAGE_SIZE, KVH_B, HD_B), 1.0)
    cache_l0_moba_v = normal((n_pool, PAGE_SIZE, KVH_B, HD_B), 1.0)
    state_l1_ssd_h = normal((DEC_BATCH, NH_C, HD_C, DS_C), 0.5)
    state_l1_ssd_conv = normal((DEC_BATCH, CONV_C - 1, CONV_DIM_C), 1.0)
    cache_l1_swa_kv0 = normal((DEC_BATCH, min(SWA_GROUPS[0][0], PAST_LEN), 2, HPG_D, HD_D), 1.0)
    cache_l1_swa_kv1 = normal((DEC_BATCH, min(SWA_GROUPS[1][0], PAST_LEN), 2, HPG_D, HD_D), 1.0)
    cache_l1_swa_kv2 = normal((DEC_BATCH, min(SWA_GROUPS[2][0], PAST_LEN), 2, HPG_D, HD_D), 1.0)
    state_ffn_conv = normal((DEPTH, DEC_BATCH, FFN_CONV - 1, 2 * D_FF), 1.0)
    page_table = jax.random.permutation(next(keys), n_pool)[:n_used].reshape(DEC_BATCH, n_pages).astype(jnp.int32)
    dt0 = jnp.exp(uniform((NH_C,), math.log(1e-3), math.log(1e-1)))
    return {
        'x_prompt': x_prompt,
        'x_sample': x_sample,
        'state_l0_mlstm_c': state_l0_mlstm_c,
        'state_l0_mlstm_n': state_l0_mlstm_n,
        'state_l0_mlstm_m': state_l0_mlstm_m,
        'cache_l0_moba_k': cache_l0_moba_k,
        'cache_l0_moba_v': cache_l0_moba_v,
        'state_l1_ssd_h': state_l1_ssd_h,
        'state_l1_ssd_conv': state_l1_ssd_conv,
        'cache_l1_swa_kv0': cache_l1_swa_kv0,
        'cache_l1_swa_kv1': cache_l1_swa_kv1,
        'cache_l1_swa_kv2': cache_l1_swa_kv2,
        'state_ffn_conv': state_ffn_conv,
        'page_table': page_table,
        'norm_mix': gain((DEPTH, D_MODEL)),
        'norm_ffn': gain((DEPTH, D_MODEL)),
        'norm_final': gain((D_MODEL,)),
        'w_in_l0': normal((D_MODEL, sum(L0_SIZES)), D_MODEL ** -0.5),
        'b_gates_l0': jnp.concatenate([normal((NH_A,), 0.1), uniform((NH_A,), 3.0, 6.0)]),
        'norm_mlstm_l0': gain((NH_A * DV_A,)),
        'w_out_l0': normal((L0_OUT, D_MODEL), L0_OUT ** -0.5),
        'w_in_l1': normal((D_MODEL, sum(L1_SIZES)), D_MODEL ** -0.5),
        'conv_w_l1': normal((CONV_C, CONV_DIM_C), CONV_C ** -0.5),
        'conv_b_l1': normal((CONV_DIM_C,), 0.01),
        'dt_bias_l1': dt0 + jnp.log(-jnp.expm1(-dt0)),
        'a_log_l1': jnp.log(uniform((NH_C,), 1.0, 16.0)),
        'd_skip_l1': 1.0 + 0.1 * jax.random.normal(next(keys), (NH_C,), f32),
        'norm_ssd_l1': gain((DI_C,)),
        'w_out_l1': normal((L1_OUT, D_MODEL), L1_OUT ** -0.5),
        'ffn_up': normal((DEPTH, D_MODEL, 2 * D_FF), D_MODEL ** -0.5),
        'ffn_conv_w': normal((DEPTH, FFN_CONV, 2 * D_FF), FFN_CONV ** -0.5),
        'ffn_conv_b': normal((DEPTH, 2 * D_FF), 0.01),
        'ffn_down': normal((DEPTH, D_FF, D_MODEL), D_FF ** -0.5),
    }


def reference(x_prompt, x_sample, state_l0_mlstm_c, state_l0_mlstm_n, state_l0_mlstm_m, cache_l0_moba_k,
              cache_l0_moba_v, state_l1_ssd_h, state_l1_ssd_conv, cache_l1_swa_kv0, cache_l1_swa_kv1,
              cache_l1_swa_kv2, state_ffn_conv, page_table, norm_mix, norm_ffn, norm_final, w_in_l0, b_gates_l0,
              norm_mlstm_l0, w_out_l0, w_in_l1, conv_w_l1, conv_b_l1, dt_bias_l1, a_log_l1, d_skip_l1, norm_ssd_l1,
              w_out_l1, ffn_up, ffn_conv_w, ffn_conv_b, ffn_down):
    n_seq, n_pages = page_table.shape
    past_len = n_pages * PAGE_SIZE
    k_past = cache_l0_moba_k[page_table].reshape(n_seq, past_len, KVH_B, HD_B)
    v_past = cache_l0_moba_v[page_table].reshape(n_seq, past_len, KVH_B, HD_B)
    even_w = (w_in_l0, b_gates_l0, norm_mlstm_l0, w_out_l0)
    odd_w = (w_in_l1, conv_w_l1, conv_b_l1, dt_bias_l1, a_log_l1, d_skip_l1, norm_ssd_l1, w_out_l1)
    ffn_w = (ffn_up, ffn_conv_w, ffn_conv_b, ffn_down)
    y_prompt, ev_p, od_p, ffn_p = run_trunk(x_prompt, 0, None, None, None, even_w, odd_w, ffn_w,
                                            norm_mix, norm_ffn, norm_final)
    even_state = (state_l0_mlstm_c, state_l0_mlstm_n, state_l0_mlstm_m, k_past, v_past)
    odd_state = (state_l1_ssd_h, state_l1_ssd_conv, (cache_l1_swa_kv0, cache_l1_swa_kv1, cache_l1_swa_kv2))
    y_sample, ev_s, od_s, ffn_s = run_trunk(x_sample, past_len, even_state, odd_state, state_ffn_conv, even_w, odd_w,
                                            ffn_w, norm_mix, norm_ffn, norm_final)
    c_p, n_p, m_p, k_p, v_p = ev_p
    c_s, n_s, m_s, k_s, v_s = ev_s
    h_p, conv_p, sw0_p, sw1_p, sw2_p = od_p
    h_s, conv_s, sw0_s, sw1_s, sw2_s = od_s
    return (y_prompt, y_sample, c_p, c_s, n_p, n_s, m_p, m_s, k_p, k_s, v_p, v_s, h_p, h_s, conv_p, conv_s,
            sw0_p, sw0_s, sw1_p, sw1_s, sw2_p, sw2_s, ffn_p, ffn_s)
```

```python
import math
import os
SKIP = os.environ.get('SKIP', '').split(',')
from contextlib import ExitStack

import numpy as np
import ml_dtypes

import concourse.bass as bass
import concourse.mybir as mybir
from concourse.bass_utils import run_bass_kernel_spmd

F32 = mybir.dt.float32
BF16 = mybir.dt.bfloat16
I32 = mybir.dt.int32
ALU = mybir.AluOpType
AF = mybir.ActivationFunctionType
AX = mybir.AxisListType

EPS = 1e-6
NEG = -30000.0


class Prog:
    ENGS = ("pe", "dve", "act", "pool", "sp")

    def __init__(self, nc, stack):
        self.nc = nc
        self.stack = stack
        self.streams = {e: [] for e in self.ENGS}
        self.semh = {}
        self.count = {}
        self.known = {e: {} for e in self.ENGS}
        self.lastw = {}
        self.readers = {}
        for e in ("pe", "dve", "act", "pool"):
            self._mksem("e:" + e)
        self.n_ops = 0

    def _mksem(self, name):
        if name not in self.semh:
            self.semh[name] = self.stack.enter_context(self.nc.semaphore("s%d" % len(self.semh)))
            self.count[name] = 0
        return name

    def _dmasem(self, key):
        if not hasattr(self, "keysem"):
            self.keysem, self.freesem, self.nd = {}, [], 0
        if key not in self.keysem:
            if self.freesem:
                self.keysem[key] = self.freesem.pop()
            else:
                self.nd += 1
                self.keysem[key] = self._mksem("d:%d" % self.nd)
        return self.keysem[key]

    def _deps(self, eng, reads, writes):
        deps = {}
        for k in list(reads) + list(writes):
            lw = self.lastw.get(k)
            if lw is not None:
                deps[lw[0]] = max(deps.get(lw[0], 0), lw[1])
        for k in writes:
            for (s, v) in self.readers.get(k, ()):
                deps[s] = max(deps.get(s, 0), v)
        waits = []
        kn = self.known[eng]
        for s, v in deps.items():
            if eng == "pe" and s == "e:pe":
                continue
            if kn.get(s, 0) >= v:
                continue
            kn[s] = v
            waits.append((s, v))
        return waits

    def _update(self, ident, reads, writes):
        for k in writes:
            self.lastw[k] = ident
            self.readers[k] = []
        for k in reads:
            if k in writes:
                continue
            self.readers.setdefault(k, []).append(ident)

    @staticmethod
    def _excl(reads, writes):
        xr = [k for k in reads if isinstance(k, tuple) and isinstance(k[0], str) and k[0].startswith("ps")]
        if xr:
            writes = list(writes) + xr
            reads = [k for k in reads if k not in xr]
        return reads, writes

    def op(self, eng, name, reads=(), writes=(), **kw):
        reads, writes = self._excl(reads, writes)
        waits = self._deps(eng, reads, writes)
        s = "e:" + eng
        self.count[s] += 1
        self.streams[eng].append((waits, [(name, kw)], s, 1))
        self._update((s, self.count[s]), reads, writes)
        self.n_ops += 1

    def dma(self, q, pairs, reads, writes, key):
        if isinstance(pairs, tuple):
            pairs = [pairs]
        fns = []
        for p in pairs:
            kw = dict(p[2]) if len(p) > 2 else {}
            name = kw.pop("_op", "dma_start")
            fns.append((name, dict(out=p[0], in_=p[1], **kw)))
        waits = self._deps(q, reads, writes)
        if q == "pool":
            if not hasattr(self, "poolsem"):
                self.poolsem = {}
            if key not in self.poolsem:
                self.poolsem[key] = self._mksem("dp:%d" % len(self.poolsem))
            s = self.poolsem[key]
        else:
            s = self._dmasem(key)
        self.count[s] += 16 * len(fns)
        self.streams[q].append((waits, list(fns), s, 16))
        self._update((s, self.count[s]), reads, writes)
        self.n_ops += len(fns)

    def barrier(self):
        snap = dict(self.count)
        for e in self.ENGS:
            waits = []
            for s, v in snap.items():
                if v > 0 and self.known[e].get(s, 0) < v and not (s == "e:pe" and e == "pe"):
                    self.known[e][s] = v
                    waits.append((s, v))
            self.streams[e].append((waits, [], None, 0))
        self.lastw.clear()
        self.readers.clear()
        if hasattr(self, "keysem"):
            self.freesem.extend(self.keysem.values())
            self.keysem.clear()

    def finish(self):
        self.barrier()

    def emit(self):
        nc = self.nc
        semh = self.semh

        def mk(ename):
            def body(eng):
                for waits, fns, sem, inc in self.streams[ename]:
                    for (s, v) in waits:
                        eng.wait_ge(semh[s], v)
                    for (name, kw) in fns:
                        ins = getattr(eng, name)(**kw)
                        ins.then_inc(semh[sem], inc)
            return body

        with nc.Block() as block:
            block.tensor(mk("pe"))
            block.vector(mk("dve"))
            block.scalar(mk("act"))
            block.gpsimd(mk("pool"))
            block.sync(mk("sp"))


class Ring:
    def __init__(self, name, tiles):
        self.name = name
        self.tiles = tiles
        self.i = 0

    def next(self):
        j = self.i % len(self.tiles)
        self.i += 1
        return self.tiles[j], (self.name, j)


def sb(stack, nc, name, shape, dtype):
    return stack.enter_context(nc.sbuf_tensor(name, list(shape), dtype))


D = 1024
L0_COLS = dict(qa=0, ka=512, va=1024, oa=1536, ia=2048, fa=2052, qb=2056, kb=2568, vb=2824)
N_L0 = 3080


class Cfg:
    def __init__(self, S=8192, debug=False, phases=("A0",), lvl=99, NPG=128, NPOOL=5120):
        self.lvl = lvl
        self.NPG = NPG
        self.NPOOL = NPOOL
        self.S = S
        self.debug = debug
        self.phases = phases
        self.G = S // 512


def rope_tables(pos, hd):
    rd = hd // 4
    half = rd // 2
    inv = (500000.0 ** (-(np.arange(half, dtype=np.float32) / np.float32(half)))).astype(np.float32)
    ang = pos.astype(np.float32)[None, :] * inv[:, None]
    cos = np.cos(ang).astype(np.float32)
    sin = np.sin(ang).astype(np.float32)
    return np.concatenate([cos, cos], 0), np.concatenate([sin, sin], 0)


def build(cfg):
    nc = bass.Bass("TRN2", target_bir_lowering=False)
    S, G = cfg.S, cfg.G
    stack = ExitStack()
    T = {}

    def din(name, shape, dt=F32):
        T[name] = nc.dram_tensor(name, list(shape), dt, kind="ExternalInput").ap()
        return T[name]

    def dout(name, shape, dt=F32):
        T[name] = nc.dram_tensor(name, list(shape), dt, kind="ExternalOutput").ap()
        return T[name]

    def dscr(name, shape, dt):
        kind = "ExternalOutput" if cfg.debug else "Internal"
        T[name] = nc.dram_tensor(name, list(shape), dt, kind=kind).ap()
        return T[name]

    x = din("x", [S, D])
    w_in0 = din("w_in_l0", [D, N_L0])
    g_mix0 = din("g_mix0", [128, 8])
    ident_f = din("ident_f", [128, 128])
    ones_f = din("ones_f", [128, 128])
    cos32 = din("cos32", [32, S])
    sin32 = din("sin32", [32, S])
    cos32q = din("cos32q", [32, S])
    sin32q = din("sin32q", [32, S])
    din("b_gates0", [4, 2])
    din("mlstm_m0", [4, 1])
    din("mlstm_c0", [4, 128, 128])
    din("mlstm_n0", [4, 128])
    din("norm_mlstm", [1, 512])
    din("cbias", [128, 128])
    ident_b = din("ident_b", [128, 128], BF16)
    din("Eall", [32, 32, 128], BF16)
    din("triT", [128, 128], BF16)
    din("triR", [128, 128], BF16)
    din("w_out0", [1024, 1024])
    din("w_out1", [768, 1024])
    for L_ in range(2):
        din("ffn_up%d" % L_, [1024, 5632])
        din("ffn_down%d" % L_, [2816, 1024])
        din("ffn_cw%d" % L_, [128, 3, 44])
        din("ffn_cb%d" % L_, [128, 44])
        din("g_ffn%d" % L_, [128, 8])
        din("ffn_tl%d" % L_, [128, 2, 44])
    din("g_fin", [128, 8])
    din("w_in_l1", [1024, N_L1])
    din("g_mix1", [128, 8])
    din("ssd_cw", [128, 4, 8])
    din("ssd_cb", [128, 8])
    din("ssd_tl", [128, 3, 8])
    for n_ in ("cosD", "sinD", "cosDq", "sinDq"):
        din(n_, [128, S])
    din("dt_bias", [8, 1])
    din("a_log", [8, 1])
    din("d_skip", [1, 8])
    din("norm_ssd", [1, 512])
    din("ssd_h0", [8, 64, 128])
    if "SMP" in cfg.phases:
        NS = 4
        din("xs", [NS, 1024]); din("rope_s", [NS, 4, 16])
        din("mlstm_c0_s", [NS, 4, 128, 128]); din("mlstm_n0_s", [NS, 4, 128]); din("mlstm_m0_s", [NS * 4, 1])
        din("bg16", [16, 2]); din("nw16", [16, 128]); din("sel2", [16, 2]); din("hv32", [32, 3])
        din("cache_k", [cfg.NPOOL * 128, 256]); din("cache_v", [cfg.NPOOL * 128, 256])
        din("page_table_s", [NS, cfg.NPG], I32); din("iota_p", [128, 1])
        for L_ in range(2):
            din("ffn_st%d" % L_, [NS, 2, 5632]); din("ffn_cwr%d" % L_, [3, 5632]); din("ffn_cbr%d" % L_, [1, 5632])
            dout("ffn_conv_s%d" % L_, [NS, 2, 5632])
        din("ssd_conv_st", [NS, 3, 1024]); din("ssd_cwr", [4, 1024]); din("ssd_cbr", [1, 1024]); din("ssd_h0_s", [NS, 8, 64, 128])
        for gi_, (win_, dil_) in enumerate(SWA):
            din("swa_c%d" % gi_, [NS, win_, 2, 4, 64])
            dout("swa_s%d" % gi_, [NS, 2, 4, 64])
        dout("y_s", [NS, 1024]); dout("mlstm_c_s", [NS, 4, 128, 128]); dout("mlstm_n_s", [NS, 4, 128]); dout("mlstm_m_s", [NS * 4, 1])
        dout("moba_k_s", [NS, 2, 128]); dout("moba_v_s", [NS, 2, 128]); dout("ssd_h_s", [NS, 8, 64, 128]); dout("ssd_conv_s", [NS, 3, 1024])
        for n_, sh_ in (("ZP0", [NS, N_L0]), ("ZQB", [NS, 512]), ("ZKB", [NS, 256]), ("MIXS0", [NS, 1024]), ("ZOS", [NS, 4, 257]), ("ZXC", [NS, 1024]),
                        ("ZZS", [NS, 512]), ("ZDT", [NS, 8]), ("ZQD", [NS, 768]), ("ZKD", [NS, 768]), ("ZVD", [NS, 768]), ("ZYS", [NS, 512]),
                        ("ZDS", [NS, 260]), ("MIXS1", [NS, 768])):
            dscr(n_, sh_, F32)
    dout("ssd_h", [8, 64, 128])
    dout("ssd_conv", [3, 1024])
    for gi_, (win_, dil_) in enumerate(SWA):
        dout("swa_kv%d" % gi_, [min(win_, S), 2, 4, 64])
    dout("ffn_conv0", [2, 5632])
    dout("ffn_conv1", [2, 5632])
    dout("y", [S, 1024])
    dout("mlstm_c", [4, 128, 128])
    dout("mlstm_n", [4, 128])
    dout("mlstm_m", [4, 1])
    mk_out = dout("moba_k", [S, 2, 128])
    mv_out = dout("moba_v", [S, 2, 128])
    XT = dscr("XT", [D, S], F32)
    QA = dscr("QA", [4, 128, S], BF16)
    KA = dscr("KA", [4, 128, S], BF16)
    VA = dscr("VA", [S, 512], BF16)
    OA = dscr("OA", [S, 512], F32)
    dscr("GAI", [4, S], F32)
    dscr("GAF", [4, S], F32)
    dscr("KAT", [S, 512], BF16)
    QB = dscr("QB", [4, 128, S], BF16)
    KB = dscr("KB", [2, 128, S], BF16)
    VB = dscr("VB", [S, 256], BF16)
    KM = dscr("KM", [128, 2, S // 256], F32)
    dscr("RQ", [4, 4, S], F32)
    dscr("GR", [4, S], F32)
    dscr("DEC", [1, 4 * (S // 128)], F32)
    dscr("MIX0", [1024, S], BF16)
    dscr("X1T", [1024, S], F32)
    dscr("XBC", [1024, S], BF16)
    dscr("DTR", [8, S], F32)
    dscr("QD", [6, 128, S], BF16)
    dscr("KD", [6, 128, S], BF16)
    dscr("VD", [S, 768], BF16)
    dscr("ZS", [S, 512], F32)
    dscr("SQ", [8, 4, S], F32)
    dscr("SR", [8, S], F32)
    dscr("SDEC", [1, 8 * (S // 128)], F32)
    dscr("MIX1", [768, S], BF16)

    with stack:
        P = Prog(nc, stack)
        ps = [stack.enter_context(nc.psum_tensor("ps%d" % i, [128, 512], F32)) for i in range(8)]
        C = dict(ps=ps, psF=Ring("psF", ps[0:4]), psT=Ring("psT", ps[4:6]), psM=Ring("psM", ps[6:8]))
        C["identf"] = identf = sb(stack, nc, "identf", [128, 128], F32)
        C["onesf"] = onesf = sb(stack, nc, "onesf", [128, 128], F32)
        C["epsc"] = epsc = sb(stack, nc, "epsc", [128, 1], F32)
        P.dma("sp", (identf[:], ident_f), [], ["identf"], "identf")
        P.dma("sp", (onesf[:], ones_f), [], ["onesf"], "onesf")
        C["identb"] = identb = sb(stack, nc, "identb", [128, 128], BF16)
        P.dma("sp", (identb[:], ident_b), [], ["identb"], "identb")
        P.op("pool", "memset", [], ["epsc"], ap=epsc[:], constant=EPS)

        if "A0" in cfg.phases:
            phase_A0(nc, P, cfg, T, C)
        if "A1" in cfg.phases:
            phase_A1(nc, P, cfg, T, C)
        if "A2" in cfg.phases:
            phase_A2(nc, P, cfg, T, C)
        if "A3" in cfg.phases:
            phase_FFN(nc, P, cfg, T, C, 0, 8, "MIX0", "XT", "X1T")
        if "B0" in cfg.phases:
            phase_B0(nc, P, cfg, T, C)
        if "B1" in cfg.phases:
            phase_B1(nc, P, cfg, T, C)
        if "B2" in cfg.phases:
            phase_B2(nc, P, cfg, T, C)
        if "B3" in cfg.phases:
            phase_FFN(nc, P, cfg, T, C, 1, 6, "MIX1", "X1T", None, final=True)
        if "SMP" in cfg.phases:
            phase_sample(nc, P, cfg, T, C)
        P.finish()
        P.emit()
    return nc


def phase_A0(nc, P, cfg, T, C):
    S, G = cfg.S, cfg.G
    psF, psT, psM = C["psF"], C["psT"], C["psM"]
    identf, onesf, epsc = C["identf"], C["onesf"], C["epsc"]
    x = T["x"]
    with ExitStack() as ph:
        w = sb(ph, nc, "w0", [128, 8, N_L0], BF16)
        wr = sb(ph, nc, "w0r", [128, 8, 6, 32], BF16)
        gain = sb(ph, nc, "gain0", [128, 8], F32)
        xtok = [sb(ph, nc, "xtok%d" % i, [128, 4, D], F32) for i in range(2)]
        xT = [sb(ph, nc, "xT%d" % i, [128, 8, 512], F32) for i in range(2)]
        sq = sb(ph, nc, "sq", [128, 8, 512], F32)
        rstd = sb(ph, nc, "rstd", [128, 512], F32)
        hnT = [sb(ph, nc, "hnT%d" % i, [128, 8, 512], BF16) for i in range(2)]
        cs = [sb(ph, nc, "cs%d" % i, [32, 4, 512], F32) for i in range(2)]
        ev_bf = Ring("evbf", [sb(ph, nc, "evbf%d" % i, [128, 512], BF16) for i in range(4)])
        ev_f = Ring("evf", [sb(ph, nc, "evf%d" % i, [128, 512], F32) for i in range(3)])
        rt = Ring("rt", [sb(ph, nc, "rt%d" % i, [32, 2, 512], F32) for i in range(2)])
        ktok = Ring("ktok", [sb(ph, nc, "ktok%d" % i, [128, 4, 256], F32) for i in range(2)])
        vtokf = Ring("vtokf", [sb(ph, nc, "vtokf%d" % i, [128, 512], F32) for i in range(3)])
        vtokb = Ring("vtokb", [sb(ph, nc, "vtokb%d" % i, [128, 512], BF16) for i in range(3)])
        km = sb(ph, nc, "km", [128, 2, S // 256], F32)

        if 'wcast' not in SKIP:
            P.dma("pool", [(w[:, c, :], T["w_in_l0"][c * 128:(c + 1) * 128, :]) for c in range(8)], [], ["w0"], "w0")
        P.dma("sp", (gain[:], T["g_mix0"]), [], ["gain0"], "gain0")
        w6 = w[:, :, 2056:2056 + 768].rearrange("p c (j d) -> p c j d", j=6)
        if 'wrot' not in SKIP:
          P.op("dve", "tensor_scalar", ["w0"], ["w0r"], out=wr[:, :, :, 0:16], in0=w6[:, :, :, 16:32], scalar1=-1.0,
             scalar2=None, op0=ALU.mult)
        if 'wrot' not in SKIP:
          P.op("dve", "tensor_copy", ["w0", "w0r"], ["w0r"], out=wr[:, :, :, 16:32], in_=w6[:, :, :, 0:16])

        def load_group(g):
            b = g % 2
            tk = slice(g * 512, (g + 1) * 512)
            P.dma("sp", (xtok[b][:], x[tk, :].rearrange("(t p) d -> p t d", p=128)), [], [("xtok", b)], ("xtok", b))
            if 'cs' not in SKIP:
              P.dma("sp", [(cs[b][:, i, :], T[n][:, tk]) for i, n in enumerate(("cos32", "sin32", "cos32q", "sin32q"))],
                  [], [("cs", b)], ("cs", b))

        load_group(0)
        for g in range(G):
            b = g % 2
            if g + 1 < G:
                load_group(g + 1)
            xt, xTg, hn, c_ = xtok[b], xT[b], hnT[b], cs[b]
            tok = slice(g * 512, (g + 1) * 512)
            pss, pssk = psM.next()
            for c in range(8):
                pt, ptk = psT.next()
                for t in range(4):
                    P.op("pe", "transpose", [("xtok", b), "identf"], [ptk], out=pt[:, t * 128:(t + 1) * 128],
                         in_=xt[:, t, c * 128:(c + 1) * 128], identity=identf[:])
                P.op("dve", "tensor_copy", [ptk], [("xT", b, c)], out=xTg[:, c, :], in_=pt[:])
                P.op("act", "activation", [ptk], [("sq", c)], out=sq[:, c, :], in_=pt[:], func=AF.Square)
                P.op("pe", "matmul", [("sq", c), "onesf"], [pssk], out=pss[:], lhsT=onesf[:], rhs=sq[:, c, :],
                     start=(c == 0), stop=(c == 7))
            P.dma("sp", [(T["XT"][c * 128:(c + 1) * 128, tok], xTg[:, c, :]) for c in range(8)],
                  [("xT", b, c) for c in range(8)], [("XT", g)], ("xTst", b))
            if cfg.lvl < 2:
                continue
            P.op("act", "activation", [pssk, "epsc"], ["rstd"], out=rstd[:], in_=pss[:], func=AF.Sqrt, bias=epsc[:], scale=1.0 / D)
            P.op("dve", "reciprocal", ["rstd"], ["rstd"], out=rstd[:], in_=rstd[:])
            for c in range(8):
                P.op("dve", "scalar_tensor_tensor", [("xT", b, c), "gain0", "rstd"], [("hnT", b, c)],
                     out=hn[:, c, :], in0=xTg[:, c, :], scalar=gain[:, c:c + 1], in1=rstd[:], op0=ALU.mult, op1=ALU.mult)

            def fm_group(col0, M, wt=None):
                pt, ptk = psF.next()
                for c in range(8):
                    lhs = (w[:, c, col0:col0 + M] if wt is None else wt(c))
                    P.op("pe", "matmul", [("hnT", b, c), "w0", "w0r"], [ptk], out=pt[0:M, :], lhsT=lhs, rhs=hn[:, c, :],
                         start=(c == 0), stop=(c == 7))
                return pt, ptk

            if cfg.lvl < 3:
                continue
            for h in range(4):
                pt, ptk = fm_group(L0_COLS["qa"] + 128 * h, 128)
                ev, evk = ev_bf.next()
                P.op("act", "copy", [ptk], [evk], out=ev[:], in_=pt[:])
                P.dma("sp", (T["QA"][h, :, tok], ev[:]), [evk], [("QA", g)], evk)
                pt, ptk = fm_group(L0_COLS["ka"] + 128 * h, 128)
                ev, evk = ev_bf.next()
                P.op("dve", "tensor_scalar", [ptk], [evk], out=ev[:], in0=pt[:], scalar1=128.0 ** -0.5, scalar2=None, op0=ALU.mult)
                P.dma("sp", (T["KA"][h, :, tok], ev[:]), [evk], [("KA", g)], evk)
            for gi_, gname in enumerate(("GAI", "GAF")):
                pt, ptk = fm_group(L0_COLS["ia"] + 4 * gi_, 4)
                ev, evk = ev_f.next()
                P.op("act", "copy", [ptk], [evk], out=ev[0:4, :], in_=pt[0:4, :])
                P.dma("sp", (T[gname][:, tok], ev[0:4, :]), [evk], [(gname, g)], evk)
            if cfg.lvl < 4:
                continue
            for j in range(6):
                isq = j < 4
                pt, ptk = fm_group(2056 + 128 * j, 128)
                pr, prk = fm_group(0, 32, wt=lambda c, j=j: wr[:, c, j, :])
                r, rk = rt.next()
                ci = 2 if isq else 0
                P.op("dve", "tensor_tensor", [ptk, ("cs", b)], [rk], out=r[:, 0, :], in0=pt[0:32, :], in1=c_[:, ci, :], op=ALU.mult)
                P.op("dve", "tensor_tensor", [prk, ("cs", b), rk], [rk], out=r[:, 1, :], in0=pr[0:32, :], in1=c_[:, ci + 1, :], op=ALU.mult)
                if isq:
                    ev, evk = ev_bf.next()
                    P.op("act", "activation", [ptk], [evk], out=ev[:], in_=pt[:], func=AF.Copy, scale=128.0 ** -0.5)
                    P.op("pool", "tensor_tensor", [rk, evk], [evk], out=ev[0:32, :], in0=r[:, 0, :], in1=r[:, 1, :], op=ALU.add)
                    P.dma("sp", (T["QB"][j, :, tok], ev[:]), [evk], [("QB", g)], evk)
                else:
                    hh = j - 4
                    ev, evk = ev_f.next()
                    P.op("act", "copy", [ptk], [evk], out=ev[:], in_=pt[:])
                    P.op("pool", "tensor_tensor", [rk, evk], [evk], out=ev[0:32, :], in0=r[:, 0, :], in1=r[:, 1, :], op=ALU.add)
                    evb, evbk = ev_bf.next()
                    P.op("pool", "tensor_copy", [evk], [evbk], out=evb[:], in_=ev[:])
                    P.dma("sp", (T["KB"][hh, :, tok], evb[:]), [evbk], [("KB", g)], evbk)
                    P.op("dve", "tensor_reduce", [evk], [("km", hh, g)], out=km[:, hh, 2 * g:2 * g + 2],
                         in_=ev[:].rearrange("p (b k) -> p b k", b=2), axis=AX.X, op=ALU.add)
                    if hh == 0:
                        kt, ktk = ktok.next()
                    pt2, pt2k = psT.next()
                    for t in range(4):
                        P.op("pe", "transpose", [evk, "identf"], [pt2k], out=pt2[:, t * 128:(t + 1) * 128],
                             in_=ev[:, t * 128:(t + 1) * 128], identity=identf[:])
                    P.op("act", "copy", [pt2k], [(ktk, hh)], out=kt[:, :, hh * 128:(hh + 1) * 128],
                         in_=pt2[:].rearrange("p (t d) -> p t d", t=4))
                    if hh == 1:
                        P.dma("sp", (T["moba_k"][tok, :, :].rearrange("(t p) h d -> p t (h d)", p=128), kt[:]),
                              [(ktk, 0), (ktk, 1)], [("moba_k", g)], ktk)
            if cfg.lvl < 5:
                continue
            for t in range(4):
                tsl = slice(g * 512 + t * 128, g * 512 + (t + 1) * 128)
                for name, col0, n in (("va", 1024, 512), ("kat", 512, 512), ("oa", 1536, 512), ("vb", 2824, 256)):
                    pt, ptk = psT.next()
                    for c in range(8):
                        P.op("pe", "matmul", [("hnT", b, c), "w0"], [ptk], out=pt[:, 0:n], lhsT=hn[:, c, t * 128:(t + 1) * 128],
                             rhs=w[:, c, col0:col0 + n], start=(c == 0), stop=(c == 7))
                    if name == "va":
                        ev, evk = vtokb.next()
                        P.op("act", "copy", [ptk], [evk], out=ev[:], in_=pt[:])
                        P.dma("sp", (T["VA"][tsl, :], ev[:]), [evk], [("VA", g)], evk)
                    elif name == "kat":
                        ev, evk = vtokb.next()
                        P.op("dve", "tensor_scalar", [ptk], [evk], out=ev[:], in0=pt[:], scalar1=128.0 ** -0.5, scalar2=None, op0=ALU.mult)
                        P.dma("sp", (T["KAT"][tsl, :], ev[:]), [evk], [("KAT", g)], evk)
                    elif name == "oa":
                        ev, evk = vtokf.next()
                        P.op("act", "activation", [ptk], [evk], out=ev[:], in_=pt[:], func=AF.Sigmoid)
                        P.dma("sp", (T["OA"][tsl, :], ev[:]), [evk], [("OA", g)], evk)
                    else:
                        ev, evk = vtokf.next()
                        P.op("dve", "tensor_copy", [ptk], [evk], out=ev[:, 0:256], in_=pt[:, 0:256])
                        P.dma("sp", (T["moba_v"][tsl, :, :].rearrange("s h d -> s (h d)"), ev[:, 0:256]), [evk], [("moba_v", g)], evk)
                        evb, evbk = vtokb.next()
                        P.op("pool", "tensor_copy", [evk], [evbk], out=evb[:, 0:256], in_=ev[:, 0:256])
                        P.dma("sp", (T["VB"][tsl, :], evb[:, 0:256]), [evbk], [("VB", g)], evbk)
        if 'km' in SKIP:
            P.barrier()
            return
        P.op("dve", "tensor_scalar", [("km", hh, g) for hh in range(2) for g in range(G)], ["kmall"],
             out=km[:], in0=km[:], scalar1=1.0 / 256, scalar2=None, op0=ALU.mult)
        P.dma("sp", (T["KM"], km[:]), ["kmall"], ["KM"], "km")
        P.barrier()


def bcast_rows(ap2d, nparts=128):
    return ap2d.unsqueeze(0).broadcast_to([nparts] + list(ap2d.shape))


def phase_A1(nc, P, cfg, T, C):
    S = cfg.S
    NCH = S // 128
    psF, psT, psM = C["psF"], C["psT"], C["psM"]
    identf, identb = C["identf"], C["identb"]
    with ExitStack() as ph:
        ig = sb(ph, nc, "r_ig", [4, S], F32)
        nlf = sb(ph, nc, "r_nlf", [4, S], F32)
        ncum = sb(ph, nc, "r_ncum", [4, S], F32)
        pm = sb(ph, nc, "r_pm", [4, S], F32)
        tmp = sb(ph, nc, "r_tmp", [4, S], F32)
        onesr = sb(ph, nc, "r_ones", [4, 128], F32)
        zerosr = sb(ph, nc, "r_zeros", [4, 128], F32)
        bg = sb(ph, nc, "r_bg", [4, 2], F32)
        nbf = sb(ph, nc, "r_nbf", [4, 1], F32)
        onec = sb(ph, nc, "r_onec", [4, 1], F32)
        Gc = sb(ph, nc, "r_G", [4, NCH], F32)
        Fc = sb(ph, nc, "r_F", [4, NCH], F32)
        mnext = sb(ph, nc, "r_mn", [4, NCH], F32)
        mc = sb(ph, nc, "r_mc", [4, NCH], F32)
        dec = sb(ph, nc, "r_dec", [4, NCH], F32)
        m0 = sb(ph, nc, "r_m0", [4, 1], F32)
        P.dma("sp", (ig[:], T["GAI"]), [("GAI", g) for g in range(cfg.G)], ["r_ig"], "r_ig")
        P.dma("sp", (nlf[:], T["GAF"]), [("GAF", g) for g in range(cfg.G)], ["r_nlf"], "r_nlf")
        P.dma("sp", (bg[:], T["b_gates0"]), [], ["r_bg"], "r_bg")
        P.dma("sp", (m0[:], T["mlstm_m0"]), [], ["r_m0"], "r_m0")
        P.op("pool", "memset", [], ["r_ones"], ap=onesr[:], constant=1.0)
        P.op("pool", "memset", [], ["r_zeros"], ap=zerosr[:], constant=0.0)
        P.op("pool", "memset", [], ["r_onec"], ap=onec[:], constant=1.0)
        P.op("dve", "tensor_scalar", ["r_bg"], ["r_nbf"], out=nbf[:], in0=bg[:, 1:2], scalar1=-1.0, scalar2=None, op0=ALU.mult)
        P.op("dve", "tensor_scalar", ["r_ig", "r_bg"], ["r_ig"], out=ig[:], in0=ig[:], scalar1=bg[:, 0:1], scalar2=None, op0=ALU.add)
        P.op("act", "activation", ["r_nlf", "r_nbf"], ["r_nlf"], out=nlf[:], in_=nlf[:], func=AF.Exp, bias=nbf[:], scale=-1.0)
        P.op("act", "activation", ["r_nlf", "r_onec"], ["r_nlf"], out=nlf[:], in_=nlf[:], func=AF.Ln, bias=onec[:], scale=1.0)
        for c in range(NCH):
            sl = slice(c * 128, (c + 1) * 128)
            P.op("dve", "tensor_tensor_scan", ["r_nlf", "r_ones"], [("r_ncum", c)], out=ncum[:, sl], data0=onesr[:], data1=nlf[:, sl],
                 initial=0.0, op0=ALU.mult, op1=ALU.add)
        allc = [("r_ncum", c) for c in range(NCH)]
        P.op("dve", "tensor_tensor", ["r_ig"] + allc, ["r_ig"], out=ig[:], in0=ig[:], in1=ncum[:], op=ALU.add)
        P.op("dve", "tensor_reduce", ["r_ig"], ["r_G"], out=Gc[:], in_=ig[:].rearrange("p (c t) -> p c t", t=128), axis=AX.X, op=ALU.max)
        P.op("dve", "tensor_scalar", allc, ["r_F"], out=Fc[:], in0=ncum[:].rearrange("p (c t) -> p c t", t=128)[:, :, 127], scalar1=-1.0,
             scalar2=None, op0=ALU.mult)
        P.op("dve", "tensor_tensor_scan", ["r_G", "r_F", "r_m0"], ["r_mn"], out=mnext[:], data0=Gc[:], data1=Fc[:], initial=m0[:],
             op0=ALU.max, op1=ALU.add)
        P.op("dve", "tensor_copy", ["r_m0"], ["r_mc"], out=mc[:, 0:1], in_=m0[:])
        if NCH > 1:
            P.op("dve", "tensor_copy", ["r_mn", "r_mc"], ["r_mc"], out=mc[:, 1:NCH], in_=mnext[:, 0:NCH - 1])
        for c in range(NCH):
            sl = slice(c * 128, (c + 1) * 128)
            P.op("dve", "tensor_tensor_scan", ["r_ig", "r_zeros", "r_mc"], [("r_pm", c)], out=pm[:, sl], data0=ig[:, sl], data1=zerosr[:],
                 initial=mc[:, c:c + 1], op0=ALU.max, op1=ALU.add)
        allp = [("r_pm", c) for c in range(NCH)]
        v3 = lambda t_: t_[:].rearrange("p (c t) -> p c t", t=128)
        bc = lambda t_: t_[:].unsqueeze(2).broadcast_to([4, NCH, 128])
        RQ = T["RQ"]
        P.op("dve", "tensor_scalar", allp, ["r_tmp"], out=tmp[:], in0=pm[:], scalar1=-1.0, scalar2=None, op0=ALU.mult)
        P.dma("sp", (RQ[:, 0, :], tmp[:]), ["r_tmp"], [("RQ", 0)], "r_tmp")
        P.op("dve", "tensor_tensor", allp + ["r_mc", "r_tmp"], ["r_tmp"], out=v3(tmp), in0=bc(mc), in1=v3(pm), op=ALU.subtract)
        P.op("act", "activation", ["r_tmp"], ["r_tmp"], out=tmp[:], in_=tmp[:], func=AF.Exp)
        P.dma("sp", (RQ[:, 1, :], tmp[:]), ["r_tmp"], [("RQ", 1)], "r_tmp")
        P.op("dve", "tensor_tensor", allp + allc + ["r_tmp"], ["r_tmp"], out=tmp[:], in0=ncum[:], in1=pm[:], op=ALU.subtract)
        P.op("act", "activation", ["r_tmp"], ["r_tmp"], out=tmp[:], in_=tmp[:], func=AF.Exp)
        P.dma("sp", (RQ[:, 2, :], tmp[:]), ["r_tmp"], [("RQ", 2)], "r_tmp")
        P.op("dve", "tensor_tensor", ["r_F", "r_mn"], ["r_dec"], out=dec[:], in0=Fc[:], in1=mnext[:], op=ALU.subtract)
        P.op("dve", "tensor_tensor", ["r_ig", "r_dec", "r_tmp"], ["r_tmp"], out=v3(tmp), in0=v3(ig), in1=bc(dec), op=ALU.add)
        P.op("act", "activation", ["r_tmp"], ["r_tmp"], out=tmp[:], in_=tmp[:], func=AF.Exp)
        P.dma("sp", (RQ[:, 3, :], tmp[:]), ["r_tmp"], [("RQ", 3)], "r_tmp")
        P.op("dve", "tensor_tensor", ["r_dec", "r_mc"], ["r_dec"], out=dec[:], in0=dec[:], in1=mc[:], op=ALU.add)
        P.op("act", "activation", ["r_dec"], ["r_dec"], out=dec[:], in_=dec[:], func=AF.Exp)
        P.dma("sp", (T["DEC"][0, :].rearrange("(h c) -> h c", h=4), dec[:]), ["r_dec"], ["DEC"], "r_dec")
        P.dma("sp", (T["GR"], ig[:]), ["r_ig"], ["GR"], "r_ig")
        P.dma("sp", (T["mlstm_m"], mnext[:, NCH - 1:NCH]), ["r_mn"], ["mlstm_m"], "r_mn")
        P.barrier()
    with ExitStack() as ph:
        cbias = sb(ph, nc, "cbias_sb", [128, 128], F32)
        nw = sb(ph, nc, "nw_mlstm", [128, 512], F32)
        decb = sb(ph, nc, "decb", [128, 4 * NCH], F32)
        cn = sb(ph, nc, "cn", [128, 4, 129], F32)
        cnb = sb(ph, nc, "cnb", [128, 4, 129], BF16)
        epsc = C["epsc"]
        NB = 2
        qT = [sb(ph, nc, "m_qT%d" % i, [128, 4, 128], BF16) for i in range(NB)]
        kT = [sb(ph, nc, "m_kT%d" % i, [128, 4, 128], BF16) for i in range(NB)]
        kt = [sb(ph, nc, "m_kt%d" % i, [128, 512], BF16) for i in range(NB)]
        va = [sb(ph, nc, "m_va%d" % i, [128, 4, 129], BF16) for i in range(NB)]
        ot = [sb(ph, nc, "m_o%d" % i, [128, 512], F32) for i in range(NB)]
        rq = [sb(ph, nc, "m_rq%d" % i, [4, 4, 128], F32) for i in range(NB)]
        gB = [sb(ph, nc, "m_gB%d" % i, [128, 4, 128], F32) for i in range(NB)]
        tq = [sb(ph, nc, "m_tq%d" % i, [128, 16], F32) for i in range(NB)]
        mixs = [sb(ph, nc, "m_mix%d" % i, [128, 4, 128], BF16) for i in range(NB)]
        e1 = Ring("m_e1", [sb(ph, nc, "m_e1%d" % i, [128, 128], F32) for i in range(2)])
        Dm = Ring("m_D", [sb(ph, nc, "m_D%d" % i, [128, 128], F32) for i in range(2)])
        wm = Ring("m_wm", [sb(ph, nc, "m_wm%d" % i, [128, 128], BF16) for i in range(2)])
        wmT = Ring("m_wmT", [sb(ph, nc, "m_wmT%d" % i, [128, 128], BF16) for i in range(2)])
        nums = Ring("m_nums", [sb(ph, nc, "m_nums%d" % i, [128, 129], F32) for i in range(2)])
        tot = Ring("m_tot", [sb(ph, nc, "m_tot%d" % i, [128, 129], F32) for i in range(2)])
        sm = Ring("m_sm", [sb(ph, nc, "m_sm%d" % i, [128, 4], F32) for i in range(2)])
        hh = Ring("m_hh", [sb(ph, nc, "m_hh%d" % i, [128, 128], F32) for i in range(2)])
        hsq = sb(ph, nc, "m_hsq", [128, 128], F32)
        hab = Ring("m_hab", [sb(ph, nc, "m_hab%d" % i, [128, 128], BF16) for i in range(2)])
        vw = Ring("m_vw", [sb(ph, nc, "m_vw%d" % i, [128, 129], BF16) for i in range(2)])
        P.dma("sp", (cbias[:], T["cbias"]), [], ["cbias"], "cbias")
        P.dma("sp", (nw[:], bcast_rows(T["norm_mlstm"])[:, 0, :]), [], ["nw"], "nw")
        P.dma("sp", (decb[:], bcast_rows(T["DEC"])[:, 0, :]), ["DEC"], ["decb"], "decb")
        P.dma("sp", [(cn[:, :, 0:128], T["mlstm_c0"].rearrange("h k v -> k h v")),
                     (cn[:, :, 128], T["mlstm_n0"].rearrange("h k -> k h"), dict(allow_slow_non_contiguous=True))],
              [], [("cn", h) for h in range(4)], "cn")
        P.op("act", "copy", [("cn", h) for h in range(4)], [("cnb", h) for h in range(4)], out=cnb[:], in_=cn[:])
        for i in range(NB):
            P.op("pool", "memset", [], [("m_va1", i)], ap=va[i][:, :, 128:129], constant=1.0)

        def load_chunk(c):
            b = c % NB
            tok = slice(c * 128, (c + 1) * 128)
            g = c // 4
            P.dma("sp", (qT[b][:], T["QA"][:, :, tok].rearrange("h p t -> p h t")), [("QA", g)], [("m_qT", b)], ("m_qT", b))
            P.dma("sp", (kT[b][:], T["KA"][:, :, tok].rearrange("h p t -> p h t")), [("KA", g)], [("m_kT", b)], ("m_kT", b))
            P.dma("sp", (kt[b][:], T["KAT"][tok, :]), [("KAT", g)], [("m_kt", b)], ("m_kt", b))
            P.dma("sp", (va[b][:, :, 0:128], T["VA"][tok, :].rearrange("s (h d) -> s h d", h=4)), [("VA", g)], [("m_va", b)], ("m_va", b))
            P.dma("sp", (ot[b][:], T["OA"][tok, :]), [("OA", g)], [("m_o", b)], ("m_o", b))
            P.dma("sp", (rq[b][:], T["RQ"][:, :, tok]), [("RQ", j) for j in range(4)], [("m_rq", b)], ("m_rq", b))
            P.dma("sp", (gB[b][:], bcast_rows(T["GR"][:, tok])), ["GR"], [("m_gB", b)], ("m_gB", b))

        load_chunk(0)
        for c in range(NCH):
            b = c % NB
            tok = slice(c * 128, (c + 1) * 128)
            if c + 1 < NCH:
                load_chunk(c + 1)
            pt, ptk = psM.next()
            for j in range(4):
                P.op("pe", "transpose", [("m_rq", b), "identf"], [ptk], out=pt[:, j * 4:(j + 1) * 4], in_=rq[b][:, j, :], identity=identf[0:4, 0:4])
            P.op("dve", "tensor_copy", [ptk], [("m_tq", b)], out=tq[b][:], in_=pt[:, 0:16])
            for h in range(4):
                npm = tq[b][:, h:h + 1]
                ain = tq[b][:, 4 + h:5 + h]
                emt = tq[b][:, 8 + h:9 + h]
                wend = tq[b][:, 12 + h:13 + h]
                pS, pSk = psF.next()
                P.op("pe", "matmul", [("m_qT", b), ("m_kT", b)], [pSk], out=pS[:, 0:128], lhsT=qT[b][:, h, :], rhs=kT[b][:, h, :], start=True, stop=True)
                e, ek = e1.next()
                P.op("pool", "tensor_tensor", [("m_gB", b), "cbias"], [ek], out=e[:], in0=gB[b][:, h, :], in1=cbias[:], op=ALU.add)
                Dt, Dk = Dm.next()
                P.op("act", "activation", [ek, ("m_tq", b)], [Dk], out=Dt[:], in_=e[:], func=AF.Exp, bias=npm, scale=1.0)
                w_, wk = wm.next()
                P.op("dve", "tensor_tensor", [pSk, Dk], [wk], out=w_[:], in0=pS[:, 0:128], in1=Dt[:], op=ALU.mult)
                pW, pWk = psT.next()
                pWb = pW[:].bitcast(BF16)
                P.op("pe", "transpose", [wk, "identb"], [pWk], out=pWb[:, 0:128], in_=w_[:], identity=identb[:])
                wT, wTk = wmT.next()
                P.op("act", "copy", [pWk], [wTk], out=wT[:], in_=pWb[:, 0:128])
                pN, pNk = psF.next()
                P.op("pe", "matmul", [wTk, ("m_va", b), ("m_va1", b)], [pNk], out=pN[:, 0:129], lhsT=wT[:], rhs=va[b][:, h, :], start=True, stop=True)
                pI, pIk = psF.next()
                P.op("pe", "matmul", [("m_qT", b), ("cnb", h)], [pIk], out=pI[:, 0:129], lhsT=qT[b][:, h, :], rhs=cnb[:, h, :], start=True, stop=True)
                nm, nmk = nums.next()
                P.op("act", "copy", [pNk], [nmk], out=nm[:], in_=pN[:, 0:129])
                tt, ttk = tot.next()
                P.op("dve", "scalar_tensor_tensor", [pIk, nmk, ("m_tq", b)], [ttk], out=tt[:], in0=pI[:, 0:129], scalar=ain, in1=nm[:],
                     op0=ALU.mult, op1=ALU.add)
                s_, sk = sm.next()
                P.op("dve", "scalar_tensor_tensor", [ttk], [sk], out=s_[:, 0:1], in0=tt[:, 128:129], scalar=-1.0, in1=tt[:, 128:129],
                     op0=ALU.mult, op1=ALU.max)
                P.op("dve", "tensor_scalar", [sk, ("m_tq", b)], [sk], out=s_[:, 0:1], in0=s_[:, 0:1], scalar1=emt, scalar2=None, op0=ALU.max)
                P.op("dve", "reciprocal", [sk], [sk], out=s_[:, 1:2], in_=s_[:, 0:1])
                hx, hk = hh.next()
                P.op("dve", "scalar_tensor_tensor", [ttk, sk, ("m_o", b)], [hk], out=hx[:], in0=tt[:, 0:128], scalar=s_[:, 1:2],
                     in1=ot[b][:, h * 128:(h + 1) * 128], op0=ALU.mult, op1=ALU.mult)
                P.op("act", "activation", [hk, sk], ["m_hsq", sk], out=hsq[:], in_=hx[:], func=AF.Square, accum_out=s_[:, 2:3])
                P.op("act", "activation", [sk, "epsc"], [sk], out=s_[:, 3:4], in_=s_[:, 2:3], func=AF.Sqrt, bias=epsc[:], scale=1.0 / 128)
                P.op("dve", "reciprocal", [sk], [sk], out=s_[:, 3:4], in_=s_[:, 3:4])
                ha, hak = hab.next()
                P.op("dve", "scalar_tensor_tensor", [hk, sk, "nw"], [hak], out=ha[:], in0=hx[:], scalar=s_[:, 3:4],
                     in1=nw[:, h * 128:(h + 1) * 128], op0=ALU.mult, op1=ALU.mult)
                pH, pHk = psT.next()
                pHb = pH[:].bitcast(BF16)
                P.op("pe", "transpose", [hak, "identb"], [pHk], out=pHb[:, 0:128], in_=ha[:], identity=identb[:])
                P.op("act", "copy", [pHk], [("m_mix", b, h)], out=mixs[b][:, h, :], in_=pHb[:, 0:128])
                v_, vk = vw.next()
                P.op("pool", "tensor_scalar", [("m_va", b), ("m_va1", b), ("m_tq", b)], [vk], out=v_[:], in0=va[b][:, h, :], scalar1=wend,
                     scalar2=None, op0=ALU.mult)
                pU, pUk = psF.next()
                P.op("pe", "matmul", [("m_kt", b), vk], [pUk], out=pU[:, 0:129], lhsT=kt[b][:, h * 128:(h + 1) * 128], rhs=v_[:], start=True, stop=True)
                P.op("dve", "scalar_tensor_tensor", [pUk, ("cn", h), "decb"], [("cn", h)], out=cn[:, h, :], in0=cn[:, h, :],
                     scalar=decb[:, h * NCH + c:h * NCH + c + 1], in1=pU[:, 0:129], op0=ALU.mult, op1=ALU.add)
                P.op("act", "copy", [("cn", h)], [("cnb", h)], out=cnb[:, h, :], in_=cn[:, h, :])
            P.dma("sp", (T["MIX0"][0:512, tok].rearrange("(h p) t -> p h t", h=4), mixs[b][:]), [("m_mix", b, h) for h in range(4)],
                  [("MIX0a", c)], ("m_mix", b))
        P.dma("sp", [(T["mlstm_c"].rearrange("h k v -> k h v"), cn[:, :, 0:128]),
                     (T["mlstm_n"].rearrange("h k -> k h"), cn[:, :, 128], dict(allow_slow_non_contiguous=True))],
              [("cn", h) for h in range(4)], ["mlstm_cn"], "cn_out")
        P.barrier()


def phase_A2(nc, P, cfg, T, C):
    S = cfg.S
    NB = S // 256
    ps = C["ps"]
    psS = Ring("psS", ps[0:2])
    psSel = Ring("psSelq", ps[4:5])
    psSelB = Ring("psSelb", ps[5:6])
    accs = [(ps[2], ps[3], ("psA", 0)), (ps[6], ps[7], ("psA", 1))]
    identf = C["identf"]
    with ExitStack() as ph:
        qT = sb(ph, nc, "b_qT", [128, 4, S], BF16)
        kT = sb(ph, nc, "b_kT", [128, 2, S], BF16)
        vt = sb(ph, nc, "b_vt", [128, S // 128, 256], BF16)
        kmf = sb(ph, nc, "b_kmf", [128, 2, NB], F32)
        kmb = sb(ph, nc, "b_kmb", [128, 2, 32], BF16)
        Eall = sb(ph, nc, "b_E", [32, 32, 128], BF16)
        tri = sb(ph, nc, "b_tri", [128, 128], BF16)
        onesb = sb(ph, nc, "b_ones", [128, 128], BF16)
        sc = Ring("b_sc", [sb(ph, nc, "b_sc%d" % i, [128, 32], F32) for i in range(2)])
        mx = Ring("b_mx", [sb(ph, nc, "b_mx%d" % i, [128, 8], F32) for i in range(2)])
        bia = Ring("b_bia", [sb(ph, nc, "b_bia%d" % i, [128, 32], F32) for i in range(2)])
        biT = Ring("b_biT", [sb(ph, nc, "b_biT%d" % i, [32, 256], BF16) for i in range(2)])
        pT = Ring("b_pT", [sb(ph, nc, "b_pT%d" % i, [128, 256], BF16) for i in range(4)])
        rd = Ring("b_rd", [sb(ph, nc, "b_rd%d" % i, [128, 256], F32) for i in range(2)])
        mo = Ring("b_mo", [sb(ph, nc, "b_mo%d" % i, [128, 256], BF16) for i in range(2)])
        P.dma("sp", [(qT[:, h, :], T["QB"][h]) for h in range(4)], [], ["b_qT"], "b_qT")
        P.dma("sp", [(kT[:, j, :], T["KB"][j]) for j in range(2)], [], ["b_kT"], "b_kT")
        P.dma("sp", (vt[:], T["VB"].rearrange("(t p) d -> p t d", p=128)), [], ["b_vt"], "b_vt")
        P.dma("sp", (kmf[:], T["KM"]), [], ["b_kmf"], "b_kmf")
        P.dma("sp", (Eall[:], T["Eall"]), [], ["b_E"], "b_E")
        P.dma("sp", (tri[:], T["triT"]), [], ["b_tri"], "b_tri")
        P.op("pool", "memset", [], ["b_ones"], ap=onesb[:], constant=1.0)
        P.op("pool", "memset", [], ["b_kmb"], ap=kmb[:], constant=0.0)
        P.op("dve", "tensor_copy", ["b_kmf", "b_kmb"], ["b_kmb"], out=kmb[:, :, 0:NB], in_=kmf[:])
        it = 0
        for B in range(NB):
            qs = slice(B * 256, (B + 1) * 256)
            for h in range(4):
                j = h // 2
                accO, accD, acck = accs[it % 2]
                it += 1
                bt, btk = None, None
                if B > 0:
                    pb, pbk = psSelB.next()
                    for qt in range(2):
                        q128 = slice(B * 256 + qt * 128, B * 256 + (qt + 1) * 128)
                        pq, pqk = psSel.next()
                        P.op("pe", "matmul", ["b_qT", "b_kmb"], [pqk], out=pq[:, 0:32], lhsT=qT[:, h, q128], rhs=kmb[:, j, :], start=True, stop=True)
                        s_, sk = sc.next()
                        P.op("pool", "memset", [], [sk], ap=s_[:], constant=-3.0e4)
                        P.op("dve", "tensor_copy", [pqk, sk], [sk], out=s_[:, 0:B], in_=pq[:, 0:B])
                        m_, mk = mx.next()
                        P.op("dve", "max", [sk], [mk], out=m_[:], in_=s_[:])
                        P.op("dve", "tensor_scalar", [mk], [mk], out=m_[:, 2:3], in0=m_[:, 2:3], scalar1=-2.0e4, scalar2=None, op0=ALU.max)
                        b_, bk = bia.next()
                        P.op("dve", "tensor_scalar", [sk, mk], [bk], out=b_[:], in0=s_[:], scalar1=m_[:, 2:3], scalar2=NEG, op0=ALU.is_lt, op1=ALU.mult)
                        P.op("pe", "transpose", [bk, "identf"], [pbk], out=pb[0:32, qt * 128:(qt + 1) * 128], in_=b_[:], identity=identf[:])
                    bt, btk = biT.next()
                    P.op("act", "copy", [pbk], [btk], out=bt[:], in_=pb[0:32, 0:256])
                nkt = 2 * B + 2
                for kt in range(nkt):
                    ks = slice(kt * 128, (kt + 1) * 128)
                    own = kt - 2 * B
                    c0 = 128 if own == 1 else 0
                    pS, pSk = psS.next()
                    P.op("pe", "matmul", ["b_kT", "b_qT"], [pSk], out=pS[:, c0:256], lhsT=kT[:, j, ks], rhs=qT[:, h, B * 256 + c0:(B + 1) * 256],
                         start=True, stop=(own >= 0))
                    if own < 0:
                        P.op("pe", "matmul", ["b_E", btk], [pSk], out=pS[:, 0:256], lhsT=Eall[:, kt // 2, :], rhs=bt[:], start=False, stop=True)
                    p_, pk = pT.next()
                    P.op("act", "activation", [pSk], [pk], out=p_[:, c0:256], in_=pS[:, c0:256], func=AF.Exp)
                    if own >= 0:
                        P.op("pool", "tensor_tensor", [pk, "b_tri"], [pk], out=p_[:, c0:c0 + 128], in0=p_[:, c0:c0 + 128], in1=tri[:], op=ALU.mult)
                    P.op("pe", "matmul", ["b_vt", pk], [acck], out=accO[:, c0:256], lhsT=vt[:, kt, j * 128:(j + 1) * 128], rhs=p_[:, c0:256],
                         start=(kt == 0), stop=(kt == nkt - 1))
                    P.op("pe", "matmul", ["b_ones", pk], [acck], out=accD[:, c0:256], lhsT=onesb[:], rhs=p_[:, c0:256],
                         start=(kt == 0), stop=(kt == nkt - 1))
                r_, rk = rd.next()
                P.op("dve", "reciprocal", [acck], [rk], out=r_[:], in_=accD[:, 0:256])
                o_, ok = mo.next()
                P.op("dve", "tensor_tensor", [acck, rk], [ok], out=o_[:], in0=accO[:, 0:256], in1=r_[:], op=ALU.mult)
                P.dma("sp", (T["MIX0"][512 + h * 128:512 + (h + 1) * 128, qs], o_[:]), [ok], [("MIX0b", B, h)], ok)
        P.barrier()


def phase_FFN(nc, P, cfg, T, C, L, KC, mixname, xin, xout, final=False, NT=None, TG=256):
    S = cfg.S if NT is None else NT
    G = (S + TG - 1) // TG
    ps = C["ps"]
    psA = Ring("psA", ps[0:4])
    psB = Ring("psB", ps[4:6])
    psC = Ring("psC", ps[6:8])
    onesf, epsc, identf = C["onesf"], C["epsc"], C["identf"]
    sfx = "%d" % L
    with ExitStack() as ph:
        wo = sb(ph, nc, "f_wo" + sfx, [128, KC, 1024], BF16)
        wup = sb(ph, nc, "f_wup" + sfx, [128, 8, 5632], BF16)
        wdn = sb(ph, nc, "f_wdn" + sfx, [128, 22, 1024], BF16)
        cw = sb(ph, nc, "f_cw" + sfx, [128, 3, 44], F32)
        cb = sb(ph, nc, "f_cb" + sfx, [128, 44], F32)
        gain = sb(ph, nc, "f_gain" + sfx, [128, 8], F32)
        gfin = sb(ph, nc, "f_gfin" + sfx, [128, 8], F32)
        TL = sb(ph, nc, "f_TL" + sfx, [128, 2, 44], F32)
        xT = [sb(ph, nc, "f_xT%s_%d" % (sfx, i), [128, 8, TG], F32) for i in range(2)]
        mx = [sb(ph, nc, "f_mix%s_%d" % (sfx, i), [128, KC, TG], BF16) for i in range(1)]
        hn = sb(ph, nc, "f_hn" + sfx, [128, 8, TG], BF16)
        act = sb(ph, nc, "f_act" + sfx, [128, 22, TG], BF16)
        rstd = sb(ph, nc, "f_rstd" + sfx, [128, TG], F32)
        sq = Ring("f_sq", [sb(ph, nc, "f_sq%s_%d" % (sfx, i), [128, TG], F32) for i in range(2)])
        U = Ring("f_U", [sb(ph, nc, "f_U%s_%d" % (sfx, i), [128, TG + 2], F32) for i in range(4)])
        Y = Ring("f_Y", [sb(ph, nc, "f_Y%s_%d" % (sfx, i), [128, TG], F32) for i in range(4)])
        Yt = Ring("f_Yt", [sb(ph, nc, "f_Yt%s_%d" % (sfx, i), [128, TG], F32) for i in range(2)])
        Sg = Ring("f_Sg", [sb(ph, nc, "f_Sg%s_%d" % (sfx, i), [128, TG], F32) for i in range(2)])
        ytok = Ring("f_ytok", [sb(ph, nc, "f_ytok%s_%d" % (sfx, i), [128, 1024 if final else 8], F32) for i in range(1)])
        tls = sb(ph, nc, "f_tls" + sfx, [88, 128], F32)
        P.dma("pool", [(wo[:, k, :], T["w_out%d" % L][k * 128:(k + 1) * 128, :]) for k in range(KC)], [], ["f_wo"], "f_wo")
        P.dma("pool", [(wup[:, k, :], T["ffn_up%d" % L][k * 128:(k + 1) * 128, :]) for k in range(8)], [], ["f_wup"], "f_wup")
        P.dma("pool", [(wdn[:, k, :], T["ffn_down%d" % L][k * 128:(k + 1) * 128, :]) for k in range(22)], [], ["f_wdn"], "f_wdn")
        P.dma("sp", [(cw[:], T["ffn_cw%d" % L]), (cb[:], T["ffn_cb%d" % L]), (gain[:], T["g_ffn%d" % L]), (gfin[:], T["g_fin"]),
                     (TL[:], T["ffn_tl%d" % L])], [], ["f_small"], "f_small")

        def load_group(g):
            b = g % 2
            tk = slice(g * TG, min(S, (g + 1) * TG))
            n = tk.stop - tk.start
            P.dma("sp", [(xT[b][:, c, 0:n], T[xin][c * 128:(c + 1) * 128, tk]) for c in range(8)], [], [("f_xT", b)], ("f_xT", b))

        def load_mix(g):
            tk = slice(g * TG, min(S, (g + 1) * TG))
            n = tk.stop - tk.start
            P.dma("sp", [(mx[0][:, k, 0:n], T[mixname][k * 128:(k + 1) * 128, tk]) for k in range(KC)], [], [("f_mix", 0)], ("f_mix", 0))

        load_group(0)
        load_mix(0)
        for g in range(G):
            b = g % 2
            tk = slice(g * TG, min(S, (g + 1) * TG))
            n = tk.stop - tk.start
            if g + 1 < G:
                load_group(g + 1)
            x_, m_ = xT[b], mx[0]
            pss, pssk = psC.next()
            for oc in range(8):
                pt, ptk = psA.next()
                for k in range(KC):
                    P.op("pe", "matmul", [("f_mix", 0), "f_wo"], [ptk], out=pt[:, 0:n], lhsT=wo[:, k, oc * 128:(oc + 1) * 128], rhs=m_[:, k, 0:n],
                         start=(k == 0), stop=(k == KC - 1))
                P.op("dve", "tensor_tensor", [ptk, ("f_xT", b)], [("f_x1", b, oc)], out=x_[:, oc, 0:n], in0=pt[:, 0:n], in1=x_[:, oc, 0:n], op=ALU.add)
                q_, qk = sq.next()
                P.op("act", "activation", [("f_x1", b, oc)], [qk], out=q_[:, 0:n], in_=x_[:, oc, 0:n], func=AF.Square)
                P.op("pe", "matmul", [qk, "onesf"], [pssk], out=pss[:, 0:n], lhsT=onesf[:], rhs=q_[:, 0:n], start=(oc == 0), stop=(oc == 7))
            if g + 1 < G:
                load_mix(g + 1)
            P.op("act", "activation", [pssk, "epsc"], ["f_rstd"], out=rstd[:, 0:n], in_=pss[:, 0:n], func=AF.Sqrt, bias=epsc[:], scale=1.0 / D)
            P.op("dve", "reciprocal", ["f_rstd"], ["f_rstd"], out=rstd[:, 0:n], in_=rstd[:, 0:n])
            for c in range(8):
                P.op("dve", "scalar_tensor_tensor", [("f_x1", b, c), "f_small", "f_rstd"], [("f_hn", c)], out=hn[:, c, 0:n], in0=x_[:, c, 0:n],
                     scalar=gain[:, c:c + 1], in1=rstd[:, 0:n], op0=ALU.mult, op1=ALU.mult)
            for gc in range(22):
                ys = []
                for half, ch in enumerate((gc, gc + 22)):
                    pt, ptk = psA.next()
                    for c in range(8):
                        P.op("pe", "matmul", [("f_hn", c), "f_wup"], [ptk], out=pt[:, 0:n], lhsT=wup[:, c, ch * 128:(ch + 1) * 128], rhs=hn[:, c, 0:n],
                             start=(c == 0), stop=(c == 7))
                    u_, uk = U.next()
                    P.op("act", "copy", [ptk], [(uk, "b")], out=u_[:, 2:2 + n], in_=pt[:, 0:n])
                    P.op("pool", "tensor_copy", [("f_TL", ch), "f_small"], [(uk, "a")], out=u_[:, 0:2], in_=TL[:, :, ch])
                    P.op("pool", "tensor_copy", [(uk, "a"), (uk, "b")], [("f_TL", ch)], out=TL[:, :, ch], in_=u_[:, n:n + 2])
                    y_, yk = Y.next()
                    ukk = [(uk, "a"), (uk, "b"), "f_small"]
                    if half == 0:
                        P.op("dve", "tensor_scalar", ukk, [yk], out=y_[:, 0:n], in0=u_[:, 2:2 + n], scalar1=cw[:, 2, ch:ch + 1], scalar2=cb[:, ch:ch + 1],
                             op0=ALU.mult, op1=ALU.add)
                        P.op("dve", "scalar_tensor_tensor", ukk + [yk], [yk], out=y_[:, 0:n], in0=u_[:, 1:1 + n], scalar=cw[:, 1, ch:ch + 1], in1=y_[:, 0:n],
                             op0=ALU.mult, op1=ALU.add)
                        P.op("dve", "scalar_tensor_tensor", ukk + [yk], [yk], out=y_[:, 0:n], in0=u_[:, 0:n], scalar=cw[:, 0, ch:ch + 1], in1=y_[:, 0:n],
                             op0=ALU.mult, op1=ALU.add)
                    else:
                        t_, tk_ = Yt.next()
                        P.op("pool", "tensor_scalar", ukk, [yk], out=y_[:, 0:n], in0=u_[:, 2:2 + n], scalar1=cw[:, 2, ch:ch + 1], scalar2=cb[:, ch:ch + 1],
                             op0=ALU.mult, op1=ALU.add)
                        for kk in (1, 0):
                            P.op("pool", "tensor_scalar", ukk + [tk_], [tk_], out=t_[:, 0:n], in0=u_[:, kk:kk + n], scalar1=cw[:, kk, ch:ch + 1], scalar2=None,
                                 op0=ALU.mult)
                            P.op("pool", "tensor_tensor", [tk_, yk], [yk], out=y_[:, 0:n], in0=y_[:, 0:n], in1=t_[:, 0:n], op=ALU.add)
                    ys.append((y_, yk))
                s_, sk = Sg.next()
                P.op("act", "activation", [ys[0][1]], [sk], out=s_[:, 0:n], in_=ys[0][0][:, 0:n], func=AF.Silu)
                P.op("dve", "tensor_tensor", [sk, ys[1][1]], [("f_act", gc)], out=act[:, gc, 0:n], in0=s_[:, 0:n], in1=ys[1][0][:, 0:n], op=ALU.mult)
            if final:
                pss2, pss2k = psC.next()
            for oc in range(8):
                pt, ptk = psB.next()
                for k in range(22):
                    P.op("pe", "matmul", [("f_act", k), "f_wdn"], [ptk], out=pt[:, 0:n], lhsT=wdn[:, k, oc * 128:(oc + 1) * 128], rhs=act[:, k, 0:n],
                         start=(k == 0), stop=(k == 21))
                P.op("dve", "tensor_tensor", [ptk, ("f_x1", b, oc)], [("f_x2", b, oc)], out=x_[:, oc, 0:n], in0=pt[:, 0:n], in1=x_[:, oc, 0:n], op=ALU.add)
                if final:
                    q_, qk = sq.next()
                    P.op("act", "activation", [("f_x2", b, oc)], [qk], out=q_[:, 0:n], in_=x_[:, oc, 0:n], func=AF.Square)
                    P.op("pe", "matmul", [qk, "onesf"], [pss2k], out=pss2[:, 0:n], lhsT=onesf[:], rhs=q_[:, 0:n], start=(oc == 0), stop=(oc == 7))
            x2k = [("f_x2", b, oc) for oc in range(8)]
            if not final:
                P.dma("sp", [(T[xout][c * 128:(c + 1) * 128, tk], x_[:, c, 0:n]) for c in range(8)], x2k, [(xout, g)], ("f_xst", b))
                P.readers.setdefault(("f_xT", b), []).extend(P.readers.get(x2k[0], []))
            else:
                P.op("act", "activation", [pss2k, "epsc"], ["f_rstd"], out=rstd[:, 0:n], in_=pss2[:, 0:n], func=AF.Sqrt, bias=epsc[:], scale=1.0 / D)
                P.op("dve", "reciprocal", ["f_rstd"], ["f_rstd"], out=rstd[:, 0:n], in_=rstd[:, 0:n])
                for c in range(8):
                    P.op("dve", "scalar_tensor_tensor", [("f_x2", b, c), "f_small", "f_rstd"], [("f_x3", b, c)], out=x_[:, c, 0:n], in0=x_[:, c, 0:n],
                         scalar=gfin[:, c:c + 1], in1=rstd[:, 0:n], op0=ALU.mult, op1=ALU.mult)
                for t in range((n + 127) // 128):
                    nt = min(128, n - t * 128)
                    yt, ytk = ytok.next()
                    for c4 in range(2):
                        pt, ptk = psA.next()
                        for cc in range(4):
                            c = c4 * 4 + cc
                            P.op("pe", "transpose", [("f_x3", b, c), "identf"], [ptk], out=pt[0:nt, cc * 128:(cc + 1) * 128],
                                 in_=x_[:, c, t * 128:t * 128 + nt], identity=identf[:])
                        P.op("act", "copy", [ptk], [(ytk, c4)], out=yt[0:nt, c4 * 512:(c4 + 1) * 512], in_=pt[0:nt, :])
                    P.dma("sp", (T["y"][g * TG + t * 128:g * TG + t * 128 + nt, :], yt[0:nt, :]), [(ytk, 0), (ytk, 1)], [("y", g, t)], ytk)
                P.readers.setdefault(("f_xT", b), []).append(P.lastw[("f_x3", b, 7)])
        pt, ptk = psA.next()
        P.op("pe", "transpose", [("f_TL", ch) for ch in range(44)] + ["identf"], [ptk], out=pt[0:88, 0:128], in_=TL[:].rearrange("p r c -> p (r c)"),
             identity=identf[:])
        P.op("act", "copy", [ptk], ["f_tls"], out=tls[:], in_=pt[0:88, 0:128])
        P.dma("sp", [(T["ffn_conv%d" % L][r, :].rearrange("(c p) -> c p", p=128), tls[r * 44:(r + 1) * 44, :]) for r in range(2)],
              ["f_tls"], ["ffn_conv_out"], "f_tls")
        P.barrier()


N_L1 = 3848
L1C = dict(z=0, xbc=512, dt=1536, qd=1544, kd=2312, vd=3080)
SWA = ((128, 1), (512, 4), (2048, 16))


def norm_group(P, C, x_, n, gain, hn, rstd, sqr, psC, xkeys, tag):
    onesf, epsc = C["onesf"], C["epsc"]
    pss, pssk = psC.next()
    for c in range(8):
        q_, qk = sqr.next()
        P.op("act", "activation", [xkeys[c]], [qk], out=q_[:, 0:n], in_=x_[:, c, 0:n], func=AF.Square)
        P.op("pe", "matmul", [qk, "onesf"], [pssk], out=pss[:, 0:n], lhsT=onesf[:], rhs=q_[:, 0:n], start=(c == 0), stop=(c == 7))
    P.op("act", "activation", [pssk, "epsc"], [tag + "rstd"], out=rstd[:, 0:n], in_=pss[:, 0:n], func=AF.Sqrt, bias=epsc[:], scale=1.0 / D)
    P.op("dve", "reciprocal", [tag + "rstd"], [tag + "rstd"], out=rstd[:, 0:n], in_=rstd[:, 0:n])
    for c in range(8):
        P.op("dve", "scalar_tensor_tensor", [xkeys[c], tag + "gain", tag + "rstd"], [(tag + "hn", c)], out=hn[:, c, 0:n], in0=x_[:, c, 0:n],
             scalar=gain[:, c:c + 1], in1=rstd[:, 0:n], op0=ALU.mult, op1=ALU.mult)


def phase_B0(nc, P, cfg, T, C):
    S, G = cfg.S, cfg.G
    ps = C["ps"]
    psF, psT, psC = Ring("psF", ps[0:4]), Ring("psT", ps[4:6]), Ring("psM", ps[6:8])
    identf = C["identf"]
    with ExitStack() as ph:
        w = sb(ph, nc, "w1", [128, 8, N_L1], BF16)
        wr = sb(ph, nc, "w1r", [128, 8, 12, 128], BF16)
        gain = sb(ph, nc, "b_gain", [128, 8], F32)
        cw = sb(ph, nc, "b_cw", [128, 4, 8], F32)
        cb = sb(ph, nc, "b_cb", [128, 8], F32)
        TL = sb(ph, nc, "b_TL", [128, 3, 8], F32)
        xT = [sb(ph, nc, "b_xT%d" % i, [128, 8, 512], F32) for i in range(2)]
        hn = sb(ph, nc, "b_hn", [128, 8, 512], BF16)
        rstd = sb(ph, nc, "b_rstd", [128, 512], F32)
        sqr = Ring("b_sq", [sb(ph, nc, "b_sq%d" % i, [128, 512], F32) for i in range(2)])
        cs = [sb(ph, nc, "b_cs%d" % i, [128, 4, 512], F32) for i in range(2)]
        U = Ring("b_U", [sb(ph, nc, "b_U%d" % i, [128, 515], F32) for i in range(2)])
        Y = Ring("b_Y", [sb(ph, nc, "b_Y%d" % i, [128, 512], F32) for i in range(2)])
        evb = Ring("b_evb", [sb(ph, nc, "b_evb%d" % i, [128, 512], BF16) for i in range(3)])
        evf = Ring("b_evf", [sb(ph, nc, "b_evf%d" % i, [128, 512], F32) for i in range(3)])
        r1 = Ring("b_r1", [sb(ph, nc, "b_r1%d" % i, [128, 512], F32) for i in range(2)])
        tkf = Ring("b_tkf", [sb(ph, nc, "b_tkf%d" % i, [128, 768], F32) for i in range(2)])
        tkb = Ring("b_tkb", [sb(ph, nc, "b_tkb%d" % i, [128, 768], BF16) for i in range(2)])
        ktk = Ring("b_ktk", [sb(ph, nc, "b_ktk%d" % i, [128, 4, 128], F32) for i in range(2)])
        tls = sb(ph, nc, "b_tls", [24, 128], F32)
        P.dma("pool", [(w[:, c, :], T["w_in_l1"][c * 128:(c + 1) * 128, :]) for c in range(8)], [], ["w1"], "w1")
        P.dma("sp", [(gain[:], T["g_mix1"]), (cw[:], T["ssd_cw"]), (cb[:], T["ssd_cb"]), (TL[:], T["ssd_tl"])], [], ["b_gain", "b_small"], "b_small")
        P.op("pool", "memset", [], ["w1r"], ap=wr[:], constant=0.0)
        wq = w[:, :, L1C["qd"]:L1C["qd"] + 1536].rearrange("p c (j e d) -> p c j e d", j=12, e=2)
        wr5 = wr[:].rearrange("p c j (e d) -> p c j e d", e=2)
        for e_ in range(2):
            P.op("dve", "tensor_scalar", ["w1", "w1r"], ["w1r"], out=wr5[:, :, :, e_, 0:8], in0=wq[:, :, :, e_, 8:16], scalar1=-1.0, scalar2=None, op0=ALU.mult)
            P.op("dve", "tensor_copy", ["w1", "w1r"], ["w1r"], out=wr5[:, :, :, e_, 8:16], in_=wq[:, :, :, e_, 0:8])

        def load_group(g):
            b = g % 2
            tk = slice(g * 512, (g + 1) * 512)
            P.dma("sp", [(xT[b][:, c, :], T["X1T"][c * 128:(c + 1) * 128, tk]) for c in range(8)], [], [("b_xT", b)], ("b_xT", b))
            P.dma("sp", [(cs[b][:, i, :], T[n][:, tk]) for i, n in enumerate(("cosD", "sinD", "cosDq", "sinDq"))], [], [("b_cs", b)], ("b_cs", b))

        load_group(0)
        for g in range(G):
            b = g % 2
            tok = slice(g * 512, (g + 1) * 512)
            if g + 1 < G:
                load_group(g + 1)
            x_, c_ = xT[b], cs[b]
            norm_group(P, C, x_, 512, gain, hn, rstd, sqr, psC, [("b_xT", b)] * 8, "b_")

            def fm_group(lhs_fn, M=128):
                pt, ptk = psF.next()
                for c in range(8):
                    P.op("pe", "matmul", [("b_hn", c), "w1", "w1r"], [ptk], out=pt[0:M, :], lhsT=lhs_fn(c), rhs=hn[:, c, :], start=(c == 0), stop=(c == 7))
                return pt, ptk

            for ch in range(8):
                col = L1C["xbc"] + ch * 128
                pt, ptk = fm_group(lambda c, col=col: w[:, c, col:col + 128])
                u_, uk = U.next()
                P.op("act", "copy", [ptk], [(uk, "b")], out=u_[:, 3:515], in_=pt[:])
                P.op("pool", "tensor_copy", [("b_TL", ch), "b_small"], [(uk, "a")], out=u_[:, 0:3], in_=TL[:, :, ch])
                P.op("pool", "tensor_copy", [(uk, "a"), (uk, "b")], [("b_TL", ch)], out=TL[:, :, ch], in_=u_[:, 512:515])
                y_, yk = Y.next()
                ukk = [(uk, "a"), (uk, "b"), "b_small"]
                P.op("dve", "tensor_scalar", ukk, [yk], out=y_[:], in0=u_[:, 3:515], scalar1=cw[:, 3, ch:ch + 1], scalar2=cb[:, ch:ch + 1], op0=ALU.mult, op1=ALU.add)
                for kk in (2, 1, 0):
                    P.op("dve", "scalar_tensor_tensor", ukk + [yk], [yk], out=y_[:], in0=u_[:, kk:kk + 512], scalar=cw[:, kk, ch:ch + 1], in1=y_[:],
                         op0=ALU.mult, op1=ALU.add)
                ev, evk = evb.next()
                P.op("act", "activation", [yk], [evk], out=ev[:], in_=y_[:], func=AF.Silu)
                P.dma("sp", (T["XBC"][ch * 128:(ch + 1) * 128, tok], ev[:]), [evk], [("XBC", g)], evk)
            pt, ptk = fm_group(lambda c: w[:, c, L1C["dt"]:L1C["dt"] + 8], M=8)
            ev, evk = evf.next()
            P.op("act", "copy", [ptk], [evk], out=ev[0:8, :], in_=pt[0:8, :])
            P.dma("sp", (T["DTR"][:, tok], ev[0:8, :]), [evk], [("DTR", g)], evk)
            for j in range(12):
                isq = j < 6
                col = L1C["qd"] + j * 128
                pt, ptk = fm_group(lambda c, col=col: w[:, c, col:col + 128])
                pr, prk = fm_group(lambda c, j=j: wr[:, c, j, :])
                ci = 2 if isq else 0
                a_, ak = r1.next()
                P.op("dve", "tensor_tensor", [ptk, ("b_cs", b)], [ak], out=a_[:], in0=pt[:], in1=c_[:, ci, :], op=ALU.mult)
                ev, evk = evf.next()
                P.op("dve", "tensor_tensor", [prk, ("b_cs", b)], [evk], out=ev[:], in0=pr[:], in1=c_[:, ci + 1, :], op=ALU.mult)
                P.op("pool", "tensor_tensor", [ak, evk], [evk], out=ev[:], in0=ev[:], in1=a_[:], op=ALU.add)
                e2, e2k = evb.next()
                P.op("act", "copy", [evk], [e2k], out=e2[:], in_=ev[:])
                P.dma("sp", (T["QD" if isq else "KD"][j % 6, :, tok], e2[:]), [e2k], [("QKD", g, j)], e2k)
                if not isq:
                    grp = (j - 6) // 2
                    win = SWA[grp][0]
                    for t in range(4):
                        t0 = g * 512 + t * 128
                        if t0 < S - min(win, S):
                            continue
                        r0 = t0 - (S - min(win, S))
                        pt2, pt2k = psT.next()
                        P.op("pe", "transpose", [evk, "identf"], [pt2k], out=pt2[:, 0:128], in_=ev[:, t * 128:(t + 1) * 128], identity=identf[:])
                        kt_, ktk_ = ktk.next()
                        P.op("act", "copy", [pt2k], [ktk_], out=kt_[:, 0, :], in_=pt2[:, 0:128])
                        hh0 = ((j - 6) % 2) * 2
                        P.dma("sp", (T["swa_kv%d" % grp][r0:r0 + 128, 0, hh0:hh0 + 2, :].rearrange("s h d -> s (h d)"), kt_[:, 0, :]),
                              [ktk_], [("swa_k", grp, g, t, j)], ktk_)
            for t in range(4):
                t0 = g * 512 + t * 128
                tsl = slice(t0, t0 + 128)
                pt, ptk = psT.next()
                for c in range(8):
                    P.op("pe", "matmul", [("b_hn", c), "w1"], [ptk], out=pt[:, 0:512], lhsT=hn[:, c, t * 128:(t + 1) * 128], rhs=w[:, c, 0:512], start=(c == 0), stop=(c == 7))
                ev, evk = evf.next()
                P.op("act", "activation", [ptk], [evk], out=ev[:], in_=pt[:], func=AF.Silu)
                P.dma("sp", (T["ZS"][tsl, :], ev[:]), [evk], [("ZS", g)], evk)
                vf, vfk = tkf.next()
                vb_, vbk = tkb.next()
                for half in range(2):
                    pt, ptk = psT.next()
                    c0 = L1C["vd"] + half * 384
                    for c in range(8):
                        P.op("pe", "matmul", [("b_hn", c), "w1"], [ptk], out=pt[:, 0:384], lhsT=hn[:, c, t * 128:(t + 1) * 128], rhs=w[:, c, c0:c0 + 384],
                             start=(c == 0), stop=(c == 7))
                    P.op("act", "copy", [ptk], [(vfk, half)], out=vf[:, half * 384:(half + 1) * 384], in_=pt[:, 0:384])
                    P.op("dve", "tensor_copy", [ptk], [(vbk, half)], out=vb_[:, half * 384:(half + 1) * 384], in_=pt[:, 0:384])
                P.dma("sp", (T["VD"][tsl, :], vb_[:]), [(vbk, 0), (vbk, 1)], [("VD", g)], vbk)
                outs = []
                for grp, (win, dil) in enumerate(SWA):
                    if t0 >= S - min(win, S):
                        r0 = t0 - (S - min(win, S))
                        outs.append((T["swa_kv%d" % grp][r0:r0 + 128, 1, :, :].rearrange("s h d -> s (h d)"), vf[:, grp * 256:(grp + 1) * 256]))
                if outs:
                    P.dma("sp", outs, [(vfk, 0), (vfk, 1)], [("swa_v", g, t)], vfk)
                else:
                    P.readers.setdefault((vfk, 0), [])
        pt, ptk = psF.next()
        P.op("pe", "transpose", [("b_TL", ch) for ch in range(8)] + ["identf"], [ptk], out=pt[0:24, 0:128], in_=TL[:].rearrange("p r c -> p (r c)"), identity=identf[:])
        P.op("act", "copy", [ptk], ["b_tls"], out=tls[:], in_=pt[0:24, 0:128])
        P.dma("sp", [(T["ssd_conv"][r, :].rearrange("(c p) -> c p", p=128), tls[r * 8:(r + 1) * 8, :]) for r in range(3)], ["b_tls"], ["ssd_conv_out"], "b_tls")
        P.barrier()


def phase_B1(nc, P, cfg, T, C):
    S = cfg.S
    NCH = S // 128
    ps = C["ps"]
    psF, psT, psM = Ring("psF", ps[0:4]), Ring("psT", ps[4:6]), Ring("psM", ps[6:8])
    identf, identb, epsc = C["identf"], C["identb"], C["epsc"]
    with ExitStack() as ph:
        dt = sb(ph, nc, "s_dt", [8, S], F32)
        acum = sb(ph, nc, "s_acum", [8, S], F32)
        tmp = sb(ph, nc, "s_tmp", [8, S], F32)
        onesr = sb(ph, nc, "s_ones", [8, 128], F32)
        sm = sb(ph, nc, "s_sm", [8, 4], F32)
        aend = sb(ph, nc, "s_aend", [8, NCH], F32)
        sdec = sb(ph, nc, "s_sdec", [8, NCH], F32)
        P.dma("sp", (dt[:], T["DTR"]), [], ["s_dt"], "s_dt")
        P.dma("sp", [(sm[:, 0:1], T["dt_bias"]), (sm[:, 1:2], T["a_log"])], [], ["s_sm"], "s_sm")
        P.op("pool", "memset", [], ["s_ones"], ap=onesr[:], constant=1.0)
        P.op("pool", "memset", ["s_sm"], ["s_sm"], ap=sm[:, 3:4], constant=1.0)
        P.op("act", "activation", ["s_sm"], ["s_sm"], out=sm[:, 2:3], in_=sm[:, 1:2], func=AF.Exp)
        P.op("dve", "tensor_scalar", ["s_sm"], ["s_sm"], out=sm[:, 2:3], in0=sm[:, 2:3], scalar1=-1.0, scalar2=None, op0=ALU.mult)
        P.op("act", "activation", ["s_dt", "s_sm"], ["s_dt"], out=dt[:], in_=dt[:], func=AF.Exp, bias=sm[:, 0:1], scale=1.0)
        P.op("act", "activation", ["s_dt", "s_sm"], ["s_dt"], out=dt[:], in_=dt[:], func=AF.Ln, bias=sm[:, 3:4], scale=1.0)
        P.op("dve", "tensor_scalar", ["s_dt", "s_sm"], ["s_tmp"], out=tmp[:], in0=dt[:], scalar1=sm[:, 2:3], scalar2=None, op0=ALU.mult)
        for c in range(NCH):
            sl = slice(c * 128, (c + 1) * 128)
            P.op("dve", "tensor_tensor_scan", ["s_tmp", "s_ones"], [("s_acum", c)], out=acum[:, sl], data0=onesr[:], data1=tmp[:, sl], initial=0.0,
                 op0=ALU.mult, op1=ALU.add)
        allc = [("s_acum", c) for c in range(NCH)]
        v3 = lambda t_: t_[:].rearrange("p (c t) -> p c t", t=128)
        bc = lambda t_: t_[:].unsqueeze(2).broadcast_to([8, NCH, 128])
        SQ = T["SQ"]
        P.dma("sp", [(SQ[:, 0, :], acum[:]), (SQ[:, 2, :], dt[:])], allc + ["s_dt"], [("SQ", 0)], "s_q0")
        P.op("act", "activation", allc + ["s_tmp"], ["s_tmp"], out=tmp[:], in_=acum[:], func=AF.Exp)
        P.dma("sp", (SQ[:, 1, :], tmp[:]), ["s_tmp"], [("SQ", 1)], "s_tmp")
        P.op("dve", "tensor_copy", allc, ["s_aend"], out=aend[:], in_=v3(acum)[:, :, 127])
        P.op("dve", "tensor_tensor", allc + ["s_aend", "s_tmp"], ["s_tmp"], out=v3(tmp), in0=bc(aend), in1=v3(acum), op=ALU.subtract)
        P.op("act", "activation", ["s_tmp"], ["s_tmp"], out=tmp[:], in_=tmp[:], func=AF.Exp)
        P.op("dve", "tensor_tensor", ["s_tmp", "s_dt"], ["s_tmp"], out=tmp[:], in0=tmp[:], in1=dt[:], op=ALU.mult)
        P.dma("sp", (SQ[:, 3, :], tmp[:]), ["s_tmp"], [("SQ", 3)], "s_tmp")
        P.op("dve", "tensor_scalar", allc + ["s_tmp"], ["s_tmp"], out=tmp[:], in0=acum[:], scalar1=-1.0, scalar2=None, op0=ALU.mult)
        P.dma("sp", (T["SR"], tmp[:]), ["s_tmp"], ["SR"], "s_tmp")
        P.op("act", "activation", ["s_aend"], ["s_sdec"], out=sdec[:], in_=aend[:], func=AF.Exp)
        P.dma("sp", (T["SDEC"][0, :].rearrange("(h c) -> h c", h=8), sdec[:]), ["s_sdec"], ["SDEC"], "s_sdec")
        P.barrier()
    with ExitStack() as ph:
        cbias = sb(ph, nc, "s_cbias", [128, 128], F32)
        nw = sb(ph, nc, "s_nw", [128, 512], F32)
        dsk = sb(ph, nc, "s_dsk", [128, 8], F32)
        decb = sb(ph, nc, "s_decb", [128, 8 * NCH], F32)
        hT = sb(ph, nc, "s_hT", [128, 8, 64], F32)
        hTb = sb(ph, nc, "s_hTb", [128, 8, 64], BF16)
        NB = 2
        xb = [sb(ph, nc, "s_xb%d" % i, [128, 8, 128], BF16) for i in range(NB)]
        zs = [sb(ph, nc, "s_zs%d" % i, [128, 512], F32) for i in range(NB)]
        sq = [sb(ph, nc, "s_sq%d" % i, [8, 4, 128], F32) for i in range(NB)]
        nB = [sb(ph, nc, "s_nB%d" % i, [128, 8, 128], F32) for i in range(NB)]
        tq = [sb(ph, nc, "s_tq%d" % i, [128, 32], F32) for i in range(NB)]
        xt = [sb(ph, nc, "s_xt%d" % i, [128, 6, 128], BF16) for i in range(NB)]
        CBs = [sb(ph, nc, "s_CB%d" % i, [128, 2, 128], F32) for i in range(NB)]
        YS = [sb(ph, nc, "s_YS%d" % i, [128, 512], F32) for i in range(NB)]
        ycb = [sb(ph, nc, "s_ycb%d" % i, [128, 512], BF16) for i in range(NB)]
        mixs = [sb(ph, nc, "s_mix%d" % i, [128, 4, 128], BF16) for i in range(NB)]
        e1 = Ring("s_e1", [sb(ph, nc, "s_e1%d" % i, [128, 128], F32) for i in range(2)])
        Dm = Ring("s_D", [sb(ph, nc, "s_D%d" % i, [128, 128], F32) for i in range(2)])
        mm = Ring("s_mm", [sb(ph, nc, "s_mm%d" % i, [128, 128], BF16) for i in range(2)])
        mmT = Ring("s_mmT", [sb(ph, nc, "s_mmT%d" % i, [128, 128], BF16) for i in range(2)])
        xd = Ring("s_xd", [sb(ph, nc, "s_xd%d" % i, [128, 64], BF16) for i in range(2)])
        xw = Ring("s_xw", [sb(ph, nc, "s_xw%d" % i, [128, 64], BF16) for i in range(2)])
        ysb = Ring("s_ysb", [sb(ph, nc, "s_ysb%d" % i, [128, 64], F32) for i in range(2)])
        st = Ring("s_st", [sb(ph, nc, "s_st%d" % i, [128, 4], F32) for i in range(2)])
        junk = sb(ph, nc, "s_junk", [128, 256], F32)
        hout = Ring("s_hout", [sb(ph, nc, "s_hout%d" % i, [64, 128], F32) for i in range(2)])
        P.dma("sp", [(cbias[:], T["cbias"]), (nw[:], bcast_rows(T["norm_ssd"])[:, 0, :]), (dsk[:], bcast_rows(T["d_skip"])[:, 0, :]),
                     (decb[:], bcast_rows(T["SDEC"])[:, 0, :])], ["SDEC"], ["s_const"], "s_const")
        P.dma("sp", (hT[:], T["ssd_h0"].rearrange("h p n -> n h p"), dict(allow_slow_non_contiguous=True)), [], [("s_hT", h) for h in range(8)], "s_hT")
        P.op("act", "copy", [("s_hT", h) for h in range(8)], [("s_hTb", h) for h in range(8)], out=hTb[:], in_=hT[:])

        def load_chunk(c):
            b = c % NB
            tok = slice(c * 128, (c + 1) * 128)
            P.dma("sp", (xb[b][:], T["XBC"][:, tok].rearrange("(c p) t -> p c t", p=128)), [], [("s_xb", b)], ("s_xb", b))
            P.dma("sp", (zs[b][:], T["ZS"][tok, :]), [], [("s_zs", b)], ("s_zs", b))
            P.dma("sp", (sq[b][:], T["SQ"][:, :, tok]), [], [("s_sq", b)], ("s_sq", b))
            P.dma("sp", (nB[b][:], bcast_rows(T["SR"][:, tok])), [], [("s_nB", b)], ("s_nB", b))

        load_chunk(0)
        for c in range(NCH):
            b = c % NB
            tok = slice(c * 128, (c + 1) * 128)
            if c + 1 < NCH:
                load_chunk(c + 1)
            pt, ptk = psM.next()
            for j in range(4):
                P.op("pe", "transpose", [("s_sq", b), "identf"], [ptk], out=pt[:, j * 8:(j + 1) * 8], in_=sq[b][:, j, :], identity=identf[0:8, 0:8])
            P.op("dve", "tensor_copy", [ptk], [("s_tq", b)], out=tq[b][:], in_=pt[:, 0:32])
            for j in range(6):
                pX, pXk = psT.next()
                pXb = pX[:].bitcast(BF16)
                P.op("pe", "transpose", [("s_xb", b), "identb"], [pXk], out=pXb[:, 0:128], in_=xb[b][:, j, :], identity=identb[:])
                P.op("act", "copy", [pXk], [("s_xt", b, j)], out=xt[b][:, j, :], in_=pXb[:, 0:128])
            for g2 in range(2):
                pC, pCk = psF.next()
                P.op("pe", "matmul", [("s_xb", b)], [pCk], out=pC[:, 0:128], lhsT=xb[b][:, 6 + g2, :], rhs=xb[b][:, 4 + g2, :], start=True, stop=True)
                P.op("act", "copy", [pCk], [("s_CB", b, g2)], out=CBs[b][:, g2, :], in_=pC[:, 0:128])
            for h in range(8):
                g2 = h // 4
                ac_t = tq[b][:, h:h + 1]
                eac = tq[b][:, 8 + h:9 + h]
                dtc = tq[b][:, 16 + h:17 + h]
                wend = tq[b][:, 24 + h:25 + h]
                xh = xt[b][:, h // 2, (h % 2) * 64:(h % 2) * 64 + 64]
                xkey = ("s_xt", b, h // 2)
                e, ek = e1.next()
                P.op("pool", "tensor_tensor", [("s_nB", b), "s_const"], [ek], out=e[:], in0=nB[b][:, h, :], in1=cbias[:], op=ALU.add)
                Dt, Dk = Dm.next()
                P.op("act", "activation", [ek, ("s_tq", b)], [Dk], out=Dt[:], in_=e[:], func=AF.Exp, bias=ac_t, scale=1.0)
                m_, mk = mm.next()
                P.op("dve", "tensor_tensor", [("s_CB", b, g2), Dk], [mk], out=m_[:], in0=CBs[b][:, g2, :], in1=Dt[:], op=ALU.mult)
                pW, pWk = psT.next()
                pWb = pW[:].bitcast(BF16)
                P.op("pe", "transpose", [mk, "identb"], [pWk], out=pWb[:, 0:128], in_=m_[:], identity=identb[:])
                mT, mTk = mmT.next()
                P.op("act", "copy", [pWk], [mTk], out=mT[:], in_=pWb[:, 0:128])
                x1, x1k = xd.next()
                P.op("pool", "tensor_scalar", [xkey, ("s_tq", b)], [x1k], out=x1[:], in0=xh, scalar1=dtc, scalar2=None, op0=ALU.mult)
                pY, pYk = psF.next()
                P.op("pe", "matmul", [mTk, x1k], [pYk], out=pY[:, 0:64], lhsT=mT[:], rhs=x1[:], start=True, stop=True)
                pI, pIk = psF.next()
                P.op("pe", "matmul", [("s_xb", b), ("s_hTb", h)], [pIk], out=pI[:, 0:64], lhsT=xb[b][:, 6 + g2, :], rhs=hTb[:, h, :], start=True, stop=True)
                y1, y1k = ysb.next()
                P.op("act", "copy", [pYk], [y1k], out=y1[:], in_=pY[:, 0:64])
                P.op("dve", "scalar_tensor_tensor", [pIk, y1k, ("s_tq", b)], [y1k], out=y1[:], in0=pI[:, 0:64], scalar=eac, in1=y1[:], op0=ALU.mult, op1=ALU.add)
                P.op("dve", "scalar_tensor_tensor", [xkey, y1k, "s_const"], [("s_YS", b, h)], out=YS[b][:, h * 64:(h + 1) * 64], in0=xh, scalar=dsk[:, h:h + 1],
                     in1=y1[:], op0=ALU.mult, op1=ALU.add)
                x2, x2k = xw.next()
                P.op("pool", "tensor_scalar", [xkey, ("s_tq", b)], [x2k], out=x2[:], in0=xh, scalar1=wend, scalar2=None, op0=ALU.mult)
                pU, pUk = psF.next()
                P.op("pe", "matmul", [("s_xt", b, 4 + g2), x2k], [pUk], out=pU[:, 0:64], lhsT=xt[b][:, 4 + g2, :], rhs=x2[:], start=True, stop=True)
                P.op("dve", "scalar_tensor_tensor", [pUk, ("s_hT", h), "s_const"], [("s_hT", h)], out=hT[:, h, :], in0=hT[:, h, :],
                     scalar=decb[:, h * NCH + c:h * NCH + c + 1], in1=pU[:, 0:64], op0=ALU.mult, op1=ALU.add)
                P.op("act", "copy", [("s_hT", h)], [("s_hTb", h)], out=hTb[:, h, :], in_=hT[:, h, :])
            ysk = [("s_YS", b, h) for h in range(8)]
            P.op("dve", "tensor_tensor", ysk + [("s_zs", b)], [("s_YS2", b)], out=YS[b][:], in0=YS[b][:], in1=zs[b][:], op=ALU.mult)
            s_, sk = st.next()
            for g2 in range(2):
                P.op("act", "activation", [("s_YS2", b), sk], ["s_junk", sk], out=junk[:], in_=YS[b][:, g2 * 256:(g2 + 1) * 256], func=AF.Square,
                     accum_out=s_[:, g2:g2 + 1])
            P.op("act", "activation", [sk, "epsc"], [sk], out=s_[:, 2:4], in_=s_[:, 0:2], func=AF.Sqrt, bias=epsc[:], scale=1.0 / 256)
            P.op("dve", "reciprocal", [sk], [sk], out=s_[:, 2:4], in_=s_[:, 2:4])
            for g2 in range(2):
                P.op("dve", "scalar_tensor_tensor", [("s_YS2", b), sk, "s_const"], [("s_ycb", b, g2)], out=ycb[b][:, g2 * 256:(g2 + 1) * 256],
                     in0=YS[b][:, g2 * 256:(g2 + 1) * 256], scalar=s_[:, 2 + g2:3 + g2], in1=nw[:, g2 * 256:(g2 + 1) * 256], op0=ALU.mult, op1=ALU.mult)
            for j in range(4):
                pH, pHk = psT.next()
                pHb = pH[:].bitcast(BF16)
                P.op("pe", "transpose", [("s_ycb", b, j // 2), "identb"], [pHk], out=pHb[:, 0:128], in_=ycb[b][:, j * 128:(j + 1) * 128], identity=identb[:])
                P.op("act", "copy", [pHk], [("s_mix", b, j)], out=mixs[b][:, j, :], in_=pHb[:, 0:128])
            P.dma("sp", (T["MIX1"][0:512, tok].rearrange("(h p) t -> p h t", h=4), mixs[b][:]), [("s_mix", b, j) for j in range(4)], [("MIX1a", c)], ("s_mix", b))
        for h in range(8):
            pt, ptk = psF.next()
            P.op("pe", "transpose", [("s_hT", h), "identf"], [ptk], out=pt[0:64, 0:128], in_=hT[:, h, :], identity=identf[:])
            ho, hok = hout.next()
            P.op("act", "copy", [ptk], [hok], out=ho[:], in_=pt[0:64, 0:128])
            P.dma("sp", (T["ssd_h"][h], ho[:]), [hok], [("ssd_h", h)], hok)
        P.barrier()


def phase_B2(nc, P, cfg, T, C):
    S = cfg.S
    ps = C["ps"]
    psS = Ring("psS", ps[0:4])
    psN = Ring("psN", ps[4:6])
    psD = Ring("psD", ps[6:8])
    with ExitStack() as ph:
        accN = sb(ph, nc, "d_accN", [128, S], F32)
        accD = sb(ph, nc, "d_accD", [128, S], F32)
        qT = sb(ph, nc, "d_qT", [128, S], BF16)
        kT = sb(ph, nc, "d_kT", [128, S], BF16)
        vt = sb(ph, nc, "d_vt", [128, S // 128 if S >= 2048 else 16, 128], BF16)
        tri = sb(ph, nc, "d_tri", [128, 2, 128], BF16)
        onesp = sb(ph, nc, "d_onesp", [128, 2, 128], BF16)
        vp = Ring("d_vp", [sb(ph, nc, "d_vp%d" % i, [128, 2, 128], BF16) for i in range(4)])
        pT = Ring("d_pT", [sb(ph, nc, "d_pT%d" % i, [128, 128], BF16) for i in range(6)])
        ob = Ring("d_ob", [sb(ph, nc, "d_ob%d" % i, [128, 512], BF16) for i in range(2)])
        P.dma("sp", [(tri[:, 0, :], T["triT"]), (tri[:, 1, :], T["triR"])], [], ["d_tri"], "d_tri")
        P.op("pool", "memset", [], ["d_onesp"], ap=onesp[:], constant=0.0)
        P.op("pool", "memset", ["d_onesp"], ["d_onesp"], ap=onesp[:, 0, 0:64], constant=1.0)
        P.op("pool", "memset", ["d_onesp"], ["d_onesp"], ap=onesp[:, 1, 64:128], constant=1.0)
        for i in range(4):
            P.op("pool", "memset", [], [("d_vp", i)], ap=vp.tiles[i][:], constant=0.0)
        for ip in range(2):
            first = True
            for g, (win, dil) in enumerate(SWA):
                j = 2 * g + ip
                ncls = S // dil
                n = min(128, ncls)
                ntile = ncls // n
                P.dma("sp", (qT[:], T["QD"][j]), [], ["d_qT"], "d_qT")
                P.dma("sp", (kT[:], T["KD"][j]), [], ["d_kT"], "d_kT")
                c0 = (g * 4 + 2 * ip) * 64
                vsrc = T["VD"][:, c0:c0 + 128].rearrange("(u p r) d -> r p u d", r=dil, p=n)
                P.dma("sp", [(vt[0:n, r * ntile:(r + 1) * ntile, :], vsrc[r]) for r in range(dil)], [], ["d_vt"], "d_vt")
                qv = qT[:].rearrange("p (u a r) -> p r u a", r=dil, a=n)
                kv = kT[:].rearrange("p (u a r) -> p r u a", r=dil, a=n)
                aN = accN[:].rearrange("p (u a r) -> p r u a", r=dil, a=n)
                aD = accD[:].rearrange("p (u a r) -> p r u a", r=dil, a=n)
                for r in range(dil):
                    for u in range(ntile):
                        pN, pNk = psN.next()
                        pD, pDk = psD.next()
                        kts = [u - 1, u] if u > 0 else [u]
                        nmm = 2 * len(kts)
                        imm = 0
                        for ku in kts:
                            v_, vk = vp.next()
                            vsrc_t = vt[0:n, r * ntile + ku, :]
                            P.op("pool", "tensor_copy", ["d_vt", vk], [vk], out=v_[0:n, 0, 0:64], in_=vsrc_t[:, 0:64])
                            P.op("pool", "tensor_copy", ["d_vt", vk], [vk], out=v_[0:n, 1, 64:128], in_=vsrc_t[:, 64:128])
                            for e in range(2):
                                pS, pSk = psS.next()
                                P.op("pe", "matmul", ["d_kT", "d_qT"], [pSk], out=pS[0:n, 0:n], lhsT=kv[e * 64:(e + 1) * 64, r, ku, :],
                                     rhs=qv[e * 64:(e + 1) * 64, r, u, :], start=True, stop=True)
                                p_, pk = pT.next()
                                P.op("act", "activation", [pSk], [pk], out=p_[0:n, 0:n], in_=pS[0:n, 0:n], func=AF.Exp)
                                P.op("dve", "tensor_tensor", [pk, "d_tri"], [pk], out=p_[0:n, 0:n], in0=p_[0:n, 0:n],
                                     in1=tri[0:n, 0 if ku == u else 1, 0:n], op=ALU.mult)
                                P.op("pe", "matmul", [vk, pk], [pNk], out=pN[:, 0:n], lhsT=v_[0:n, e, :], rhs=p_[0:n, 0:n], start=(imm == 0), stop=(imm == nmm - 1))
                                P.op("pe", "matmul", ["d_onesp", pk], [pDk], out=pD[:, 0:n], lhsT=onesp[0:n, e, :], rhs=p_[0:n, 0:n], start=(imm == 0),
                                     stop=(imm == nmm - 1))
                                imm += 1
                        if first:
                            P.op("act", "copy", [pNk], [("d_acc", r, u, g)], out=aN[:, r, u, :], in_=pN[:, 0:n])
                            P.op("dve", "tensor_copy", [pDk], [("d_accD", r, u, g)], out=aD[:, r, u, :], in_=pD[:, 0:n])
                        else:
                            P.op("dve", "tensor_tensor", [pNk, "d_accall"], [("d_acc", r, u, g)], out=aN[:, r, u, :], in0=pN[:, 0:n], in1=aN[:, r, u, :], op=ALU.add)
                            P.op("dve", "tensor_tensor", [pDk, "d_accall"], [("d_accD", r, u, g)], out=aD[:, r, u, :], in0=pD[:, 0:n], in1=aD[:, r, u, :], op=ALU.add)
                keys = [("d_acc", r, u, g) for r in range(dil) for u in range(ntile)] + [("d_accD", r, u, g) for r in range(dil) for u in range(ntile)]
                P.op("pool", "memset", keys + ["d_accall"], ["d_accall"], ap=onesp[0:1, 0, 64:65], constant=0.0)
                first = False
            for t0 in range(0, S, 512):
                P.op("dve", "reciprocal", ["d_accall"], [("d_rc", t0)], out=accD[:, t0:t0 + 512], in_=accD[:, t0:t0 + 512])
                o_, ok = ob.next()
                P.op("dve", "tensor_tensor", [("d_rc", t0), "d_accall"], [ok], out=o_[:], in0=accN[:, t0:t0 + 512], in1=accD[:, t0:t0 + 512], op=ALU.mult)
                P.dma("sp", (T["MIX1"][512 + ip * 128:512 + (ip + 1) * 128, t0:t0 + 512], o_[:]), [ok], [("MIX1b", ip, t0)], ok)
            P.op("pool", "memset", [("d_rc", t0) for t0 in range(0, S, 512)] + ["d_accall"], ["d_accall"], ap=onesp[0:1, 0, 64:65], constant=0.0)
        P.barrier()


def bh_in(dst, src_rows, H):
    return [(dst[b * H:(b + 1) * H, :], src_rows[b].rearrange("(h d) -> h d", h=H)) for b in range(src_rows.shape[0])]


def bh_out(dst_rows, src, H):
    return [(dst_rows[b].rearrange("(h d) -> h d", h=H), src[b * H:(b + 1) * H, :]) for b in range(dst_rows.shape[0])]


def sample_mlstm(nc, P, cfg, T, C, NS):
    epsc = C["epsc"]
    NP = NS * 4
    with ExitStack() as ph:
        c0 = sb(ph, nc, "zm_c0", [NP, 128, 128], F32)
        prod = sb(ph, nc, "zm_prod", [NP, 128, 128], F32)
        v6 = sb(ph, nc, "zm_v6", [NP, 8, 128], F32)
        sc = sb(ph, nc, "zm_sc", [NP, 24], F32)
        bg = sb(ph, nc, "zm_bg", [NP, 2], F32)
        nw = sb(ph, nc, "zm_nw", [NP, 128], F32)
        q, k, v, o, n0, qc, num, tmp = [v6[:, j, :] for j in range(8)]
        Z = T["ZP0"]
        hv = lambda d_, c0_, w_=512: bh_in(d_, Z[:, c0_:c0_ + w_], 4)
        P.dma("sp", hv(q, 0) + hv(k, 512) + hv(v, 1024) + hv(o, 1536) + hv(sc[:, 0:1], 2048, 4) + hv(sc[:, 1:2], 2052, 4) +
                    [(n0, T["mlstm_n0_s"].rearrange("b h d -> (b h) d")),
                     (sc[:, 2:3], T["mlstm_m0_s"]), (bg[:], T["bg16"]), (nw[:], T["nw16"]),
                     (c0[:], T["mlstm_c0_s"].rearrange("b h k v -> (b h) k v"))], ["ZP0"], ["zm_in"], "zm_in")
        K_ = ["zm_in", "zm_w"]
        col = lambda j: sc[:, j:j + 1]
        def op(eng, name, **kw):
            P.op(eng, name, K_, ["zm_w"], **kw)
        op("dve", "tensor_tensor", out=col(3), in0=col(0), in1=bg[:, 0:1], op=ALU.add)
        op("dve", "tensor_tensor", out=col(4), in0=col(1), in1=bg[:, 1:2], op=ALU.add)
        op("act", "activation", out=col(4), in_=col(4), func=AF.Exp, scale=-1.0)
        op("dve", "tensor_scalar", out=col(4), in0=col(4), scalar1=1.0, scalar2=None, op0=ALU.add)
        op("act", "activation", out=col(4), in_=col(4), func=AF.Ln)
        op("dve", "tensor_tensor", out=col(5), in0=col(2), in1=col(4), op=ALU.subtract)
        op("dve", "tensor_tensor", out=col(6), in0=col(5), in1=col(3), op=ALU.max)
        op("dve", "tensor_tensor", out=col(7), in0=col(5), in1=col(6), op=ALU.subtract)
        op("dve", "tensor_tensor", out=col(8), in0=col(3), in1=col(6), op=ALU.subtract)
        op("act", "activation", out=sc[:, 7:9], in_=sc[:, 7:9], func=AF.Exp)
        op("act", "activation", out=col(9), in_=col(6), func=AF.Exp, scale=-1.0)
        op("dve", "tensor_scalar", out=k, in0=k, scalar1=128.0 ** -0.5, scalar2=None, op0=ALU.mult)
        op("dve", "scalar_tensor_tensor", out=tmp, in0=q, scalar=1.0, in1=k, op0=ALU.mult, op1=ALU.mult, accum_out=col(10))
        op("dve", "scalar_tensor_tensor", out=tmp, in0=q, scalar=1.0, in1=n0, op0=ALU.mult, op1=ALU.mult, accum_out=col(11))
        op("dve", "tensor_tensor", out=prod[:], in0=c0[:], in1=q.unsqueeze(2).broadcast_to([NP, 128, 128]), op=ALU.mult)
        op("dve", "tensor_reduce", out=qc, in_=prod[:].rearrange("p k v -> p v k"), axis=AX.X, op=ALU.add)
        op("dve", "tensor_tensor", out=col(12), in0=col(10), in1=col(8), op=ALU.mult)
        op("dve", "tensor_scalar", out=num, in0=v, scalar1=col(12), scalar2=None, op0=ALU.mult)
        op("dve", "scalar_tensor_tensor", out=num, in0=qc, scalar=col(7), in1=num, op0=ALU.mult, op1=ALU.add)
        op("dve", "scalar_tensor_tensor", out=col(13), in0=col(11), scalar=col(7), in1=col(12), op0=ALU.mult, op1=ALU.add)
        op("dve", "scalar_tensor_tensor", out=col(14), in0=col(13), scalar=-1.0, in1=col(13), op0=ALU.mult, op1=ALU.max)
        op("dve", "tensor_tensor", out=col(14), in0=col(14), in1=col(9), op=ALU.max)
        op("dve", "reciprocal", out=col(14), in_=col(14))
        op("act", "activation", out=o, in_=o, func=AF.Sigmoid)
        op("dve", "scalar_tensor_tensor", out=num, in0=num, scalar=col(14), in1=o, op0=ALU.mult, op1=ALU.mult)
        op("act", "activation", out=tmp, in_=num, func=AF.Square, accum_out=col(15))
        P.op("act", "activation", K_ + ["epsc"], ["zm_w"], out=col(16), in_=col(15), func=AF.Sqrt, bias=epsc[0:NP, :], scale=1.0 / 128)
        op("dve", "reciprocal", out=col(16), in_=col(16))
        op("dve", "scalar_tensor_tensor", out=num, in0=num, scalar=col(16), in1=nw[:], op0=ALU.mult, op1=ALU.mult)
        op("dve", "tensor_tensor", out=prod[:], in0=k.unsqueeze(2).broadcast_to([NP, 128, 128]), in1=v.unsqueeze(1).broadcast_to([NP, 128, 128]), op=ALU.mult)
        op("dve", "tensor_scalar", out=c0[:], in0=c0[:], scalar1=col(7), scalar2=None, op0=ALU.mult)
        op("dve", "scalar_tensor_tensor", out=c0[:], in0=prod[:], scalar=col(8), in1=c0[:], op0=ALU.mult, op1=ALU.add)
        op("dve", "tensor_scalar", out=n0, in0=n0, scalar1=col(7), scalar2=None, op0=ALU.mult)
        op("dve", "scalar_tensor_tensor", out=n0, in0=k, scalar=col(8), in1=n0, op0=ALU.mult, op1=ALU.add)
        P.dma("sp", [(T["mlstm_c_s"].rearrange("b h k v -> (b h) k v"), c0[:]), (T["mlstm_n_s"].rearrange("b h d -> (b h) d"), n0),
                     (T["mlstm_m_s"], col(6))] + bh_out(T["MIXS0"][:, 0:512], num, 4), K_, ["MIXS0a"], "zm_out")
        P.barrier()


def sample_moba(nc, P, cfg, T, C, NS):
    NPG = cfg.NPG
    NBK = NPG // 2
    W = max(NBK, 8)
    ps = C["ps"]
    psN, psSc = Ring("psN", ps[0:2]), Ring("psSc", ps[2:3])
    with ExitStack() as ph:
        ptab = sb(ph, nc, "zb_ptab", [128, NS, NPG], I32)
        idx = sb(ph, nc, "zb_idx", [128, NS, NPG], I32)
        iot = sb(ph, nc, "zb_iota", [128, 1], F32)
        qbc = sb(ph, nc, "zb_qbc", [128, NS, 512], F32)
        onec = sb(ph, nc, "zb_onec", [128, 1], F32)
        Kt = Ring("zb_K", [sb(ph, nc, "zb_K%d" % i, [128, 256], F32) for i in range(4)])
        Vt = Ring("zb_V", [sb(ph, nc, "zb_V%d" % i, [128, 257], F32) for i in range(4)])
        prod = Ring("zb_prod", [sb(ph, nc, "zb_prod%d" % i, [128, 4, 128], F32) for i in range(2)])
        Sk = Ring("zb_Sk", [sb(ph, nc, "zb_Sk%d" % i, [128, 8], F32) for i in range(4)])
        NB = sb(ph, nc, "zb_NB", [4, NBK, 257], F32)
        sc = sb(ph, nc, "zb_sc", [4, W], F32)
        mx = sb(ph, nc, "zb_mx", [4, 8], F32)
        sel = sb(ph, nc, "zb_sel", [4, NBK], F32)
        O = sb(ph, nc, "zb_O", [4, 257], F32)
        P.dma("sp", [(ptab[:], bcast_rows(T["page_table_s"])), (iot[:], T["iota_p"]), (qbc[:], bcast_rows(T["ZQB"]))], ["ZP0"], ["zb_in"], "zb_in")
        P.op("pool", "memset", [], ["zb_onec"], ap=onec[:], constant=1.0)
        for i in range(4):
            P.op("pool", "memset", [], [("zb_V1", i)], ap=Vt.tiles[i][:, 256:257], constant=1.0)
        P.op("dve", "tensor_scalar", ["zb_in"], ["zb_idx"], out=idx[:], in0=ptab[:], scalar1=128.0, scalar2=iot[:, 0:1], op0=ALU.mult, op1=ALU.add)
        for b in range(NS):
            P.op("pool", "memset", [("zb_scw", b - 1)], ["zb_sc"], ap=sc[:], constant=-3.0e4)
            pSc, pSck = psSc.next()
            for pg in range(NPG):
                blk = pg // 2
                kt, ktk = Kt.next()
                vt, vtk = Vt.next()
                off = bass.IndirectOffsetOnAxis(ap=idx[:, b, pg:pg + 1], axis=0)
                P.dma("pool", (kt[:], T["cache_k"], dict(_op="indirect_dma_start", out_offset=None, in_offset=off)), ["zb_idx"], [ktk], ktk)
                P.dma("pool", (vt[:, 0:256], T["cache_v"], dict(_op="indirect_dma_start", out_offset=None, in_offset=off)), ["zb_idx"], [vtk], vtk)
                pr, prk = prod.next()
                kv4 = kt[:].rearrange("p (j d) -> p j d", j=2).unsqueeze(2).broadcast_to([128, 2, 2, 128])
                P.op("dve", "tensor_tensor", [ktk, "zb_in"], [prk], out=pr[:].rearrange("p (j g) d -> p j g d", j=2), in0=kv4,
                     in1=qbc[:, b, :].rearrange("p (j g d) -> p j g d", j=2, g=2), op=ALU.mult)
                s_, sk = Sk.next()
                P.op("dve", "tensor_reduce", [prk], [(sk, 0)], out=s_[:, 0:4], in_=pr[:], axis=AX.X, op=ALU.add)
                P.op("act", "activation", [(sk, 0)], [(sk, 1)], out=s_[:, 4:8], in_=s_[:, 0:4], func=AF.Exp)
                P.op("pe", "matmul", [(sk, 0), "zb_onec"], [pSck], out=pSc[0:4, blk:blk + 1], lhsT=s_[:, 0:4], rhs=onec[:], start=(pg % 2 == 0), stop=(pg % 2 == 1))
                if pg % 2 == 0:
                    pN, pNk = psN.next()
                P.op("pe", "matmul", [(sk, 1), vtk, ("zb_V1", Vt.tiles.index(vt))], [pNk], out=pN[0:4, 0:257], lhsT=s_[:, 4:8], rhs=vt[:], start=(pg % 2 == 0), stop=(pg % 2 == 1))
                if pg % 2 == 1:
                    P.op("act", "copy", [pNk, ("zb_Ow", b - 1)], [("zb_NB", blk)], out=NB[:, blk, :], in_=pN[0:4, 0:257])
            P.op("dve", "tensor_copy", [pSck, "zb_sc"], ["zb_sc"], out=sc[:, 0:NBK], in_=pSc[0:4, 0:NBK])
            P.op("dve", "max", ["zb_sc"], ["zb_mx"], out=mx[:], in_=sc[:])
            P.op("dve", "tensor_scalar", ["zb_mx"], ["zb_mx"], out=mx[:, 2:3], in0=mx[:, 2:3], scalar1=-2.0e4, scalar2=None, op0=ALU.max)
            P.op("dve", "tensor_scalar", ["zb_sc", "zb_mx"], ["zb_sel", ("zb_scw", b)], out=sel[:], in0=sc[:, 0:NBK], scalar1=mx[:, 2:3], scalar2=None, op0=ALU.is_ge)
            nbk = [("zb_NB", blk) for blk in range(NBK)]
            P.op("dve", "tensor_tensor", nbk + ["zb_sel"], ["zb_NBw"], out=NB[:], in0=NB[:], in1=sel[:].unsqueeze(2).broadcast_to([4, NBK, 257]), op=ALU.mult)
            P.op("dve", "tensor_reduce", ["zb_NBw"], ["zb_O", ("zb_Ow", b)], out=O[:], in_=NB[:].rearrange("p b d -> p d b"), axis=AX.X, op=ALU.add)
            P.dma("sp", (T["ZOS"][b], O[:]), ["zb_O"], [("ZOS", b)], "zb_O")
        P.barrier()
    NP = NS * 4
    with ExitStack() as ph:
        O16 = sb(ph, nc, "zc_O", [NP, 257], F32)
        v4 = sb(ph, nc, "zc_v4", [NP, 4, 128], F32)
        s2 = sb(ph, nc, "zc_s2", [NP, 2], F32)
        sc = sb(ph, nc, "zc_sc", [NP, 4], F32)
        q, k, v, tmp = [v4[:, j, :] for j in range(4)]
        vbc = L0_COLS["vb"]
        loads = [(O16[:], T["ZOS"].rearrange("b h d -> (b h) d")), (q, T["ZQB"].rearrange("b (h d) -> (b h) d", h=4)), (s2[:], T["sel2"])]
        for b in range(NS):
            for j in range(2):
                r0 = b * 4 + j * 2
                loads.append((v4[r0:r0 + 2, 1, :], bcast_rows(T["ZKB"][b:b + 1, j * 128:(j + 1) * 128], 2)[:, 0, :]))
                loads.append((v4[r0:r0 + 2, 2, :], bcast_rows(T["ZP0"][b:b + 1, vbc + j * 128:vbc + (j + 1) * 128], 2)[:, 0, :]))
        P.dma("sp", loads, [], ["zc_in"], "zc_in")
        K_ = ["zc_in", "zc_w"]
        def op(eng, name, **kw):
            P.op(eng, name, K_, ["zc_w"], **kw)
        op("dve", "scalar_tensor_tensor", out=tmp, in0=q, scalar=1.0, in1=k, op0=ALU.mult, op1=ALU.mult, accum_out=sc[:, 0:1])
        op("act", "activation", out=sc[:, 1:2], in_=sc[:, 0:1], func=AF.Exp)
        op("dve", "tensor_scalar", out=tmp, in0=O16[:, 0:128], scalar1=s2[:, 0:1], scalar2=None, op0=ALU.mult)
        op("dve", "scalar_tensor_tensor", out=tmp, in0=O16[:, 128:256], scalar=s2[:, 1:2], in1=tmp, op0=ALU.mult, op1=ALU.add)
        op("dve", "scalar_tensor_tensor", out=tmp, in0=v, scalar=sc[:, 1:2], in1=tmp, op0=ALU.mult, op1=ALU.add)
        op("dve", "tensor_tensor", out=sc[:, 2:3], in0=O16[:, 256:257], in1=sc[:, 1:2], op=ALU.add)
        op("dve", "reciprocal", out=sc[:, 2:3], in_=sc[:, 2:3])
        op("dve", "tensor_scalar", out=tmp, in0=tmp, scalar1=sc[:, 2:3], scalar2=None, op0=ALU.mult)
        P.dma("sp", bh_out(T["MIXS0"][:, 512:1024], tmp, 4), K_, ["MIXS0b"], "zc_out")
        P.barrier()


def sample_ssd(nc, P, cfg, T, C, NS):
    epsc = C["epsc"]
    NP = NS * 8
    with ExitStack() as ph:
        h0 = sb(ph, nc, "zs_h0", [NP, 64, 128], F32)
        prod = sb(ph, nc, "zs_prod", [NP, 64, 128], F32)
        Bc = sb(ph, nc, "zs_BC", [NP, 2, 128], F32)
        xv = sb(ph, nc, "zs_x", [NP, 4, 64], F32)
        sc = sb(ph, nc, "zs_sc", [NP, 12], F32)
        hv = sb(ph, nc, "zs_hv", [NP, 3], F32)
        x, hC, y, tmp = [xv[:, j, :] for j in range(4)]
        loads = bh_in(x, T["ZXC"][:, 0:512], 8) + [(sc[:, 0:1], T["ZDT"].rearrange("b (h o) -> (b h) o", o=1)),
                 (hv[:], T["hv32"]), (h0[:], T["ssd_h0_s"].rearrange("b h p n -> (b h) p n"))]
        for b in range(NS):
            for g in range(2):
                r0 = b * 8 + g * 4
                for j in range(2):
                    c0_ = 512 + j * 256 + g * 128
                    loads.append((Bc[r0:r0 + 4, j, :], bcast_rows(T["ZXC"][b:b + 1, c0_:c0_ + 128], 4)[:, 0, :]))
        P.dma("sp", loads, ["ZL1"], ["zs_in"], "zs_in")
        K_ = ["zs_in", "zs_w"]
        col = lambda j: sc[:, j:j + 1]
        def op(eng, name, **kw):
            P.op(eng, name, K_, ["zs_w"], **kw)
        op("dve", "tensor_tensor", out=col(1), in0=col(0), in1=hv[:, 0:1], op=ALU.add)
        op("act", "activation", out=col(1), in_=col(1), func=AF.Exp)
        op("dve", "tensor_scalar", out=col(1), in0=col(1), scalar1=1.0, scalar2=None, op0=ALU.add)
        op("act", "activation", out=col(1), in_=col(1), func=AF.Ln)
        op("act", "activation", out=col(2), in_=hv[:, 1:2], func=AF.Exp)
        op("dve", "scalar_tensor_tensor", out=col(3), in0=col(2), scalar=-1.0, in1=col(1), op0=ALU.mult, op1=ALU.mult)
        op("act", "activation", out=col(3), in_=col(3), func=AF.Exp)
        op("dve", "scalar_tensor_tensor", out=prod[:, 0, :], in0=Bc[:, 0, :], scalar=1.0, in1=Bc[:, 1, :], op0=ALU.mult, op1=ALU.mult, accum_out=col(4))
        op("dve", "tensor_tensor", out=col(5), in0=col(4), in1=col(1), op=ALU.mult)
        op("dve", "tensor_tensor", out=prod[:], in0=h0[:], in1=Bc[:, 1, :].unsqueeze(1).broadcast_to([NP, 64, 128]), op=ALU.mult)
        op("dve", "tensor_reduce", out=hC, in_=prod[:], axis=AX.X, op=ALU.add)
        op("dve", "tensor_scalar", out=y, in0=x, scalar1=col(5), scalar2=None, op0=ALU.mult)
        op("dve", "scalar_tensor_tensor", out=y, in0=hC, scalar=col(3), in1=y, op0=ALU.mult, op1=ALU.add)
        op("dve", "scalar_tensor_tensor", out=y, in0=x, scalar=hv[:, 2:3], in1=y, op0=ALU.mult, op1=ALU.add)
        op("dve", "tensor_tensor", out=prod[:], in0=x.unsqueeze(2).broadcast_to([NP, 64, 128]), in1=Bc[:, 0, :].unsqueeze(1).broadcast_to([NP, 64, 128]), op=ALU.mult)
        op("dve", "tensor_scalar", out=h0[:], in0=h0[:], scalar1=col(3), scalar2=None, op0=ALU.mult)
        op("dve", "scalar_tensor_tensor", out=h0[:], in0=prod[:], scalar=col(1), in1=h0[:], op0=ALU.mult, op1=ALU.add)
        P.dma("sp", [(T["ssd_h_s"].rearrange("b h p n -> (b h) p n"), h0[:]), (T["ZYS"].rearrange("b (h p) -> (b h) p", h=8), y)], K_, ["ZYS"], "zs_out")
        P.barrier()
    with ExitStack() as ph:
        ys = sb(ph, nc, "zt_ys", [NS, 512], F32)
        zs = sb(ph, nc, "zt_zs", [NS, 512], F32)
        nw = sb(ph, nc, "zt_nw", [NS, 512], F32)
        junk = sb(ph, nc, "zt_junk", [NS, 256], F32)
        sc = sb(ph, nc, "zt_sc", [NS, 4], F32)
        P.dma("sp", [(ys[:], T["ZYS"]), (zs[:], T["ZZS"]), (nw[:], bcast_rows(T["norm_ssd"], NS)[:, 0, :])], [], ["zt_in"], "zt_in")
        K_ = ["zt_in", "zt_w"]
        P.op("dve", "tensor_tensor", K_, ["zt_w"], out=ys[:], in0=ys[:], in1=zs[:], op=ALU.mult)
        for g in range(2):
            P.op("act", "activation", K_, ["zt_w"], out=junk[:], in_=ys[:, g * 256:(g + 1) * 256], func=AF.Square, accum_out=sc[:, g:g + 1])
        P.op("act", "activation", K_ + ["epsc"], ["zt_w"], out=sc[:, 2:4], in_=sc[:, 0:2], func=AF.Sqrt, bias=epsc[0:NS, :], scale=1.0 / 256)
        P.op("dve", "reciprocal", K_, ["zt_w"], out=sc[:, 2:4], in_=sc[:, 2:4])
        for g in range(2):
            P.op("dve", "scalar_tensor_tensor", K_, ["zt_w"], out=ys[:, g * 256:(g + 1) * 256], in0=ys[:, g * 256:(g + 1) * 256], scalar=sc[:, 2 + g:3 + g],
                 in1=nw[:, g * 256:(g + 1) * 256], op0=ALU.mult, op1=ALU.mult)
        P.dma("sp", (T["MIXS1"][:, 0:512], ys[:]), K_, ["MIXS1a"], "zt_out")
        P.barrier()


def sample_dil(nc, P, cfg, T, C, NS):
    ps = C["ps"]
    psN, psD = Ring("psN", ps[0:2]), Ring("psD", ps[2:4])
    with ExitStack() as ph:
        qbc = sb(ph, nc, "zd_qbc", [128, NS, 768], F32)
        onec = sb(ph, nc, "zd_onec", [128, 1], F32)
        Kt = Ring("zd_K", [sb(ph, nc, "zd_K%d" % i, [128, 256], F32) for i in range(2)])
        Vt = Ring("zd_V", [sb(ph, nc, "zd_V%d" % i, [128, 256], F32) for i in range(2)])
        prod = Ring("zd_prod", [sb(ph, nc, "zd_prod%d" % i, [128, 4, 64], F32) for i in range(2)])
        Sk = Ring("zd_Sk", [sb(ph, nc, "zd_Sk%d" % i, [128, 8], F32) for i in range(2)])
        ds = Ring("zd_ds", [sb(ph, nc, "zd_ds%d" % i, [1, 260], F32) for i in range(2)])
        P.dma("sp", (qbc[:], bcast_rows(T["ZQD"])), ["ZL1"], ["zd_in"], "zd_in")
        P.op("pool", "memset", [], ["zd_onec"], ap=onec[:], constant=1.0)
        for b in range(NS):
            pN, pNk = psN.next()
            pD, pDk = psD.next()
            for g, (win, dil) in enumerate(SWA):
                kt, ktk = Kt.next()
                vt, vtk = Vt.next()
                src = T["swa_c%d" % g][b].rearrange("(k s) c h d -> s k c (h d)", s=dil)[0]
                P.dma("sp", (kt[:], src[:, 0, :]), [], [ktk], ktk)
                P.dma("sp", (vt[:], src[:, 1, :]), [], [vtk], vtk)
                pr, prk = prod.next()
                P.op("dve", "tensor_tensor", [ktk, "zd_in"], [prk], out=pr[:], in0=kt[:].rearrange("p (h d) -> p h d", h=4),
                     in1=qbc[:, b, g * 256:(g + 1) * 256].rearrange("p (h d) -> p h d", h=4), op=ALU.mult)
                s_, sk = Sk.next()
                P.op("dve", "tensor_reduce", [prk], [(sk, 0)], out=s_[:, 0:4], in_=pr[:], axis=AX.X, op=ALU.add)
                P.op("act", "activation", [(sk, 0)], [(sk, 1)], out=s_[:, 4:8], in_=s_[:, 0:4], func=AF.Exp)
                P.op("dve", "tensor_tensor", [vtk, (sk, 1), prk], [prk], out=pr[:], in0=vt[:].rearrange("p (h d) -> p h d", h=4),
                     in1=s_[:, 4:8].unsqueeze(2).broadcast_to([128, 4, 64]), op=ALU.mult)
                P.op("pe", "matmul", [prk, "zd_onec"], [pNk], out=pN[0:1, 0:256], lhsT=onec[:], rhs=pr[:].rearrange("p h d -> p (h d)"), start=(g == 0), stop=(g == 2))
                P.op("pe", "matmul", [(sk, 1), "zd_onec"], [pDk], out=pD[0:1, 0:4], lhsT=onec[:], rhs=s_[:, 4:8], start=(g == 0), stop=(g == 2))
            d_, dk = ds.next()
            P.op("act", "copy", [pNk], [(dk, 0)], out=d_[:, 0:256], in_=pN[0:1, 0:256])
            P.op("dve", "tensor_copy", [pDk], [(dk, 1)], out=d_[:, 256:260], in_=pD[0:1, 0:4])
            P.dma("sp", (T["ZDS"][b:b + 1, :], d_[:]), [(dk, 0), (dk, 1)], [("ZDS", b)], dk)
        P.barrier()
    with ExitStack() as ph:
        qkv = sb(ph, nc, "ze_qkv", [NS, 3, 768], F32)
        D_ = sb(ph, nc, "ze_D", [NS, 260], F32)
        pn = sb(ph, nc, "ze_pn", [NS, 16], F32)
        acc = sb(ph, nc, "ze_acc", [NS, 260], F32)
        P.dma("sp", [(qkv[:, 0, :], T["ZQD"]), (qkv[:, 1, :], T["ZKD"]), (qkv[:, 2, :], T["ZVD"]), (D_[:], T["ZDS"])], [], ["ze_in"], "ze_in")
        K_ = ["ze_in", "ze_w"]
        def op(eng, name, **kw):
            P.op(eng, name, K_, ["ze_w"], **kw)
        op("dve", "tensor_tensor", out=qkv[:, 0, :], in0=qkv[:, 0, :], in1=qkv[:, 1, :], op=ALU.mult)
        op("dve", "tensor_reduce", out=pn[:, 0:12], in_=qkv[:, 0, :].rearrange("p (h d) -> p h d", h=12), axis=AX.X, op=ALU.add)
        op("act", "activation", out=pn[:, 0:12], in_=pn[:, 0:12], func=AF.Exp)
        op("dve", "tensor_tensor", out=qkv[:, 2, :].rearrange("p (h d) -> p h d", h=12), in0=qkv[:, 2, :].rearrange("p (h d) -> p h d", h=12),
           in1=pn[:, 0:12].unsqueeze(2).broadcast_to([NS, 12, 64]), op=ALU.mult)
        op("dve", "tensor_reduce", out=acc[:, 0:256], in_=qkv[:, 2, :].rearrange("p (g s) -> p s g", g=3), axis=AX.X, op=ALU.add)
        op("dve", "tensor_reduce", out=acc[:, 256:260], in_=pn[:, 0:12].rearrange("p (g s) -> p s g", g=3), axis=AX.X, op=ALU.add)
        op("dve", "tensor_tensor", out=acc[:], in0=acc[:], in1=D_[:], op=ALU.add)
        op("dve", "reciprocal", out=acc[:, 256:260], in_=acc[:, 256:260])
        op("dve", "tensor_tensor", out=acc[:, 0:256].rearrange("p (s d) -> p s d", s=4), in0=acc[:, 0:256].rearrange("p (s d) -> p s d", s=4),
           in1=acc[:, 256:260].unsqueeze(2).broadcast_to([NS, 4, 64]), op=ALU.mult)
        P.dma("sp", (T["MIXS1"][:, 512:768], acc[:, 0:256]), K_, ["MIXS1b"], "ze_out")
        P.barrier()


def phase_sample(nc, P, cfg, T, C):
    NS = 4
    NPG = cfg.NPG
    NBK = NPG // 2
    ps = C["ps"]
    psA, psB, psC = Ring("psA", ps[0:4]), Ring("psB", ps[4:6]), Ring("psC", ps[6:8])
    identf, identb, epsc, onesf = C["identf"], C["identb"], C["epsc"], C["onesf"]
    with ExitStack() as outer:
        xT = sb(outer, nc, "z_xT", [128, 8, NS], F32)
        hn = sb(outer, nc, "z_hn", [128, 8, NS], BF16)
        rstd = sb(outer, nc, "z_rstd", [128, NS], F32)
        sqr = Ring("z_sq", [sb(outer, nc, "z_sq%d" % i, [128, NS], F32) for i in range(2)])
        gains = sb(outer, nc, "z_gains", [128, 5, 8], F32)
        xs_tok = sb(outer, nc, "z_xtok", [NS, 1024], F32)
        rope = sb(outer, nc, "z_rope", [NS, 4, 16], F32)
        P.dma("sp", [(xs_tok[:], T["xs"]), (rope[:], T["rope_s"])] +
              [(gains[:, k, :], T[n]) for k, n in enumerate(("g_mix0", "g_ffn0", "g_mix1", "g_ffn1", "g_fin"))], [], ["z_in"], "z_in")
        pt, ptk = psA.next()
        for c in range(8):
            P.op("pe", "transpose", ["z_in", "identf"], [ptk], out=pt[:, c * NS:(c + 1) * NS], in_=xs_tok[:, c * 128:(c + 1) * 128], identity=identf[0:NS, 0:NS])
        P.op("dve", "tensor_copy", [ptk], ["z_xT"], out=xT[:].rearrange("p c n -> p (c n)"), in_=pt[:, 0:8 * NS])

        def do_norm(gi):
            pss, pssk = psC.next()
            for c in range(8):
                q_, qk = sqr.next()
                P.op("act", "activation", ["z_xT"], [qk], out=q_[:], in_=xT[:, c, :], func=AF.Square)
                P.op("pe", "matmul", [qk, "onesf"], [pssk], out=pss[:, 0:NS], lhsT=onesf[:], rhs=q_[:], start=(c == 0), stop=(c == 7))
            P.op("act", "activation", [pssk, "epsc"], ["z_rstd"], out=rstd[:], in_=pss[:, 0:NS], func=AF.Sqrt, bias=epsc[:], scale=1.0 / D)
            P.op("dve", "reciprocal", ["z_rstd"], ["z_rstd"], out=rstd[:], in_=rstd[:])
            for c in range(8):
                P.op("dve", "scalar_tensor_tensor", ["z_xT", "z_in", "z_rstd"], ["z_hn"], out=hn[:, c, :], in0=xT[:, c, :], scalar=gains[:, gi, c:c + 1],
                     in1=rstd[:], op0=ALU.mult, op1=ALU.mult)

        def proj_tok(w, wkey, N, dst, dkey):
            for n0 in range(0, N, 512):
                n1 = min(N, n0 + 512)
                pp, ppk = psA.next()
                for c in range(8):
                    P.op("pe", "matmul", ["z_hn", wkey], [ppk], out=pp[0:NS, 0:n1 - n0], lhsT=hn[:, c, :], rhs=w[:, c, n0:n1], start=(c == 0), stop=(c == 7))
                P.op("act", "copy", [ppk], [dkey], out=dst[:, n0:n1], in_=pp[0:NS, 0:n1 - n0])

        def rope_tok(src, skey, col0, nh, hd, tcol, scale, dst, dkey):
            half = hd // 8
            sv = src[:, col0:col0 + nh * hd].rearrange("p (h d) -> p h d", h=nh)
            dv = dst[:].rearrange("p (h d) -> p h d", h=nh)
            cosb = rope[:, tcol, 0:half].unsqueeze(1).broadcast_to([NS, nh, half])
            sinb = rope[:, tcol + 1, 0:half].unsqueeze(1).broadcast_to([NS, nh, half])
            P.op("dve", "tensor_scalar", [skey], [dkey], out=dst[:], in0=src[:, col0:col0 + nh * hd], scalar1=scale, scalar2=None, op0=ALU.mult)
            t1 = rtmp[:, 0, 0:nh * half].rearrange("p (h d) -> p h d", h=nh)
            t2 = rtmp[:, 1, 0:nh * half].rearrange("p (h d) -> p h d", h=nh)
            x1, x2 = sv[:, :, 0:half], sv[:, :, half:2 * half]
            P.op("dve", "tensor_tensor", [skey, "z_in"], ["z_rtmp"], out=t1, in0=x1, in1=cosb, op=ALU.mult)
            P.op("dve", "tensor_tensor", [skey, "z_in", "z_rtmp"], ["z_rtmp"], out=t2, in0=x2, in1=sinb, op=ALU.mult)
            P.op("dve", "tensor_tensor", ["z_rtmp"], ["z_rtmp"], out=t1, in0=t1, in1=t2, op=ALU.subtract)
            P.op("dve", "tensor_scalar", ["z_rtmp", dkey], [dkey], out=dv[:, :, 0:half], in0=t1, scalar1=scale, scalar2=None, op0=ALU.mult)
            P.op("dve", "tensor_tensor", [skey, "z_in", "z_rtmp"], ["z_rtmp"], out=t1, in0=x2, in1=cosb, op=ALU.mult)
            P.op("dve", "tensor_tensor", [skey, "z_in", "z_rtmp"], ["z_rtmp"], out=t2, in0=x1, in1=sinb, op=ALU.mult)
            P.op("dve", "tensor_tensor", ["z_rtmp"], ["z_rtmp"], out=t1, in0=t1, in1=t2, op=ALU.add)
            P.op("dve", "tensor_scalar", ["z_rtmp", dkey], [dkey], out=dv[:, :, half:2 * half], in0=t1, scalar1=scale, scalar2=None, op0=ALU.mult)

        rtmp = sb(outer, nc, "z_rtmp", [NS, 2, 128], F32)

        def sample_ffn(L, KC, mixname, final):
            with ExitStack() as ph:
                wo = sb(ph, nc, "zf_wo%d" % L, [128, KC, 1024], BF16)
                wup = sb(ph, nc, "zf_wup%d" % L, [128, 8, 5632], BF16)
                wdn = sb(ph, nc, "zf_wdn%d" % L, [128, 22, 1024], BF16)
                mixt = sb(ph, nc, "zf_mixt%d" % L, [NS, KC * 128], F32)
                mixT = sb(ph, nc, "zf_mixT%d" % L, [128, KC, NS], BF16)
                actT = sb(ph, nc, "zf_actT%d" % L, [128, 22, NS], BF16)
                ytok = sb(ph, nc, "zf_ytok%d" % L, [NS, 1024], F32)
                uC = Ring("zf_uC", [sb(ph, nc, "zf_uC%d_%d" % (L, i), [NS, 2, 256], F32) for i in range(2)])
                stC = Ring("zf_stC", [sb(ph, nc, "zf_stC%d_%d" % (L, i), [NS, 2, 2, 256], F32) for i in range(2)])
                cwC = Ring("zf_cwC", [sb(ph, nc, "zf_cwC%d_%d" % (L, i), [NS, 3, 2, 256], F32) for i in range(2)])
                cbC = Ring("zf_cbC", [sb(ph, nc, "zf_cbC%d_%d" % (L, i), [NS, 2, 256], F32) for i in range(2)])
                yC = Ring("zf_yC", [sb(ph, nc, "zf_yC%d_%d" % (L, i), [NS, 2, 256], F32) for i in range(2)])
                tC = Ring("zf_tC", [sb(ph, nc, "zf_tC%d_%d" % (L, i), [NS, 2, 256], F32) for i in range(2)])
                aC = Ring("zf_aC", [sb(ph, nc, "zf_aC%d_%d" % (L, i), [NS, 256], F32) for i in range(2)])
                P.dma("pool", [(wo[:, k, :], T["w_out%d" % L][k * 128:(k + 1) * 128, :]) for k in range(KC)], [], ["zf_wo"], ("zf_wo", L))
                P.dma("pool", [(wup[:, k, :], T["ffn_up%d" % L][k * 128:(k + 1) * 128, :]) for k in range(8)], [], ["zf_wup"], ("zf_wup", L))
                P.dma("pool", [(wdn[:, k, :], T["ffn_down%d" % L][k * 128:(k + 1) * 128, :]) for k in range(22)], [], ["zf_wdn"], ("zf_wdn", L))
                P.dma("sp", (mixt[:], T[mixname]), [], ["zf_in"], "zf_in")
                pt, ptk = psA.next()
                for c in range(KC):
                    P.op("pe", "transpose", ["zf_in", "identf"], [ptk], out=pt[:, c * NS:(c + 1) * NS], in_=mixt[:, c * 128:(c + 1) * 128], identity=identf[0:NS, 0:NS])
                P.op("dve", "tensor_copy", [ptk], ["zf_mixT"], out=mixT[:].rearrange("p c n -> p (c n)"), in_=pt[:, 0:KC * NS])
                for oc in range(8):
                    pp, ppk = psB.next()
                    for k in range(KC):
                        P.op("pe", "matmul", ["zf_mixT", "zf_wo"], [ppk], out=pp[:, 0:NS], lhsT=wo[:, k, oc * 128:(oc + 1) * 128], rhs=mixT[:, k, :], start=(k == 0), stop=(k == KC - 1))
                    P.op("dve", "tensor_tensor", [ppk, "z_xT"], ["z_xT"], out=xT[:, oc, :], in0=pp[:, 0:NS], in1=xT[:, oc, :], op=ALU.add)
                do_norm(1 + 2 * L)
                pA, pAk = psC.next()
                for jj in range(11):
                    cols = [slice(jj * 256, (jj + 1) * 256), slice(2816 + jj * 256, 2816 + (jj + 1) * 256)]
                    u_, uk = uC.next(); st_, stk = stC.next(); cw_, cwk = cwC.next(); cb_, cbk = cbC.next()
                    y_, yk = yC.next(); t_, tk = tC.next(); a_, ak = aC.next()
                    lds = []
                    for e, cs_ in enumerate(cols):
                        lds += [(st_[:, :, e, :], T["ffn_st%d" % L][:, :, cs_]), (cw_[:, :, e, :], bcast_rows(T["ffn_cwr%d" % L][:, cs_], NS)),
                                (cb_[:, e, :], bcast_rows(T["ffn_cbr%d" % L][:, cs_], NS)[:, 0, :])]
                    P.dma("sp", lds, [], [stk, cwk, cbk], stk)
                    for e, cs_ in enumerate(cols):
                        pp, ppk = psA.next()
                        for c in range(8):
                            P.op("pe", "matmul", ["z_hn", "zf_wup"], [ppk], out=pp[0:NS, 0:256], lhsT=hn[:, c, :], rhs=wup[:, c, cs_], start=(c == 0), stop=(c == 7))
                        P.op("act", "copy", [ppk], [(uk, e)], out=u_[:, e, :], in_=pp[0:NS, 0:256])
                    uks = [(uk, 0), (uk, 1)]
                    P.dma("sp", [(T["ffn_conv_s%d" % L][:, 0, cs_], st_[:, 1, e, :]) for e, cs_ in enumerate(cols)] +
                                [(T["ffn_conv_s%d" % L][:, 1, cs_], u_[:, e, :]) for e, cs_ in enumerate(cols)], uks + [stk], [("ffn_conv_s", jj)], uk)
                    P.op("dve", "tensor_tensor", uks + [cwk], [yk], out=y_[:], in0=u_[:], in1=cw_[:, 2, :, :], op=ALU.mult)
                    P.op("dve", "tensor_tensor", [yk, cbk], [yk], out=y_[:], in0=y_[:], in1=cb_[:], op=ALU.add)
                    for k in range(2):
                        P.op("pool", "tensor_tensor", [stk, cwk, tk], [tk], out=t_[:], in0=st_[:, k, :, :], in1=cw_[:, k, :, :], op=ALU.mult)
                        P.op("dve", "tensor_tensor", [yk, tk], [yk], out=y_[:], in0=y_[:], in1=t_[:], op=ALU.add)
                    P.op("act", "activation", [yk, tk], [tk], out=t_[:, 0, :], in_=y_[:, 0, :], func=AF.Silu)
                    P.op("dve", "tensor_tensor", [yk, tk], [ak], out=a_[:], in0=t_[:, 0, :], in1=y_[:, 1, :], op=ALU.mult)
                    for e in range(2):
                        c = jj * 2 + e
                        P.op("pe", "transpose", [ak, "identf"], [pAk], out=pA[:, c * NS:(c + 1) * NS], in_=a_[:, e * 128:(e + 1) * 128], identity=identf[0:NS, 0:NS])
                P.op("dve", "tensor_copy", [pAk], ["zf_actT"], out=actT[:].rearrange("p c n -> p (c n)"), in_=pA[:, 0:22 * NS])
                for oc in range(8):
                    pp, ppk = psB.next()
                    for k in range(22):
                        P.op("pe", "matmul", ["zf_actT", "zf_wdn"], [ppk], out=pp[:, 0:NS], lhsT=wdn[:, k, oc * 128:(oc + 1) * 128], rhs=actT[:, k, :], start=(k == 0), stop=(k == 21))
                    P.op("dve", "tensor_tensor", [ppk, "z_xT"], ["z_xT"], out=xT[:, oc, :], in0=pp[:, 0:NS], in1=xT[:, oc, :], op=ALU.add)
                if final:
                    do_norm(4)
                    for c in range(8):
                        P.op("dve", "scalar_tensor_tensor", ["z_xT", "z_in", "z_rstd"], ["z_xT"], out=xT[:, c, :], in0=xT[:, c, :], scalar=gains[:, 4, c:c + 1],
                             in1=rstd[:], op0=ALU.mult, op1=ALU.mult)
                    for c4 in range(2):
                        pt, ptk = psA.next()
                        for cc in range(4):
                            c = c4 * 4 + cc
                            P.op("pe", "transpose", ["z_xT", "identf"], [ptk], out=pt[0:NS, cc * 128:(cc + 1) * 128], in_=xT[:, c, :], identity=identf[:])
                        P.op("act", "copy", [ptk], ["zf_ytok"], out=ytok[:, c4 * 512:(c4 + 1) * 512], in_=pt[0:NS, :])
                    P.dma("sp", (T["y_s"], ytok[:]), ["zf_ytok"], ["y_s"], "zf_ytok")
                P.barrier()

        with ExitStack() as ph:
            w = sb(ph, nc, "z_w0", [128, 8, N_L0], BF16)
            P0 = sb(ph, nc, "z_P0", [NS, N_L0], F32)
            qb = sb(ph, nc, "z_qb", [NS, 512], F32)
            kb = sb(ph, nc, "z_kb", [NS, 256], F32)
            P.dma("pool", [(w[:, c, :], T["w_in_l0"][c * 128:(c + 1) * 128, :]) for c in range(8)], [], ["z_w0"], "z_w0")
            do_norm(0)
            proj_tok(w, "z_w0", N_L0, P0, "z_P0")
            rope_tok(P0, "z_P0", L0_COLS["qb"], 4, 128, 0, 128.0 ** -0.5, qb, "z_qb")
            rope_tok(P0, "z_P0", L0_COLS["kb"], 2, 128, 0, 1.0, kb, "z_kb")
            P.dma("sp", [(T["moba_k_s"].rearrange("b h d -> b (h d)"), kb[:]), (T["moba_v_s"].rearrange("b h d -> b (h d)"), P0[:, L0_COLS["vb"]:L0_COLS["vb"] + 256]),
                         (T["ZP0"], P0[:]), (T["ZQB"], qb[:]), (T["ZKB"], kb[:])], ["z_P0", "z_qb", "z_kb"], ["ZP0"], "z_P0o")
            P.barrier()
        sample_mlstm(nc, P, cfg, T, C, NS)
        sample_moba(nc, P, cfg, T, C, NS)
        sample_ffn(0, 8, "MIXS0", False)
        with ExitStack() as ph:
            w = sb(ph, nc, "z_w1", [128, 8, N_L1], BF16)
            P1 = sb(ph, nc, "z_P1", [NS, N_L1], F32)
            qd = sb(ph, nc, "z_qd", [NS, 768], F32)
            kd = sb(ph, nc, "z_kd", [NS, 768], F32)
            cst = sb(ph, nc, "z_cst", [NS, 3, 1024], F32)
            cwb = sb(ph, nc, "z_cwb", [NS, 4, 1024], F32)
            cbb = sb(ph, nc, "z_cbb", [NS, 1024], F32)
            xc = sb(ph, nc, "z_xc", [NS, 1024], F32)
            tmp = sb(ph, nc, "z_tmp1", [NS, 1024], F32)
            zs = sb(ph, nc, "z_zs", [NS, 512], F32)
            P.dma("pool", [(w[:, c, :], T["w_in_l1"][c * 128:(c + 1) * 128, :]) for c in range(8)], [], ["z_w1"], "z_w1")
            P.dma("sp", [(cst[:], T["ssd_conv_st"]), (cwb[:], bcast_rows(T["ssd_cwr"], NS)), (cbb[:], bcast_rows(T["ssd_cbr"], NS)[:, 0, :])], [], ["z_l1in"], "z_l1in")
            do_norm(2)
            proj_tok(w, "z_w1", N_L1, P1, "z_P1")
            rope_tok(P1, "z_P1", L1C["qd"], 12, 64, 2, 0.125, qd, "z_qd")
            rope_tok(P1, "z_P1", L1C["kd"], 12, 64, 2, 1.0, kd, "z_kd")
            xr = P1[:, L1C["xbc"]:L1C["xbc"] + 1024]
            P.op("dve", "tensor_tensor", ["z_P1", "z_l1in"], ["z_xc"], out=xc[:], in0=xr, in1=cwb[:, 3, :], op=ALU.mult)
            P.op("dve", "tensor_tensor", ["z_xc", "z_l1in"], ["z_xc"], out=xc[:], in0=xc[:], in1=cbb[:], op=ALU.add)
            for k in range(3):
                P.op("pool", "tensor_tensor", ["z_l1in", "z_tmp1"], ["z_tmp1"], out=tmp[:], in0=cst[:, k, :], in1=cwb[:, k, :], op=ALU.mult)
                P.op("dve", "tensor_tensor", ["z_xc", "z_tmp1"], ["z_xc"], out=xc[:], in0=xc[:], in1=tmp[:], op=ALU.add)
            P.op("act", "activation", ["z_xc"], ["z_xc"], out=xc[:], in_=xc[:], func=AF.Silu)
            P.op("act", "activation", ["z_P1"], ["z_zs"], out=zs[:], in_=P1[:, 0:512], func=AF.Silu)
            vd = P1[:, L1C["vd"]:L1C["vd"] + 768]
            outs = [(T["ssd_conv_s"][:, 0, :], cst[:, 1, :]), (T["ssd_conv_s"][:, 1, :], cst[:, 2, :]), (T["ssd_conv_s"][:, 2, :], xr),
                    (T["ZXC"], xc[:]), (T["ZZS"], zs[:]), (T["ZDT"], P1[:, L1C["dt"]:L1C["dt"] + 8]), (T["ZQD"], qd[:]), (T["ZKD"], kd[:]), (T["ZVD"], vd)]
            for g in range(3):
                outs.append((T["swa_s%d" % g][:, 0, :, :].rearrange("b h d -> b (h d)"), kd[:, g * 256:(g + 1) * 256]))
                outs.append((T["swa_s%d" % g][:, 1, :, :].rearrange("b h d -> b (h d)"), vd[:, g * 256:(g + 1) * 256]))
            P.dma("sp", outs, ["z_P1", "z_qd", "z_kd", "z_xc", "z_zs", "z_l1in"], ["ZL1"], "z_P1o")
            P.barrier()
        sample_ssd(nc, P, cfg, T, C, NS)
        sample_dil(nc, P, cfg, T, C, NS)
        sample_ffn(1, 6, "MIXS1", True)


OUT_SHAPES = [
    (2, 8192, 1024), (32, 1, 1024), (2, 4, 128, 128), (32, 4, 128, 128), (2, 4, 128), (32, 4, 128), (2, 4), (32, 4),
    (2, 8192, 2, 128), (32, 1, 2, 128), (2, 8192, 2, 128), (32, 1, 2, 128), (2, 8, 64, 128), (32, 8, 64, 128),
    (2, 3, 1024), (32, 3, 1024), (2, 128, 2, 4, 64), (32, 1, 2, 4, 64), (2, 512, 2, 4, 64), (32, 1, 2, 4, 64),
    (2, 2048, 2, 4, 64), (32, 1, 2, 4, 64), (2, 2, 2, 5632), (2, 32, 2, 5632)]


def host_consts(S):
    pos = np.arange(S)
    cos, sin = rope_tables(pos, 128)
    sc = np.float32(128.0 ** -0.5)
    tt = np.arange(128)
    cb = np.where(tt[None, :] <= tt[:, None], 0.0, NEG).astype(np.float32)
    c16, s16 = rope_tables(pos, 64)
    cosD = np.ones((128, S), np.float32)
    sinD = np.zeros((128, S), np.float32)
    for e_ in (0, 64):
        cosD[e_:e_ + 16] = c16
        sinD[e_:e_ + 16] = s16
    return dict(ident_f=np.eye(128, dtype=np.float32), ones_f=np.ones((128, 128), np.float32), cbias=cb,
                cosD=cosD, sinD=sinD, cosDq=(cosD * np.float32(0.125)).astype(np.float32), sinDq=(sinD * np.float32(0.125)).astype(np.float32),
                ident_b=np.eye(128, dtype=np.float32).astype(ml_dtypes.bfloat16),
                Eall=np.ascontiguousarray(np.broadcast_to(np.eye(32, dtype=np.float32)[:, :, None], (32, 32, 128))).astype(ml_dtypes.bfloat16),
                triT=(tt[:, None] <= tt[None, :]).astype(np.float32).astype(ml_dtypes.bfloat16),
                triR=(tt[:, None] >= tt[None, :]).astype(np.float32).astype(ml_dtypes.bfloat16),
                cos32=cos, sin32=sin, cos32q=(cos * sc).astype(np.float32), sin32q=(sin * sc).astype(np.float32))


def fm(v):
    v = np.asarray(v, np.float32)
    return np.ascontiguousarray(v.reshape(-1, 128).T)


def make_inmap(inp, x, S, state=None):
    m = dict(host_consts(S))
    m["x"] = np.ascontiguousarray(x)
    m["w_in_l0"] = np.ascontiguousarray(inp["w_in_l0"])
    m["g_mix0"] = fm(inp["norm_mix"][0])
    bg = np.asarray(inp["b_gates_l0"], np.float32)
    m["b_gates0"] = np.ascontiguousarray(bg.reshape(2, 4).T)
    m["mlstm_m0"] = np.zeros((4, 1), np.float32)
    m["mlstm_c0"] = np.zeros((4, 128, 128), np.float32)
    m["mlstm_n0"] = np.zeros((4, 128), np.float32)
    m["norm_mlstm"] = np.asarray(inp["norm_mlstm_l0"], np.float32).reshape(1, 512)
    m["w_out0"] = np.ascontiguousarray(inp["w_out_l0"])
    m["w_out1"] = np.ascontiguousarray(inp["w_out_l1"])
    for L in range(2):
        m["ffn_up%d" % L] = np.ascontiguousarray(inp["ffn_up"][L])
        m["ffn_down%d" % L] = np.ascontiguousarray(inp["ffn_down"][L])
        m["ffn_cw%d" % L] = np.ascontiguousarray(np.asarray(inp["ffn_conv_w"][L], np.float32).reshape(3, 44, 128).transpose(2, 0, 1))
        m["ffn_cb%d" % L] = fm(inp["ffn_conv_b"][L])
        m["g_ffn%d" % L] = fm(inp["norm_ffn"][L])
        m["ffn_tl%d" % L] = np.zeros((128, 2, 44), np.float32)
    m["g_fin"] = fm(inp["norm_final"])
    m["w_in_l1"] = np.ascontiguousarray(inp["w_in_l1"])
    m["g_mix1"] = fm(inp["norm_mix"][1])
    m["ssd_cw"] = np.ascontiguousarray(np.asarray(inp["conv_w_l1"], np.float32).reshape(4, 8, 128).transpose(2, 0, 1))
    m["ssd_cb"] = fm(inp["conv_b_l1"])
    m["ssd_tl"] = np.zeros((128, 3, 8), np.float32)
    m["dt_bias"] = np.asarray(inp["dt_bias_l1"], np.float32).reshape(8, 1)
    m["a_log"] = np.asarray(inp["a_log_l1"], np.float32).reshape(8, 1)
    m["d_skip"] = np.asarray(inp["d_skip_l1"], np.float32).reshape(1, 8)
    m["norm_ssd"] = np.asarray(inp["norm_ssd_l1"], np.float32).reshape(1, 512)
    m["ssd_h0"] = np.zeros((8, 64, 128), np.float32)
    return m


def make_sample_inmap(inp, sl, past_len):
    f = lambda a: np.ascontiguousarray(np.asarray(a, np.float32))
    m = {}
    m["xs"] = f(inp["x_sample"][sl, 0])
    pos = np.array([past_len])
    c128, s128 = rope_tables(pos, 128)
    c64, s64 = rope_tables(pos, 64)
    r = np.zeros((4, 16), np.float32)
    r[0, :16] = c128[:16, 0]; r[1, :16] = s128[:16, 0]; r[2, :8] = c64[:8, 0]; r[3, :8] = s64[:8, 0]
    m["rope_s"] = np.ascontiguousarray(np.broadcast_to(r[None], (4, 4, 16)))
    m["mlstm_c0_s"] = f(inp["state_l0_mlstm_c"][sl]); m["mlstm_n0_s"] = f(inp["state_l0_mlstm_n"][sl])
    m["mlstm_m0_s"] = f(inp["state_l0_mlstm_m"][sl]).reshape(16, 1)
    bg = np.asarray(inp["b_gates_l0"], np.float32).reshape(2, 4).T
    m["bg16"] = np.ascontiguousarray(np.tile(bg, (4, 1)))
    m["nw16"] = np.ascontiguousarray(np.tile(np.asarray(inp["norm_mlstm_l0"], np.float32).reshape(4, 128), (4, 1)))
    m["sel2"] = np.ascontiguousarray(np.tile(np.array([[1, 0], [1, 0], [0, 1], [0, 1]], np.float32), (4, 1)))
    hv = np.stack([np.asarray(inp["dt_bias_l1"], np.float32), np.asarray(inp["a_log_l1"], np.float32), np.asarray(inp["d_skip_l1"], np.float32)], 1)
    m["hv32"] = np.ascontiguousarray(np.tile(hv, (4, 1)))
    ck = np.asarray(inp["cache_l0_moba_k"]); cv = np.asarray(inp["cache_l0_moba_v"])
    m["cache_k"] = ck.reshape(ck.shape[0] * 128, 256); m["cache_v"] = cv.reshape(cv.shape[0] * 128, 256)
    m["page_table_s"] = np.ascontiguousarray(np.asarray(inp["page_table"], np.int32)[sl])
    m["iota_p"] = np.arange(128, dtype=np.float32).reshape(128, 1)
    for L in range(2):
        m["ffn_st%d" % L] = f(inp["state_ffn_conv"][L][sl])
        m["ffn_cwr%d" % L] = f(inp["ffn_conv_w"][L]); m["ffn_cbr%d" % L] = f(inp["ffn_conv_b"][L]).reshape(1, 5632)
    m["ssd_conv_st"] = f(inp["state_l1_ssd_conv"][sl]); m["ssd_cwr"] = f(inp["conv_w_l1"]); m["ssd_cbr"] = f(inp["conv_b_l1"]).reshape(1, 1024)
    m["ssd_h0_s"] = f(inp["state_l1_ssd_h"][sl])
    for g in range(3):
        m["swa_c%d" % g] = f(inp["cache_l1_swa_kv%d" % g][sl])
    return m


_NC_CACHE = {}


ALL_PHASES = ("A0", "A1", "A2", "A3", "B0", "B1", "B2", "B3")


def kernel(**inp):
    S = 8192
    inp = {k: np.asarray(v) for k, v in inp.items()}
    n_pool = inp["cache_l0_moba_k"].shape[0]
    n_pages = inp["page_table"].shape[1]
    cfg = Cfg(S=S, debug=False, phases=ALL_PHASES + ("SMP",), NPG=n_pages, NPOOL=n_pool)
    if "nc" not in _NC_CACHE:
        _NC_CACHE["nc"] = build(cfg)
    nc = _NC_CACHE["nc"]
    base = make_inmap(inp, inp["x_prompt"][0], S)
    in_maps = []
    for c in range(8):
        m = dict(base)
        m["x"] = np.ascontiguousarray(inp["x_prompt"][c % 2])
        m.update(make_sample_inmap(inp, slice(4 * c, 4 * c + 4), n_pages * 128))
        in_maps.append(m)
    res = run_bass_kernel_spmd(nc, in_maps, core_ids=list(range(8)))
    R = res.results
    outs = [None] * 24
    two = lambda name: np.stack([np.asarray(R[0][name]), np.asarray(R[1][name])])
    cat = lambda name: np.concatenate([np.asarray(R[c][name]) for c in range(8)], 0)
    outs[0] = two("y")
    outs[1] = cat("y_s").reshape(32, 1, 1024)
    outs[2] = two("mlstm_c")
    outs[3] = cat("mlstm_c_s")
    outs[4] = two("mlstm_n")
    outs[5] = cat("mlstm_n_s")
    outs[6] = two("mlstm_m")[:, :, 0]
    outs[7] = cat("mlstm_m_s").reshape(32, 4)
    outs[8] = two("moba_k")
    outs[9] = cat("moba_k_s").reshape(32, 1, 2, 128)
    outs[10] = two("moba_v")
    outs[11] = cat("moba_v_s").reshape(32, 1, 2, 128)
    outs[12] = two("ssd_h")
    outs[13] = cat("ssd_h_s")
    outs[14] = two("ssd_conv")
    outs[15] = cat("ssd_conv_s")
    for g in range(3):
        outs[16 + 2 * g] = two("swa_kv%d" % g)
        outs[17 + 2 * g] = cat("swa_s%d" % g).reshape(32, 1, 2, 4, 64)
    outs[22] = np.stack([two("ffn_conv0"), two("ffn_conv1")])
    outs[23] = np.stack([cat("ffn_conv_s0"), cat("ffn_conv_s1")])
    return tuple(np.ascontiguousarray(o, dtype=np.float32) for o in outs)
```

```python
import math
import os
SKIP = os.environ.get('SKIP', '').split(',')
from contextlib import ExitStack

import numpy as np
import ml_dtypes

import concourse.bass as bass
import concourse.mybir as mybir
from concourse.bass_utils import run_bass_kernel_spmd

F32 = mybir.dt.float32
BF16 = mybir.dt.bfloat16
I32 = mybir.dt.int32
ALU = mybir.AluOpType
AF = mybir.ActivationFunctionType
AX = mybir.AxisListType

EPS = 1e-6
NEG = -30000.0


class Prog:
    ENGS = ("pe", "dve", "act", "pool", "sp")

    def __init__(self, nc, stack):
        self.nc = nc
        self.stack = stack
        self.streams = {e: [] for e in self.ENGS}
        self.semh = {}
        self.count = {}
        self.known = {e: {} for e in self.ENGS}
        self.lastw = {}
        self.readers = {}
        for e in ("pe", "dve", "act", "pool"):
            self._mksem("e:" + e)
        self.n_ops = 0

    def _mksem(self, name):
        if name not in self.semh:
            self.semh[name] = self.stack.enter_context(self.nc.semaphore("s%d" % len(self.semh)))
            self.count[name] = 0
        return name

    def _dmasem(self, key):
        if not hasattr(self, "keysem"):
            self.keysem, self.freesem, self.nd = {}, [], 0
        if key not in self.keysem:
            if self.freesem:
                self.keysem[key] = self.freesem.pop()
            else:
                self.nd += 1
                self.keysem[key] = self._mksem("d:%d" % self.nd)
        return self.keysem[key]

    def _deps(self, eng, reads, writes):
        deps = {}
        for k in list(reads) + list(writes):
            lw = self.lastw.get(k)
            if lw is not None:
                deps[lw[0]] = max(deps.get(lw[0], 0), lw[1])
        for k in writes:
            for (s, v) in self.readers.get(k, ()):
                deps[s] = max(deps.get(s, 0), v)
        waits = []
        kn = self.known[eng]
        for s, v in deps.items():
            if eng == "pe" and s == "e:pe":
                continue
            if kn.get(s, 0) >= v:
                continue
            kn[s] = v
            waits.append((s, v))
        return waits

    def _update(self, ident, reads, writes):
        for k in writes:
            self.lastw[k] = ident
            self.readers[k] = []
        for k in reads:
            if k in writes:
                continue
            self.readers.setdefault(k, []).append(ident)

    @staticmethod
    def _excl(reads, writes):
        xr = [k for k in reads if isinstance(k, tuple) and isinstance(k[0], str) and k[0].startswith("ps")]
        if xr:
            writes = list(writes) + xr
            reads = [k for k in reads if k not in xr]
        return reads, writes

    def op(self, eng, name, reads=(), writes=(), **kw):
        reads, writes = self._excl(reads, writes)
        waits = self._deps(eng, reads, writes)
        s = "e:" + eng
        self.count[s] += 1
        self.streams[eng].append((waits, [(name, kw)], s, 1))
        self._update((s, self.count[s]), reads, writes)
        self.n_ops += 1

    def dma(self, q, pairs, reads, writes, key):
        if isinstance(pairs, tuple):
            pairs = [pairs]
        fns = []
        for p in pairs:
            kw = dict(p[2]) if len(p) > 2 else {}
            name = kw.pop("_op", "dma_start")
            fns.append((name, dict(out=p[0], in_=p[1], **kw)))
        waits = self._deps(q, reads, writes)
        if q == "pool":
            if not hasattr(self, "poolsem"):
                self.poolsem = {}
            if key not in self.poolsem:
                self.poolsem[key] = self._mksem("dp:%d" % len(self.poolsem))
            s = self.poolsem[key]
        else:
            s = self._dmasem(key)
        self.count[s] += 16 * len(fns)
        self.streams[q].append((waits, list(fns), s, 16))
        self._update((s, self.count[s]), reads, writes)
        self.n_ops += len(fns)

    def barrier(self):
        snap = dict(self.count)
        for e in self.ENGS:
            waits = []
            for s, v in snap.items():
                if v > 0 and self.known[e].get(s, 0) < v and not (s == "e:pe" and e == "pe"):
                    self.known[e][s] = v
                    waits.append((s, v))
            self.streams[e].append((waits, [], None, 0))
        self.lastw.clear()
        self.readers.clear()
        if hasattr(self, "keysem"):
            self.freesem.extend(self.keysem.values())
            self.keysem.clear()

    def finish(self):
        self.barrier()

    def emit(self):
        nc = self.nc
        semh = self.semh

        def mk(ename):
            def body(eng):
                for waits, fns, sem, inc in self.streams[ename]:
                    for (s, v) in waits:
                        eng.wait_ge(semh[s], v)
                    for (name, kw) in fns:
                        ins = getattr(eng, name)(**kw)
                        ins.then_inc(semh[sem], inc)
            return body

        with nc.Block() as block:
            block.tensor(mk("pe"))
            block.vector(mk("dve"))
            block.scalar(mk("act"))
            block.gpsimd(mk("pool"))
            block.sync(mk("sp"))


class Ring:
    def __init__(self, name, tiles):
        self.name = name
        self.tiles = tiles
        self.i = 0

    def next(self):
        j = self.i % len(self.tiles)
        self.i += 1
        return self.tiles[j], (self.name, j)


def sb(stack, nc, name, shape, dtype):
    return stack.enter_context(nc.sbuf_tensor(name, list(shape), dtype))


D = 1024
L0_COLS = dict(qa=0, ka=512, va=1024, oa=1536, ia=2048, fa=2052, qb=2056, kb=2568, vb=2824)
N_L0 = 3080


class Cfg:
    def __init__(self, S=8192, debug=False, phases=("A0",), lvl=99, NPG=128, NPOOL=5120):
        self.lvl = lvl
        self.NPG = NPG
        self.NPOOL = NPOOL
        self.S = S
        self.debug = debug
        self.phases = phases
        self.G = S // 512


def rope_tables(pos, hd):
    rd = hd // 4
    half = rd // 2
    inv = (500000.0 ** (-(np.arange(half, dtype=np.float32) / np.float32(half)))).astype(np.float32)
    ang = pos.astype(np.float32)[None, :] * inv[:, None]
    cos = np.cos(ang).astype(np.float32)
    sin = np.sin(ang).astype(np.float32)
    return np.concatenate([cos, cos], 0), np.concatenate([sin, sin], 0)


def build(cfg):
    nc = bass.Bass("TRN2", target_bir_lowering=False)
    S, G = cfg.S, cfg.G
    stack = ExitStack()
    T = {}

    def din(name, shape, dt=F32):
        T[name] = nc.dram_tensor(name, list(shape), dt, kind="ExternalInput").ap()
        return T[name]

    def dout(name, shape, dt=F32):
        T[name] = nc.dram_tensor(name, list(shape), dt, kind="ExternalOutput").ap()
        return T[name]

    def dscr(name, shape, dt):
        kind = "ExternalOutput" if cfg.debug else "Internal"
        T[name] = nc.dram_tensor(name, list(shape), dt, kind=kind).ap()
        return T[name]

    x = din("x", [S, D])
    w_in0 = din("w_in_l0", [D, N_L0])
    g_mix0 = din("g_mix0", [128, 8])
    ident_f = din("ident_f", [128, 128])
    ones_f = din("ones_f", [128, 128])
    cos32 = din("cos32", [32, S])
    sin32 = din("sin32", [32, S])
    cos32q = din("cos32q", [32, S])
    sin32q = din("sin32q", [32, S])
    din("b_gates0", [4, 2])
    din("mlstm_m0", [4, 1])
    din("mlstm_c0", [4, 128, 128])
    din("mlstm_n0", [4, 128])
    din("norm_mlstm", [1, 512])
    din("cbias", [128, 128])
    ident_b = din("ident_b", [128, 128], BF16)
    din("Eall", [32, 32, 128], BF16)
    din("triT", [128, 128], BF16)
    din("triR", [128, 128], BF16)
    din("w_out0", [1024, 1024])
    din("w_out1", [768, 1024])
    for L_ in range(2):
        din("ffn_up%d" % L_, [1024, 5632])
        din("ffn_down%d" % L_, [2816, 1024])
        din("ffn_cw%d" % L_, [128, 3, 44])
        din("ffn_cb%d" % L_, [128, 44])
        din("g_ffn%d" % L_, [128, 8])
        din("ffn_tl%d" % L_, [128, 2, 44])
    din("g_fin", [128, 8])
    din("w_in_l1", [1024, N_L1])
    din("g_mix1", [128, 8])
    din("ssd_cw", [128, 4, 8])
    din("ssd_cb", [128, 8])
    din("ssd_tl", [128, 3, 8])
    for n_ in ("cosD", "sinD", "cosDq", "sinDq"):
        din(n_, [128, S])
    din("dt_bias", [8, 1])
    din("a_log", [8, 1])
    din("d_skip", [1, 8])
    din("norm_ssd", [1, 512])
    din("ssd_h0", [8, 64, 128])
    if "SMP" in cfg.phases:
        NS = 4
        din("xs", [NS, 1024]); din("rope_s", [NS, 4, 16])
        din("mlstm_c0_s", [NS, 4, 128, 128]); din("mlstm_n0_s", [NS, 4, 128]); din("mlstm_m0_s", [NS * 4, 1])
        din("bg16", [16, 2]); din("nw16", [16, 128]); din("sel2", [16, 2]); din("hv32", [32, 3])
        din("cache_k", [cfg.NPOOL * 128, 256]); din("cache_v", [cfg.NPOOL * 128, 256])
        din("page_table_s", [NS, cfg.NPG], I32); din("iota_p", [128, 1])
        for L_ in range(2):
            din("ffn_st%d" % L_, [NS, 2, 5632]); din("ffn_cwr%d" % L_, [3, 5632]); din("ffn_cbr%d" % L_, [1, 5632])
            dout("ffn_conv_s%d" % L_, [NS, 2, 5632])
        din("ssd_conv_st", [NS, 3, 1024]); din("ssd_cwr", [4, 1024]); din("ssd_cbr", [1, 1024]); din("ssd_h0_s", [NS, 8, 64, 128])
        for gi_, (win_, dil_) in enumerate(SWA):
            din("swa_c%d" % gi_, [NS, win_, 2, 4, 64])
            dout("swa_s%d" % gi_, [NS, 2, 4, 64])
        dout("y_s", [NS, 1024]); dout("mlstm_c_s", [NS, 4, 128, 128]); dout("mlstm_n_s", [NS, 4, 128]); dout("mlstm_m_s", [NS * 4, 1])
        dout("moba_k_s", [NS, 2, 128]); dout("moba_v_s", [NS, 2, 128]); dout("ssd_h_s", [NS, 8, 64, 128]); dout("ssd_conv_s", [NS, 3, 1024])
        for n_, sh_ in (("ZP0", [NS, N_L0]), ("ZQB", [NS, 512]), ("ZKB", [NS, 256]), ("MIXS0", [NS, 1024]), ("ZOS", [NS, 4, 257]), ("ZXC", [NS, 1024]),
                        ("ZZS", [NS, 512]), ("ZDT", [NS, 8]), ("ZQD", [NS, 768]), ("ZKD", [NS, 768]), ("ZVD", [NS, 768]), ("ZYS", [NS, 512]),
                        ("ZDS", [NS, 260]), ("MIXS1", [NS, 768])):
            dscr(n_, sh_, F32)
    dout("ssd_h", [8, 64, 128])
    dout("ssd_conv", [3, 1024])
    for gi_, (win_, dil_) in enumerate(SWA):
        dout("swa_kv%d" % gi_, [min(win_, S), 2, 4, 64])
    dout("ffn_conv0", [2, 5632])
    dout("ffn_conv1", [2, 5632])
    dout("y", [S, 1024])
    dout("mlstm_c", [4, 128, 128])
    dout("mlstm_n", [4, 128])
    dout("mlstm_m", [4, 1])
    mk_out = dout("moba_k", [S, 2, 128])
    mv_out = dout("moba_v", [S, 2, 128])
    XT = dscr("XT", [D, S], F32)
    QA = dscr("QA", [4, 128, S], BF16)
    KA = dscr("KA", [4, 128, S], BF16)
    VA = dscr("VA", [S, 512], BF16)
    OA = dscr("OA", [S, 512], F32)
    dscr("GAI", [4, S], F32)
    dscr("GAF", [4, S], F32)
    dscr("KAT", [S, 512], BF16)
    QB = dscr("QB", [4, 128, S], BF16)
    KB = dscr("KB", [2, 128, S], BF16)
    VB = dscr("VB", [S, 256], BF16)
    KM = dscr("KM", [128, 2, S // 256], F32)
    dscr("RQ", [4, 4, S], F32)
    dscr("GR", [4, S], F32)
    dscr("DEC", [1, 4 * (S // 128)], F32)
    dscr("MIX0", [1024, S], BF16)
    dscr("X1T", [1024, S], F32)
    dscr("XBC", [1024, S], BF16)
    dscr("DTR", [8, S], F32)
    dscr("QD", [6, 128, S], BF16)
    dscr("KD", [6, 128, S], BF16)
    dscr("VD", [S, 768], BF16)
    dscr("ZS", [S, 512], F32)
    dscr("SQ", [8, 4, S], F32)
    dscr("SR", [8, S], F32)
    dscr("SDEC", [1, 8 * (S // 128)], F32)
    dscr("MIX1", [768, S], BF16)

    with stack:
        P = Prog(nc, stack)
        ps = [stack.enter_context(nc.psum_tensor("ps%d" % i, [128, 512], F32)) for i in range(8)]
        C = dict(ps=ps, psF=Ring("psF", ps[0:4]), psT=Ring("psT", ps[4:6]), psM=Ring("psM", ps[6:8]))
        C["identf"] = identf = sb(stack, nc, "identf", [128, 128], F32)
        C["onesf"] = onesf = sb(stack, nc, "onesf", [128, 128], F32)
        C["epsc"] = epsc = sb(stack, nc, "epsc", [128, 1], F32)
        P.dma("sp", (identf[:], ident_f), [], ["identf"], "identf")
        P.dma("sp", (onesf[:], ones_f), [], ["onesf"], "onesf")
        C["identb"] = identb = sb(stack, nc, "identb", [128, 128], BF16)
        P.dma("sp", (identb[:], ident_b), [], ["identb"], "identb")
        P.op("pool", "memset", [], ["epsc"], ap=epsc[:], constant=EPS)

        if "A0" in cfg.phases:
            phase_A0(nc, P, cfg, T, C)
        if "A1" in cfg.phases:
            phase_A1(nc, P, cfg, T, C)
        if "A2" in cfg.phases:
            phase_A2(nc, P, cfg, T, C)
        if "A3" in cfg.phases:
            phase_FFN(nc, P, cfg, T, C, 0, 8, "MIX0", "XT", "X1T")
        if "B0" in cfg.phases:
            phase_B0(nc, P, cfg, T, C)
        if "B1" in cfg.phases:
            phase_B1(nc, P, cfg, T, C)
        if "B2" in cfg.phases:
            phase_B2(nc, P, cfg, T, C)
        if "B3" in cfg.phases:
            phase_FFN(nc, P, cfg, T, C, 1, 6, "MIX1", "X1T", None, final=True)
        if "SMP" in cfg.phases:
            phase_sample(nc, P, cfg, T, C)
        P.finish()
        P.emit()
    return nc


def phase_A0(nc, P, cfg, T, C):
    S, G = cfg.S, cfg.G
    psF, psT, psM = C["psF"], C["psT"], C["psM"]
    identf, onesf, epsc = C["identf"], C["onesf"], C["epsc"]
    x = T["x"]
    with ExitStack() as ph:
        w = sb(ph, nc, "w0", [128, 8, N_L0], BF16)
        wr = sb(ph, nc, "w0r", [128, 8, 6, 32], BF16)
        gain = sb(ph, nc, "gain0", [128, 8], F32)
        xtok = [sb(ph, nc, "xtok%d" % i, [128, 4, D], F32) for i in range(2)]
        xT = [sb(ph, nc, "xT%d" % i, [128, 8, 512], F32) for i in range(2)]
        sq = sb(ph, nc, "sq", [128, 8, 512], F32)
        rstd = sb(ph, nc, "rstd", [128, 512], F32)
        hnT = [sb(ph, nc, "hnT%d" % i, [128, 8, 512], BF16) for i in range(2)]
        cs = [sb(ph, nc, "cs%d" % i, [32, 4, 512], F32) for i in range(2)]
        ev_bf = Ring("evbf", [sb(ph, nc, "evbf%d" % i, [128, 512], BF16) for i in range(4)])
        ev_f = Ring("evf", [sb(ph, nc, "evf%d" % i, [128, 512], F32) for i in range(3)])
        rt = Ring("rt", [sb(ph, nc, "rt%d" % i, [32, 2, 512], F32) for i in range(2)])
        ktok = Ring("ktok", [sb(ph, nc, "ktok%d" % i, [128, 4, 256], F32) for i in range(2)])
        vtokf = Ring("vtokf", [sb(ph, nc, "vtokf%d" % i, [128, 512], F32) for i in range(3)])
        vtokb = Ring("vtokb", [sb(ph, nc, "vtokb%d" % i, [128, 512], BF16) for i in range(3)])
        km = sb(ph, nc, "km", [128, 2, S // 256], F32)

        if 'wcast' not in SKIP:
            P.dma("pool", [(w[:, c, :], T["w_in_l0"][c * 128:(c + 1) * 128, :]) for c in range(8)], [], ["w0"], "w0")
        P.dma("sp", (gain[:], T["g_mix0"]), [], ["gain0"], "gain0")
        w6 = w[:, :, 2056:2056 + 768].rearrange("p c (j d) -> p c j d", j=6)
        if 'wrot' not in SKIP:
          P.op("dve", "tensor_scalar", ["w0"], ["w0r"], out=wr[:, :, :, 0:16], in0=w6[:, :, :, 16:32], scalar1=-1.0,
             scalar2=None, op0=ALU.mult)
        if 'wrot' not in SKIP:
          P.op("dve", "tensor_copy", ["w0", "w0r"], ["w0r"], out=wr[:, :, :, 16:32], in_=w6[:, :, :, 0:16])

        def load_group(g):
            b = g % 2
            tk = slice(g * 512, (g + 1) * 512)
            P.dma("sp", (xtok[b][:], x[tk, :].rearrange("(t p) d -> p t d", p=128)), [], [("xtok", b)], ("xtok", b))
            if 'cs' not in SKIP:
              P.dma("sp", [(cs[b][:, i, :], T[n][:, tk]) for i, n in enumerate(("cos32", "sin32", "cos32q", "sin32q"))],
                  [], [("cs", b)], ("cs", b))

        load_group(0)
        for g in range(G):
            b = g % 2
            if g + 1 < G:
                load_group(g + 1)
            xt, xTg, hn, c_ = xtok[b], xT[b], hnT[b], cs[b]
            tok = slice(g * 512, (g + 1) * 512)
            pss, pssk = psM.next()
            for c in range(8):
                pt, ptk = psT.next()
                for t in range(4):
                    P.op("pe", "transpose", [("xtok", b), "identf"], [ptk], out=pt[:, t * 128:(t + 1) * 128],
                         in_=xt[:, t, c * 128:(c + 1) * 128], identity=identf[:])
                P.op("dve", "tensor_copy", [ptk], [("xT", b, c)], out=xTg[:, c, :], in_=pt[:])
                P.op("act", "activation", [ptk], [("sq", c)], out=sq[:, c, :], in_=pt[:], func=AF.Square)
                P.op("pe", "matmul", [("sq", c), "onesf"], [pssk], out=pss[:], lhsT=onesf[:], rhs=sq[:, c, :],
                     start=(c == 0), stop=(c == 7))
            P.dma("sp", [(T["XT"][c * 128:(c + 1) * 128, tok], xTg[:, c, :]) for c in range(8)],
                  [("xT", b, c) for c in range(8)], [("XT", g)], ("xTst", b))
            if cfg.lvl < 2:
                continue
            P.op("act", "activation", [pssk, "epsc"], ["rstd"], out=rstd[:], in_=pss[:], func=AF.Sqrt, bias=epsc[:], scale=1.0 / D)
            P.op("dve", "reciprocal", ["rstd"], ["rstd"], out=rstd[:], in_=rstd[:])
            for c in range(8):
                P.op("dve", "scalar_tensor_tensor", [("xT", b, c), "gain0", "rstd"], [("hnT", b, c)],
                     out=hn[:, c, :], in0=xTg[:, c, :], scalar=gain[:, c:c + 1], in1=rstd[:], op0=ALU.mult, op1=ALU.mult)

            def fm_group(col0, M, wt=None):
                pt, ptk = psF.next()
                for c in range(8):
                    lhs = (w[:, c, col0:col0 + M] if wt is None else wt(c))
                    P.op("pe", "matmul", [("hnT", b, c), "w0", "w0r"], [ptk], out=pt[0:M, :], lhsT=lhs, rhs=hn[:, c, :],
                         start=(c == 0), stop=(c == 7))
                return pt, ptk

            if cfg.lvl < 3:
                continue
            for h in range(4):
                pt, ptk = fm_group(L0_COLS["qa"] + 128 * h, 128)
                ev, evk = ev_bf.next()
                P.op("act", "copy", [ptk], [evk], out=ev[:], in_=pt[:])
                P.dma("sp", (T["QA"][h, :, tok], ev[:]), [evk], [("QA", g)], evk)
                pt, ptk = fm_group(L0_COLS["ka"] + 128 * h, 128)
                ev, evk = ev_bf.next()
                P.op("dve", "tensor_scalar", [ptk], [evk], out=ev[:], in0=pt[:], scalar1=128.0 ** -0.5, scalar2=None, op0=ALU.mult)
                P.dma("sp", (T["KA"][h, :, tok], ev[:]), [evk], [("KA", g)], evk)
            for gi_, gname in enumerate(("GAI", "GAF")):
                pt, ptk = fm_group(L0_COLS["ia"] + 4 * gi_, 4)
                ev, evk = ev_f.next()
                P.op("act", "copy", [ptk], [evk], out=ev[0:4, :], in_=pt[0:4, :])
                P.dma("sp", (T[gname][:, tok], ev[0:4, :]), [evk], [(gname, g)], evk)
            if cfg.lvl < 4:
                continue
            for j in range(6):
                isq = j < 4
                pt, ptk = fm_group(2056 + 128 * j, 128)
                pr, prk = fm_group(0, 32, wt=lambda c, j=j: wr[:, c, j, :])
                r, rk = rt.next()
                ci = 2 if isq else 0
                P.op("dve", "tensor_tensor", [ptk, ("cs", b)], [rk], out=r[:, 0, :], in0=pt[0:32, :], in1=c_[:, ci, :], op=ALU.mult)
                P.op("dve", "tensor_tensor", [prk, ("cs", b), rk], [rk], out=r[:, 1, :], in0=pr[0:32, :], in1=c_[:, ci + 1, :], op=ALU.mult)
                if isq:
                    ev, evk = ev_bf.next()
                    P.op("act", "activation", [ptk], [evk], out=ev[:], in_=pt[:], func=AF.Copy, scale=128.0 ** -0.5)
                    P.op("pool", "tensor_tensor", [rk, evk], [evk], out=ev[0:32, :], in0=r[:, 0, :], in1=r[:, 1, :], op=ALU.add)
                    P.dma("sp", (T["QB"][j, :, tok], ev[:]), [evk], [("QB", g)], evk)
                else:
                    hh = j - 4
                    ev, evk = ev_f.next()
                    P.op("act", "copy", [ptk], [evk], out=ev[:], in_=pt[:])
                    P.op("pool", "tensor_tensor", [rk, evk], [evk], out=ev[0:32, :], in0=r[:, 0, :], in1=r[:, 1, :], op=ALU.add)
                    evb, evbk = ev_bf.next()
                    P.op("pool", "tensor_copy", [evk], [evbk], out=evb[:], in_=ev[:])
                    P.dma("sp", (T["KB"][hh, :, tok], evb[:]), [evbk], [("KB", g)], evbk)
                    P.op("dve", "tensor_reduce", [evk], [("km", hh, g)], out=km[:, hh, 2 * g:2 * g + 2],
                         in_=ev[:].rearrange("p (b k) -> p b k", b=2), axis=AX.X, op=ALU.add)
                    if hh == 0:
                        kt, ktk = ktok.next()
                    pt2, pt2k = psT.next()
                    for t in range(4):
                        P.op("pe", "transpose", [evk, "identf"], [pt2k], out=pt2[:, t * 128:(t + 1) * 128],
                             in_=ev[:, t * 128:(t + 1) * 128], identity=identf[:])
                    P.op("act", "copy", [pt2k], [(ktk, hh)], out=kt[:, :, hh * 128:(hh + 1) * 128],
                         in_=pt2[:].rearrange("p (t d) -> p t d", t=4))
                    if hh == 1:
                        P.dma("sp", (T["moba_k"][tok, :, :].rearrange("(t p) h d -> p t (h d)", p=128), kt[:]),
                              [(ktk, 0), (ktk, 1)], [("moba_k", g)], ktk)
            if cfg.lvl < 5:
                continue
            for t in range(4):
                tsl = slice(g * 512 + t * 128, g * 512 + (t + 1) * 128)
                for name, col0, n in (("va", 1024, 512), ("kat", 512, 512), ("oa", 1536, 512), ("vb", 2824, 256)):
                    pt, ptk = psT.next()
                    for c in range(8):
                        P.op("pe", "matmul", [("hnT", b, c), "w0"], [ptk], out=pt[:, 0:n], lhsT=hn[:, c, t * 128:(t + 1) * 128],
                             rhs=w[:, c, col0:col0 + n], start=(c == 0), stop=(c == 7))
                    if name == "va":
                        ev, evk = vtokb.next()
                        P.op("act", "copy", [ptk], [evk], out=ev[:], in_=pt[:])
                        P.dma("sp", (T["VA"][tsl, :], ev[:]), [evk], [("VA", g)], evk)
                    elif name == "kat":
                        ev, evk = vtokb.next()
                        P.op("dve", "tensor_scalar", [ptk], [evk], out=ev[:], in0=pt[:], scalar1=128.0 ** -0.5, scalar2=None, op0=ALU.mult)
                        P.dma("sp", (T["KAT"][tsl, :], ev[:]), [evk], [("KAT", g)], evk)
                    elif name == "oa":
                        ev, evk = vtokf.next()
                        P.op("act", "activation", [ptk], [evk], out=ev[:], in_=pt[:], func=AF.Sigmoid)
                        P.dma("sp", (T["OA"][tsl, :], ev[:]), [evk], [("OA", g)], evk)
                    else:
                        ev, evk = vtokf.next()
                        P.op("dve", "tensor_copy", [ptk], [evk], out=ev[:, 0:256], in_=pt[:, 0:256])
                        P.dma("sp", (T["moba_v"][tsl, :, :].rearrange("s h d -> s (h d)"), ev[:, 0:256]), [evk], [("moba_v", g)], evk)
                        evb, evbk = vtokb.next()
                        P.op("pool", "tensor_copy", [evk], [evbk], out=evb[:, 0:256], in_=ev[:, 0:256])
                        P.dma("sp", (T["VB"][tsl, :], evb[:, 0:256]), [evbk], [("VB", g)], evbk)
        if 'km' in SKIP:
            P.barrier()
            return
        P.op("dve", "tensor_scalar", [("km", hh, g) for hh in range(2) for g in range(G)], ["kmall"],
             out=km[:], in0=km[:], scalar1=1.0 / 256, scalar2=None, op0=ALU.mult)
        P.dma("sp", (T["KM"], km[:]), ["kmall"], ["KM"], "km")
        P.barrier()


def bcast_rows(ap2d, nparts=128):
    return ap2d.unsqueeze(0).broadcast_to([nparts] + list(ap2d.shape))


def phase_A1(nc, P, cfg, T, C):
    S = cfg.S
    NCH = S // 128
    psF, psT, psM = C["psF"], C["psT"], C["psM"]
    identf, identb = C["identf"], C["identb"]
    with ExitStack() as ph:
        ig = sb(ph, nc, "r_ig", [4, S], F32)
        nlf = sb(ph, nc, "r_nlf", [4, S], F32)
        ncum = sb(ph, nc, "r_ncum", [4, S], F32)
        pm = sb(ph, nc, "r_pm", [4, S], F32)
        tmp = sb(ph, nc, "r_tmp", [4, S], F32)
        onesr = sb(ph, nc, "r_ones", [4, 128], F32)
        zerosr = sb(ph, nc, "r_zeros", [4, 128], F32)
        bg = sb(ph, nc, "r_bg", [4, 2], F32)
        nbf = sb(ph, nc, "r_nbf", [4, 1], F32)
        onec = sb(ph, nc, "r_onec", [4, 1], F32)
        Gc = sb(ph, nc, "r_G", [4, NCH], F32)
        Fc = sb(ph, nc, "r_F", [4, NCH], F32)
        mnext = sb(ph, nc, "r_mn", [4, NCH], F32)
        mc = sb(ph, nc, "r_mc", [4, NCH], F32)
        dec = sb(ph, nc, "r_dec", [4, NCH], F32)
        m0 = sb(ph, nc, "r_m0", [4, 1], F32)
        P.dma("sp", (ig[:], T["GAI"]), [("GAI", g) for g in range(cfg.G)], ["r_ig"], "r_ig")
        P.dma("sp", (nlf[:], T["GAF"]), [("GAF", g) for g in range(cfg.G)], ["r_nlf"], "r_nlf")
        P.dma("sp", (bg[:], T["b_gates0"]), [], ["r_bg"], "r_bg")
        P.dma("sp", (m0[:], T["mlstm_m0"]), [], ["r_m0"], "r_m0")
        P.op("pool", "memset", [], ["r_ones"], ap=onesr[:], constant=1.0)
        P.op("pool", "memset", [], ["r_zeros"], ap=zerosr[:], constant=0.0)
        P.op("pool", "memset", [], ["r_onec"], ap=onec[:], constant=1.0)
        P.op("dve", "tensor_scalar", ["r_bg"], ["r_nbf"], out=nbf[:], in0=bg[:, 1:2], scalar1=-1.0, scalar2=None, op0=ALU.mult)
        P.op("dve", "tensor_scalar", ["r_ig", "r_bg"], ["r_ig"], out=ig[:], in0=ig[:], scalar1=bg[:, 0:1], scalar2=None, op0=ALU.add)
        P.op("act", "activation", ["r_nlf", "r_nbf"], ["r_nlf"], out=nlf[:], in_=nlf[:], func=AF.Exp, bias=nbf[:], scale=-1.0)
        P.op("act", "activation", ["r_nlf", "r_onec"], ["r_nlf"], out=nlf[:], in_=nlf[:], func=AF.Ln, bias=onec[:], scale=1.0)
        for c in range(NCH):
            sl = slice(c * 128, (c + 1) * 128)
            P.op("dve", "tensor_tensor_scan", ["r_nlf", "r_ones"], [("r_ncum", c)], out=ncum[:, sl], data0=onesr[:], data1=nlf[:, sl],
                 initial=0.0, op0=ALU.mult, op1=ALU.add)
        allc = [("r_ncum", c) for c in range(NCH)]
        P.op("dve", "tensor_tensor", ["r_ig"] + allc, ["r_ig"], out=ig[:], in0=ig[:], in1=ncum[:], op=ALU.add)
        P.op("dve", "tensor_reduce", ["r_ig"], ["r_G"], out=Gc[:], in_=ig[:].rearrange("p (c t) -> p c t", t=128), axis=AX.X, op=ALU.max)
        P.op("dve", "tensor_scalar", allc, ["r_F"], out=Fc[:], in0=ncum[:].rearrange("p (c t) -> p c t", t=128)[:, :, 127], scalar1=-1.0,
             scalar2=None, op0=ALU.mult)
        P.op("dve", "tensor_tensor_scan", ["r_G", "r_F", "r_m0"], ["r_mn"], out=mnext[:], data0=Gc[:], data1=Fc[:], initial=m0[:],
             op0=ALU.max, op1=ALU.add)
        P.op("dve", "tensor_copy", ["r_m0"], ["r_mc"], out=mc[:, 0:1], in_=m0[:])
        if NCH > 1:
            P.op("dve", "tensor_copy", ["r_mn", "r_mc"], ["r_mc"], out=mc[:, 1:NCH], in_=mnext[:, 0:NCH - 1])
        for c in range(NCH):
            sl = slice(c * 128, (c + 1) * 128)
            P.op("dve", "tensor_tensor_scan", ["r_ig", "r_zeros", "r_mc"], [("r_pm", c)], out=pm[:, sl], data0=ig[:, sl], data1=zerosr[:],
                 initial=mc[:, c:c + 1], op0=ALU.max, op1=ALU.add)
        allp = [("r_pm", c) for c in range(NCH)]
        v3 = lambda t_: t_[:].rearrange("p (c t) -> p c t", t=128)
        bc = lambda t_: t_[:].unsqueeze(2).broadcast_to([4, NCH, 128])
        RQ = T["RQ"]
        P.op("dve", "tensor_scalar", allp, ["r_tmp"], out=tmp[:], in0=pm[:], scalar1=-1.0, scalar2=None, op0=ALU.mult)
        P.dma("sp", (RQ[:, 0, :], tmp[:]), ["r_tmp"], [("RQ", 0)], "r_tmp")
        P.op("dve", "tensor_tensor", allp + ["r_mc", "r_tmp"], ["r_tmp"], out=v3(tmp), in0=bc(mc), in1=v3(pm), op=ALU.subtract)
        P.op("act", "activation", ["r_tmp"], ["r_tmp"], out=tmp[:], in_=tmp[:], func=AF.Exp)
        P.dma("sp", (RQ[:, 1, :], tmp[:]), ["r_tmp"], [("RQ", 1)], "r_tmp")
        P.op("dve", "tensor_tensor", allp + allc + ["r_tmp"], ["r_tmp"], out=tmp[:], in0=ncum[:], in1=pm[:], op=ALU.subtract)
        P.op("act", "activation", ["r_tmp"], ["r_tmp"], out=tmp[:], in_=tmp[:], func=AF.Exp)
        P.dma("sp", (RQ[:, 2, :], tmp[:]), ["r_tmp"], [("RQ", 2)], "r_tmp")
        P.op("dve", "tensor_tensor", ["r_F", "r_mn"], ["r_dec"], out=dec[:], in0=Fc[:], in1=mnext[:], op=ALU.subtract)
        P.op("dve", "tensor_tensor", ["r_ig", "r_dec", "r_tmp"], ["r_tmp"], out=v3(tmp), in0=v3(ig), in1=bc(dec), op=ALU.add)
        P.op("act", "activation", ["r_tmp"], ["r_tmp"], out=tmp[:], in_=tmp[:], func=AF.Exp)
        P.dma("sp", (RQ[:, 3, :], tmp[:]), ["r_tmp"], [("RQ", 3)], "r_tmp")
        P.op("dve", "tensor_tensor", ["r_dec", "r_mc"], ["r_dec"], out=dec[:], in0=dec[:], in1=mc[:], op=ALU.add)
        P.op("act", "activation", ["r_dec"], ["r_dec"], out=dec[:], in_=dec[:], func=AF.Exp)
        P.dma("sp", (T["DEC"][0, :].rearrange("(h c) -> h c", h=4), dec[:]), ["r_dec"], ["DEC"], "r_dec")
        P.dma("sp", (T["GR"], ig[:]), ["r_ig"], ["GR"], "r_ig")
        P.dma("sp", (T["mlstm_m"], mnext[:, NCH - 1:NCH]), ["r_mn"], ["mlstm_m"], "r_mn")
        P.barrier()
    with ExitStack() as ph:
        cbias = sb(ph, nc, "cbias_sb", [128, 128], F32)
        nw = sb(ph, nc, "nw_mlstm", [128, 512], F32)
        decb = sb(ph, nc, "decb", [128, 4 * NCH], F32)
        cn = sb(ph, nc, "cn", [128, 4, 129], F32)
        cnb = sb(ph, nc, "cnb", [128, 4, 129], BF16)
        epsc = C["epsc"]
        NB = 2
        qT = [sb(ph, nc, "m_qT%d" % i, [128, 4, 128], BF16) for i in range(NB)]
        kT = [sb(ph, nc, "m_kT%d" % i, [128, 4, 128], BF16) for i in range(NB)]
        kt = [sb(ph, nc, "m_kt%d" % i, [128, 512], BF16) for i in range(NB)]
        va = [sb(ph, nc, "m_va%d" % i, [128, 4, 129], BF16) for i in range(NB)]
        ot = [sb(ph, nc, "m_o%d" % i, [128, 512], F32) for i in range(NB)]
        rq = [sb(ph, nc, "m_rq%d" % i, [4, 4, 128], F32) for i in range(NB)]
        gB = [sb(ph, nc, "m_gB%d" % i, [128, 4, 128], F32) for i in range(NB)]
        tq = [sb(ph, nc, "m_tq%d" % i, [128, 16], F32) for i in range(NB)]
        mixs = [sb(ph, nc, "m_mix%d" % i, [128, 4, 128], BF16) for i in range(NB)]
        e1 = Ring("m_e1", [sb(ph, nc, "m_e1%d" % i, [128, 128], F32) for i in range(2)])
        Dm = Ring("m_D", [sb(ph, nc, "m_D%d" % i, [128, 128], F32) for i in range(2)])
        wm = Ring("m_wm", [sb(ph, nc, "m_wm%d" % i, [128, 128], BF16) for i in range(2)])
        wmT = Ring("m_wmT", [sb(ph, nc, "m_wmT%d" % i, [128, 128], BF16) for i in range(2)])
        nums = Ring("m_nums", [sb(ph, nc, "m_nums%d" % i, [128, 129], F32) for i in range(2)])
        tot = Ring("m_tot", [sb(ph, nc, "m_tot%d" % i, [128, 129], F32) for i in range(2)])
        sm = Ring("m_sm", [sb(ph, nc, "m_sm%d" % i, [128, 4], F32) for i in range(2)])
        hh = Ring("m_hh", [sb(ph, nc, "m_hh%d" % i, [128, 128], F32) for i in range(2)])
        hsq = sb(ph, nc, "m_hsq", [128, 128], F32)
        hab = Ring("m_hab", [sb(ph, nc, "m_hab%d" % i, [128, 128], BF16) for i in range(2)])
        vw = Ring("m_vw", [sb(ph, nc, "m_vw%d" % i, [128, 129], BF16) for i in range(2)])
        P.dma("sp", (cbias[:], T["cbias"]), [], ["cbias"], "cbias")
        P.dma("sp", (nw[:], bcast_rows(T["norm_mlstm"])[:, 0, :]), [], ["nw"], "nw")
        P.dma("sp", (decb[:], bcast_rows(T["DEC"])[:, 0, :]), ["DEC"], ["decb"], "decb")
        P.dma("sp", [(cn[:, :, 0:128], T["mlstm_c0"].rearrange("h k v -> k h v")),
                     (cn[:, :, 128], T["mlstm_n0"].rearrange("h k -> k h"), dict(allow_slow_non_contiguous=True))],
              [], [("cn", h) for h in range(4)], "cn")
        P.op("act", "copy", [("cn", h) for h in range(4)], [("cnb", h) for h in range(4)], out=cnb[:], in_=cn[:])
        for i in range(NB):
            P.op("pool", "memset", [], [("m_va1", i)], ap=va[i][:, :, 128:129], constant=1.0)

        def load_chunk(c):
            b = c % NB
            tok = slice(c * 128, (c + 1) * 128)
            g = c // 4
            P.dma("sp", (qT[b][:], T["QA"][:, :, tok].rearrange("h p t -> p h t")), [("QA", g)], [("m_qT", b)], ("m_qT", b))
            P.dma("sp", (kT[b][:], T["KA"][:, :, tok].rearrange("h p t -> p h t")), [("KA", g)], [("m_kT", b)], ("m_kT", b))
            P.dma("sp", (kt[b][:], T["KAT"][tok, :]), [("KAT", g)], [("m_kt", b)], ("m_kt", b))
            P.dma("sp", (va[b][:, :, 0:128], T["VA"][tok, :].rearrange("s (h d) -> s h d", h=4)), [("VA", g)], [("m_va", b)], ("m_va", b))
            P.dma("sp", (ot[b][:], T["OA"][tok, :]), [("OA", g)], [("m_o", b)], ("m_o", b))
            P.dma("sp", (rq[b][:], T["RQ"][:, :, tok]), [("RQ", j) for j in range(4)], [("m_rq", b)], ("m_rq", b))
            P.dma("sp", (gB[b][:], bcast_rows(T["GR"][:, tok])), ["GR"], [("m_gB", b)], ("m_gB", b))

        load_chunk(0)
        for c in range(NCH):
            b = c % NB
            tok = slice(c * 128, (c + 1) * 128)
            if c + 1 < NCH:
                load_chunk(c + 1)
            pt, ptk = psM.next()
            for j in range(4):
                P.op("pe", "transpose", [("m_rq", b), "identf"], [ptk], out=pt[:, j * 4:(j + 1) * 4], in_=rq[b][:, j, :], identity=identf[0:4, 0:4])
            P.op("dve", "tensor_copy", [ptk], [("m_tq", b)], out=tq[b][:], in_=pt[:, 0:16])
            for h in range(4):
                npm = tq[b][:, h:h + 1]
                ain = tq[b][:, 4 + h:5 + h]
                emt = tq[b][:, 8 + h:9 + h]
                wend = tq[b][:, 12 + h:13 + h]
                pS, pSk = psF.next()
                P.op("pe", "matmul", [("m_qT", b), ("m_kT", b)], [pSk], out=pS[:, 0:128], lhsT=qT[b][:, h, :], rhs=kT[b][:, h, :], start=True, stop=True)
                e, ek = e1.next()
                P.op("pool", "tensor_tensor", [("m_gB", b), "cbias"], [ek], out=e[:], in0=gB[b][:, h, :], in1=cbias[:], op=ALU.add)
                Dt, Dk = Dm.next()
                P.op("act", "activation", [ek, ("m_tq", b)], [Dk], out=Dt[:], in_=e[:], func=AF.Exp, bias=npm, scale=1.0)
                w_, wk = wm.next()
                P.op("dve", "tensor_tensor", [pSk, Dk], [wk], out=w_[:], in0=pS[:, 0:128], in1=Dt[:], op=ALU.mult)
                pW, pWk = psT.next()
                pWb = pW[:].bitcast(BF16)
                P.op("pe", "transpose", [wk, "identb"], [pWk], out=pWb[:, 0:128], in_=w_[:], identity=identb[:])
                wT, wTk = wmT.next()
                P.op("act", "copy", [pWk], [wTk], out=wT[:], in_=pWb[:, 0:128])
                pN, pNk = psF.next()
                P.op("pe", "matmul", [wTk, ("m_va", b), ("m_va1", b)], [pNk], out=pN[:, 0:129], lhsT=wT[:], rhs=va[b][:, h, :], start=True, stop=True)
                pI, pIk = psF.next()
                P.op("pe", "matmul", [("m_qT", b), ("cnb", h)], [pIk], out=pI[:, 0:129], lhsT=qT[b][:, h, :], rhs=cnb[:, h, :], start=True, stop=True)
                nm, nmk = nums.next()
                P.op("act", "copy", [pNk], [nmk], out=nm[:], in_=pN[:, 0:129])
                tt, ttk = tot.next()
                P.op("dve", "scalar_tensor_tensor", [pIk, nmk, ("m_tq", b)], [ttk], out=tt[:], in0=pI[:, 0:129], scalar=ain, in1=nm[:],
                     op0=ALU.mult, op1=ALU.add)
                s_, sk = sm.next()
                P.op("dve", "scalar_tensor_tensor", [ttk], [sk], out=s_[:, 0:1], in0=tt[:, 128:129], scalar=-1.0, in1=tt[:, 128:129],
                     op0=ALU.mult, op1=ALU.max)
                P.op("dve", "tensor_scalar", [sk, ("m_tq", b)], [sk], out=s_[:, 0:1], in0=s_[:, 0:1], scalar1=emt, scalar2=None, op0=ALU.max)
                P.op("dve", "reciprocal", [sk], [sk], out=s_[:, 1:2], in_=s_[:, 0:1])
                hx, hk = hh.next()
                P.op("dve", "scalar_tensor_tensor", [ttk, sk, ("m_o", b)], [hk], out=hx[:], in0=tt[:, 0:128], scalar=s_[:, 1:2],
                     in1=ot[b][:, h * 128:(h + 1) * 128], op0=ALU.mult, op1=ALU.mult)
                P.op("dve", "scalar_tensor_tensor", [hk, sk], ["m_hsq", sk], out=hsq[:], in0=hx[:], scalar=1.0, in1=hx[:], op0=ALU.mult, op1=ALU.mult,
                     accum_out=s_[:, 2:3])
                P.op("act", "activation", [sk, "epsc"], [sk], out=s_[:, 3:4], in_=s_[:, 2:3], func=AF.Ln, bias=epsc[:], scale=1.0 / 128)
                P.op("act", "activation", [sk], [sk], out=s_[:, 3:4], in_=s_[:, 3:4], func=AF.Exp, scale=-0.5)
                ha, hak = hab.next()
                P.op("dve", "scalar_tensor_tensor", [hk, sk, "nw"], [hak], out=ha[:], in0=hx[:], scalar=s_[:, 3:4],
                     in1=nw[:, h * 128:(h + 1) * 128], op0=ALU.mult, op1=ALU.mult)
                pH, pHk = psT.next()
                pHb = pH[:].bitcast(BF16)
                P.op("pe", "transpose", [hak, "identb"], [pHk], out=pHb[:, 0:128], in_=ha[:], identity=identb[:])
                P.op("act", "copy", [pHk], [("m_mix", b, h)], out=mixs[b][:, h, :], in_=pHb[:, 0:128])
                v_, vk = vw.next()
                P.op("pool", "tensor_scalar", [("m_va", b), ("m_va1", b), ("m_tq", b)], [vk], out=v_[:], in0=va[b][:, h, :], scalar1=wend,
                     scalar2=0.0, op0=ALU.mult, op1=ALU.add)
                pU, pUk = psF.next()
                P.op("pe", "matmul", [("m_kt", b), vk], [pUk], out=pU[:, 0:129], lhsT=kt[b][:, h * 128:(h + 1) * 128], rhs=v_[:], start=True, stop=True)
                P.op("dve", "scalar_tensor_tensor", [pUk, ("cn", h), "decb"], [("cn", h)], out=cn[:, h, :], in0=cn[:, h, :],
                     scalar=decb[:, h * NCH + c:h * NCH + c + 1], in1=pU[:, 0:129], op0=ALU.mult, op1=ALU.add)
                P.op("act", "copy", [("cn", h)], [("cnb", h)], out=cnb[:, h, :], in_=cn[:, h, :])
            P.dma("sp", (T["MIX0"][0:512, tok].rearrange("(h p) t -> p h t", h=4), mixs[b][:]), [("m_mix", b, h) for h in range(4)],
                  [("MIX0a", c)], ("m_mix", b))
        P.dma("sp", [(T["mlstm_c"].rearrange("h k v -> k h v"), cn[:, :, 0:128]),
                     (T["mlstm_n"].rearrange("h k -> k h"), cn[:, :, 128], dict(allow_slow_non_contiguous=True))],
              [("cn", h) for h in range(4)], ["mlstm_cn"], "cn_out")
        P.barrier()


def phase_A2(nc, P, cfg, T, C):
    S = cfg.S
    NB = S // 256
    ps = C["ps"]
    psS = Ring("psS", ps[0:2])
    psSel = Ring("psSelq", ps[4:5])
    psSelB = Ring("psSelb", ps[5:6])
    accs = [(ps[2], ps[3], ("psA", 0)), (ps[6], ps[7], ("psA", 1))]
    identf = C["identf"]
    with ExitStack() as ph:
        qT = sb(ph, nc, "b_qT", [128, 4, S], BF16)
        kT = sb(ph, nc, "b_kT", [128, 2, S], BF16)
        vt = sb(ph, nc, "b_vt", [128, S // 128, 256], BF16)
        kmf = sb(ph, nc, "b_kmf", [128, 2, NB], F32)
        kmb = sb(ph, nc, "b_kmb", [128, 2, 32], BF16)
        Eall = sb(ph, nc, "b_E", [32, 32, 128], BF16)
        tri = sb(ph, nc, "b_tri", [128, 128], BF16)
        onesb = sb(ph, nc, "b_ones", [128, 128], BF16)
        sc = Ring("b_sc", [sb(ph, nc, "b_sc%d" % i, [128, 32], F32) for i in range(2)])
        mx = Ring("b_mx", [sb(ph, nc, "b_mx%d" % i, [128, 8], F32) for i in range(2)])
        bia = Ring("b_bia", [sb(ph, nc, "b_bia%d" % i, [128, 32], F32) for i in range(2)])
        biT = Ring("b_biT", [sb(ph, nc, "b_biT%d" % i, [32, 256], BF16) for i in range(2)])
        pT = Ring("b_pT", [sb(ph, nc, "b_pT%d" % i, [128, 256], BF16) for i in range(4)])
        rd = Ring("b_rd", [sb(ph, nc, "b_rd%d" % i, [128, 256], F32) for i in range(2)])
        mo = Ring("b_mo", [sb(ph, nc, "b_mo%d" % i, [128, 256], BF16) for i in range(2)])
        P.dma("sp", [(qT[:, h, :], T["QB"][h]) for h in range(4)], [], ["b_qT"], "b_qT")
        P.dma("sp", [(kT[:, j, :], T["KB"][j]) for j in range(2)], [], ["b_kT"], "b_kT")
        P.dma("sp", (vt[:], T["VB"].rearrange("(t p) d -> p t d", p=128)), [], ["b_vt"], "b_vt")
        P.dma("sp", (kmf[:], T["KM"]), [], ["b_kmf"], "b_kmf")
        P.dma("sp", (Eall[:], T["Eall"]), [], ["b_E"], "b_E")
        P.dma("sp", (tri[:], T["triT"]), [], ["b_tri"], "b_tri")
        P.op("pool", "memset", [], ["b_ones"], ap=onesb[:], constant=1.0)
        P.op("pool", "memset", [], ["b_kmb"], ap=kmb[:], constant=0.0)
        P.op("dve", "tensor_copy", ["b_kmf", "b_kmb"], ["b_kmb"], out=kmb[:, :, 0:NB], in_=kmf[:])
        it = 0
        for B in range(NB):
            qs = slice(B * 256, (B + 1) * 256)
            for h in range(4):
                j = h // 2
                accO, accD, acck = accs[it % 2]
                it += 1
                bt, btk = None, None
                if B > 0:
                    pb, pbk = psSelB.next()
                    for qt in range(2):
                        q128 = slice(B * 256 + qt * 128, B * 256 + (qt + 1) * 128)
                        pq, pqk = psSel.next()
                        P.op("pe", "matmul", ["b_qT", "b_kmb"], [pqk], out=pq[:, 0:32], lhsT=qT[:, h, q128], rhs=kmb[:, j, :], start=True, stop=True)
                        s_, sk = sc.next()
                        P.op("pool", "memset", [], [sk], ap=s_[:], constant=-3.0e4)
                        P.op("dve", "tensor_copy", [pqk, sk], [sk], out=s_[:, 0:B], in_=pq[:, 0:B])
                        m_, mk = mx.next()
                        P.op("dve", "max", [sk], [mk], out=m_[:], in_=s_[:])
                        P.op("dve", "tensor_scalar", [mk], [mk], out=m_[:, 2:3], in0=m_[:, 2:3], scalar1=-2.0e4, scalar2=None, op0=ALU.max)
                        b_, bk = bia.next()
                        P.op("dve", "tensor_scalar", [sk, mk], [bk], out=b_[:], in0=s_[:], scalar1=m_[:, 2:3], scalar2=NEG, op0=ALU.is_lt, op1=ALU.mult)
                        P.op("pe", "transpose", [bk, "identf"], [pbk], out=pb[0:32, qt * 128:(qt + 1) * 128], in_=b_[:], identity=identf[:])
                    bt, btk = biT.next()
                    P.op("act", "copy", [pbk], [btk], out=bt[:], in_=pb[0:32, 0:256])
                nkt = 2 * B + 2

                def qk(kt):
                    ks = slice(kt * 128, (kt + 1) * 128)
                    own = kt - 2 * B
                    c0 = 128 if own == 1 else 0
                    pS, pSk = psS.next()
                    P.op("pe", "matmul", ["b_kT", "b_qT"], [pSk], out=pS[:, c0:256], lhsT=kT[:, j, ks], rhs=qT[:, h, B * 256 + c0:(B + 1) * 256],
                         start=True, stop=(own >= 0))
                    if own < 0:
                        P.op("pe", "matmul", ["b_E", btk], [pSk], out=pS[:, 0:256], lhsT=Eall[:, kt // 2, :], rhs=bt[:], start=False, stop=True)
                    return pS, pSk, own, c0

                nxt = qk(0)
                for kt in range(nkt):
                    pS, pSk, own, c0 = nxt
                    if kt + 1 < nkt:
                        nxt = qk(kt + 1)
                    p_, pk = pT.next()
                    P.op("act", "activation", [pSk], [pk], out=p_[:, c0:256], in_=pS[:, c0:256], func=AF.Exp)
                    if own >= 0:
                        P.op("pool", "tensor_tensor", [pk, "b_tri"], [pk], out=p_[:, c0:c0 + 128], in0=p_[:, c0:c0 + 128], in1=tri[:], op=ALU.mult)
                    P.op("pe", "matmul", ["b_vt", pk], [acck], out=accO[:, c0:256], lhsT=vt[:, kt, j * 128:(j + 1) * 128], rhs=p_[:, c0:256],
                         start=(kt == 0), stop=(kt == nkt - 1))
                    P.op("pe", "matmul", ["b_ones", pk], [acck], out=accD[:, c0:256], lhsT=onesb[:], rhs=p_[:, c0:256],
                         start=(kt == 0), stop=(kt == nkt - 1))
                r_, rk = rd.next()
                P.op("dve", "reciprocal", [acck], [rk], out=r_[:], in_=accD[:, 0:256])
                o_, ok = mo.next()
                P.op("dve", "tensor_tensor", [acck, rk], [ok], out=o_[:], in0=accO[:, 0:256], in1=r_[:], op=ALU.mult)
                P.dma("sp", (T["MIX0"][512 + h * 128:512 + (h + 1) * 128, qs], o_[:]), [ok], [("MIX0b", B, h)], ok)
        P.barrier()


def phase_FFN(nc, P, cfg, T, C, L, KC, mixname, xin, xout, final=False, NT=None, TG=256):
    S = cfg.S if NT is None else NT
    G = (S + TG - 1) // TG
    ps = C["ps"]
    psA = Ring("psA", ps[0:4])
    psB = Ring("psB", ps[4:6])
    psC = Ring("psC", ps[6:8])
    onesf, epsc, identf = C["onesf"], C["epsc"], C["identf"]
    sfx = "%d" % L
    with ExitStack() as ph:
        wo = sb(ph, nc, "f_wo" + sfx, [128, KC, 1024], BF16)
        wup = sb(ph, nc, "f_wup" + sfx, [128, 8, 5632], BF16)
        wdn = sb(ph, nc, "f_wdn" + sfx, [128, 22, 1024], BF16)
        cw = sb(ph, nc, "f_cw" + sfx, [128, 3, 44], F32)
        cb = sb(ph, nc, "f_cb" + sfx, [128, 44], F32)
        gain = sb(ph, nc, "f_gain" + sfx, [128, 8], F32)
        gfin = sb(ph, nc, "f_gfin" + sfx, [128, 8], F32)
        TL = sb(ph, nc, "f_TL" + sfx, [128, 2, 44], F32)
        xT = [sb(ph, nc, "f_xT%s_%d" % (sfx, i), [128, 8, TG], F32) for i in range(2)]
        mx = [sb(ph, nc, "f_mix%s_%d" % (sfx, i), [128, KC, TG], BF16) for i in range(1)]
        hn = sb(ph, nc, "f_hn" + sfx, [128, 8, TG], BF16)
        act = sb(ph, nc, "f_act" + sfx, [128, 22, TG], BF16)
        rstd = sb(ph, nc, "f_rstd" + sfx, [128, TG], F32)
        sq = Ring("f_sq", [sb(ph, nc, "f_sq%s_%d" % (sfx, i), [128, TG], F32) for i in range(2)])
        U = Ring("f_U", [sb(ph, nc, "f_U%s_%d" % (sfx, i), [128, TG + 2], F32) for i in range(4)])
        Y = Ring("f_Y", [sb(ph, nc, "f_Y%s_%d" % (sfx, i), [128, TG], F32) for i in range(4)])
        Yt = Ring("f_Yt", [sb(ph, nc, "f_Yt%s_%d" % (sfx, i), [128, TG], F32) for i in range(2)])
        Sg = Ring("f_Sg", [sb(ph, nc, "f_Sg%s_%d" % (sfx, i), [128, TG], F32) for i in range(2)])
        ytok = Ring("f_ytok", [sb(ph, nc, "f_ytok%s_%d" % (sfx, i), [128, 1024 if final else 8], F32) for i in range(1)])
        tls = sb(ph, nc, "f_tls" + sfx, [88, 128], F32)
        P.dma("pool", [(wo[:, k, :], T["w_out%d" % L][k * 128:(k + 1) * 128, :]) for k in range(KC)], [], ["f_wo"], "f_wo")
        P.dma("pool", [(wup[:, k, :], T["ffn_up%d" % L][k * 128:(k + 1) * 128, :]) for k in range(8)], [], ["f_wup"], "f_wup")
        P.dma("pool", [(wdn[:, k, :], T["ffn_down%d" % L][k * 128:(k + 1) * 128, :]) for k in range(22)], [], ["f_wdn"], "f_wdn")
        P.dma("sp", [(cw[:], T["ffn_cw%d" % L]), (cb[:], T["ffn_cb%d" % L]), (gain[:], T["g_ffn%d" % L]), (gfin[:], T["g_fin"]),
                     (TL[:], T["ffn_tl%d" % L])], [], ["f_small"], "f_small")

        def load_group(g):
            b = g % 2
            tk = slice(g * TG, min(S, (g + 1) * TG))
            n = tk.stop - tk.start
            P.dma("sp", [(xT[b][:, c, 0:n], T[xin][c * 128:(c + 1) * 128, tk]) for c in range(8)], [], [("f_xT", b)], ("f_xT", b))

        def load_mix(g):
            tk = slice(g * TG, min(S, (g + 1) * TG))
            n = tk.stop - tk.start
            P.dma("sp", [(mx[0][:, k, 0:n], T[mixname][k * 128:(k + 1) * 128, tk]) for k in range(KC)], [], [("f_mix", 0)], ("f_mix", 0))

        load_group(0)
        load_mix(0)
        for g in range(G):
            b = g % 2
            tk = slice(g * TG, min(S, (g + 1) * TG))
            n = tk.stop - tk.start
            if g + 1 < G:
                load_group(g + 1)
            x_, m_ = xT[b], mx[0]
            pss, pssk = psC.next()
            for oc in range(8):
                pt, ptk = psA.next()
                for k in range(KC):
                    P.op("pe", "matmul", [("f_mix", 0), "f_wo"], [ptk], out=pt[:, 0:n], lhsT=wo[:, k, oc * 128:(oc + 1) * 128], rhs=m_[:, k, 0:n],
                         start=(k == 0), stop=(k == KC - 1))
                P.op("dve", "tensor_tensor", [ptk, ("f_xT", b)], [("f_x1", b, oc)], out=x_[:, oc, 0:n], in0=pt[:, 0:n], in1=x_[:, oc, 0:n], op=ALU.add)
                q_, qk = sq.next()
                P.op("act", "activation", [("f_x1", b, oc)], [qk], out=q_[:, 0:n], in_=x_[:, oc, 0:n], func=AF.Square)
                P.op("pe", "matmul", [qk, "onesf"], [pssk], out=pss[:, 0:n], lhsT=onesf[:], rhs=q_[:, 0:n], start=(oc == 0), stop=(oc == 7))
            if g + 1 < G:
                load_mix(g + 1)
            P.op("act", "activation", [pssk, "epsc"], ["f_rstd"], out=rstd[:, 0:n], in_=pss[:, 0:n], func=AF.Sqrt, bias=epsc[:], scale=1.0 / D)
            P.op("dve", "reciprocal", ["f_rstd"], ["f_rstd"], out=rstd[:, 0:n], in_=rstd[:, 0:n])
            for c in range(8):
                P.op("dve", "scalar_tensor_tensor", [("f_x1", b, c), "f_small", "f_rstd"], [("f_hn", c)], out=hn[:, c, 0:n], in0=x_[:, c, 0:n],
                     scalar=gain[:, c:c + 1], in1=rstd[:, 0:n], op0=ALU.mult, op1=ALU.mult)
            for gc in range(22):
                ys = []
                for half, ch in enumerate((gc, gc + 22)):
                    pt, ptk = psA.next()
                    for c in range(8):
                        P.op("pe", "matmul", [("f_hn", c), "f_wup"], [ptk], out=pt[:, 0:n], lhsT=wup[:, c, ch * 128:(ch + 1) * 128], rhs=hn[:, c, 0:n],
                             start=(c == 0), stop=(c == 7))
                    u_, uk = U.next()
                    P.op("act", "copy", [ptk], [(uk, "b")], out=u_[:, 2:2 + n], in_=pt[:, 0:n])
                    P.op("pool", "tensor_copy", [("f_TL", ch), "f_small"], [(uk, "a")], out=u_[:, 0:2], in_=TL[:, :, ch])
                    P.op("pool", "tensor_copy", [(uk, "a"), (uk, "b")], [("f_TL", ch)], out=TL[:, :, ch], in_=u_[:, n:n + 2])
                    y_, yk = Y.next()
                    ukk = [(uk, "a"), (uk, "b"), "f_small"]
                    if half == 0:
                        P.op("dve", "tensor_scalar", ukk, [yk], out=y_[:, 0:n], in0=u_[:, 2:2 + n], scalar1=cw[:, 2, ch:ch + 1], scalar2=cb[:, ch:ch + 1],
                             op0=ALU.mult, op1=ALU.add)
                        P.op("dve", "scalar_tensor_tensor", ukk + [yk], [yk], out=y_[:, 0:n], in0=u_[:, 1:1 + n], scalar=cw[:, 1, ch:ch + 1], in1=y_[:, 0:n],
                             op0=ALU.mult, op1=ALU.add)
                        P.op("dve", "scalar_tensor_tensor", ukk + [yk], [yk], out=y_[:, 0:n], in0=u_[:, 0:n], scalar=cw[:, 0, ch:ch + 1], in1=y_[:, 0:n],
                             op0=ALU.mult, op1=ALU.add)
                    else:
                        P.op("dve", "tensor_scalar", ukk, [yk], out=y_[:, 0:n], in0=u_[:, 2:2 + n], scalar1=cw[:, 2, ch:ch + 1], scalar2=cb[:, ch:ch + 1],
                             op0=ALU.mult, op1=ALU.add)
                        P.op("dve", "scalar_tensor_tensor", ukk + [yk], [yk], out=y_[:, 0:n], in0=u_[:, 1:1 + n], scalar=cw[:, 1, ch:ch + 1], in1=y_[:, 0:n],
                             op0=ALU.mult, op1=ALU.add)
                        P.op("dve", "scalar_tensor_tensor", ukk + [yk], [yk], out=y_[:, 0:n], in0=u_[:, 0:n], scalar=cw[:, 0, ch:ch + 1], in1=y_[:, 0:n],
                             op0=ALU.mult, op1=ALU.add)
                    ys.append((y_, yk))
                s_, sk = Sg.next()
                P.op("act", "activation", [ys[0][1]], [sk], out=s_[:, 0:n], in_=ys[0][0][:, 0:n], func=AF.Silu)
                P.op("dve", "tensor_tensor", [sk, ys[1][1]], [("f_act", gc)], out=act[:, gc, 0:n], in0=s_[:, 0:n], in1=ys[1][0][:, 0:n], op=ALU.mult)
            if final:
                pss2, pss2k = psC.next()
            for oc in range(8):
                pt, ptk = psB.next()
                for k in range(22):
                    P.op("pe", "matmul", [("f_act", k), "f_wdn"], [ptk], out=pt[:, 0:n], lhsT=wdn[:, k, oc * 128:(oc + 1) * 128], rhs=act[:, k, 0:n],
                         start=(k == 0), stop=(k == 21))
                P.op("dve", "tensor_tensor", [ptk, ("f_x1", b, oc)], [("f_x2", b, oc)], out=x_[:, oc, 0:n], in0=pt[:, 0:n], in1=x_[:, oc, 0:n], op=ALU.add)
                if final:
                    q_, qk = sq.next()
                    P.op("act", "activation", [("f_x2", b, oc)], [qk], out=q_[:, 0:n], in_=x_[:, oc, 0:n], func=AF.Square)
                    P.op("pe", "matmul", [qk, "onesf"], [pss2k], out=pss2[:, 0:n], lhsT=onesf[:], rhs=q_[:, 0:n], start=(oc == 0), stop=(oc == 7))
            x2k = [("f_x2", b, oc) for oc in range(8)]
            if not final:
                P.dma("sp", [(T[xout][c * 128:(c + 1) * 128, tk], x_[:, c, 0:n]) for c in range(8)], x2k, [(xout, g)], ("f_xst", b))
                P.readers.setdefault(("f_xT", b), []).extend(P.readers.get(x2k[0], []))
            else:
                P.op("act", "activation", [pss2k, "epsc"], ["f_rstd"], out=rstd[:, 0:n], in_=pss2[:, 0:n], func=AF.Sqrt, bias=epsc[:], scale=1.0 / D)
                P.op("dve", "reciprocal", ["f_rstd"], ["f_rstd"], out=rstd[:, 0:n], in_=rstd[:, 0:n])
                for c in range(8):
                    P.op("dve", "scalar_tensor_tensor", [("f_x2", b, c), "f_small", "f_rstd"], [("f_x3", b, c)], out=x_[:, c, 0:n], in0=x_[:, c, 0:n],
                         scalar=gfin[:, c:c + 1], in1=rstd[:, 0:n], op0=ALU.mult, op1=ALU.mult)
                for t in range((n + 127) // 128):
                    nt = min(128, n - t * 128)
                    yt, ytk = ytok.next()
                    for c4 in range(2):
                        pt, ptk = psA.next()
                        for cc in range(4):
                            c = c4 * 4 + cc
                            P.op("pe", "transpose", [("f_x3", b, c), "identf"], [ptk], out=pt[0:nt, cc * 128:(cc + 1) * 128],
                                 in_=x_[:, c, t * 128:t * 128 + nt], identity=identf[:])
                        P.op("act", "copy", [ptk], [(ytk, c4)], out=yt[0:nt, c4 * 512:(c4 + 1) * 512], in_=pt[0:nt, :])
                    P.dma("sp", (T["y"][g * TG + t * 128:g * TG + t * 128 + nt, :], yt[0:nt, :]), [(ytk, 0), (ytk, 1)], [("y", g, t)], ytk)
                P.readers.setdefault(("f_xT", b), []).append(P.lastw[("f_x3", b, 7)])
        pt, ptk = psA.next()
        P.op("pe", "transpose", [("f_TL", ch) for ch in range(44)] + ["identf"], [ptk], out=pt[0:88, 0:128], in_=TL[:].rearrange("p r c -> p (r c)"),
             identity=identf[:])
        P.op("act", "copy", [ptk], ["f_tls"], out=tls[:], in_=pt[0:88, 0:128])
        P.dma("sp", [(T["ffn_conv%d" % L][r, :].rearrange("(c p) -> c p", p=128), tls[r * 44:(r + 1) * 44, :]) for r in range(2)],
              ["f_tls"], ["ffn_conv_out"], "f_tls")
        P.barrier()


N_L1 = 3848
L1C = dict(z=0, xbc=512, dt=1536, qd=1544, kd=2312, vd=3080)
SWA = ((128, 1), (512, 4), (2048, 16))


def norm_group(P, C, x_, n, gain, hn, rstd, sqr, psC, xkeys, tag):
    onesf, epsc = C["onesf"], C["epsc"]
    pss, pssk = psC.next()
    for c in range(8):
        q_, qk = sqr.next()
        P.op("act", "activation", [xkeys[c]], [qk], out=q_[:, 0:n], in_=x_[:, c, 0:n], func=AF.Square)
        P.op("pe", "matmul", [qk, "onesf"], [pssk], out=pss[:, 0:n], lhsT=onesf[:], rhs=q_[:, 0:n], start=(c == 0), stop=(c == 7))
    P.op("act", "activation", [pssk, "epsc"], [tag + "rstd"], out=rstd[:, 0:n], in_=pss[:, 0:n], func=AF.Sqrt, bias=epsc[:], scale=1.0 / D)
    P.op("dve", "reciprocal", [tag + "rstd"], [tag + "rstd"], out=rstd[:, 0:n], in_=rstd[:, 0:n])
    for c in range(8):
        P.op("dve", "scalar_tensor_tensor", [xkeys[c], tag + "gain", tag + "rstd"], [(tag + "hn", c)], out=hn[:, c, 0:n], in0=x_[:, c, 0:n],
             scalar=gain[:, c:c + 1], in1=rstd[:, 0:n], op0=ALU.mult, op1=ALU.mult)


def phase_B0(nc, P, cfg, T, C):
    S, G = cfg.S, cfg.G
    ps = C["ps"]
    psF, psT, psC = Ring("psF", ps[0:4]), Ring("psT", ps[4:6]), Ring("psM", ps[6:8])
    identf = C["identf"]
    with ExitStack() as ph:
        w = sb(ph, nc, "w1", [128, 8, N_L1], BF16)
        wr = sb(ph, nc, "w1r", [128, 8, 12, 128], BF16)
        gain = sb(ph, nc, "b_gain", [128, 8], F32)
        cw = sb(ph, nc, "b_cw", [128, 4, 8], F32)
        cb = sb(ph, nc, "b_cb", [128, 8], F32)
        TL = sb(ph, nc, "b_TL", [128, 3, 8], F32)
        xT = [sb(ph, nc, "b_xT%d" % i, [128, 8, 512], F32) for i in range(2)]
        hn = sb(ph, nc, "b_hn", [128, 8, 512], BF16)
        rstd = sb(ph, nc, "b_rstd", [128, 512], F32)
        sqr = Ring("b_sq", [sb(ph, nc, "b_sq%d" % i, [128, 512], F32) for i in range(2)])
        cs = [sb(ph, nc, "b_cs%d" % i, [128, 4, 512], F32) for i in range(2)]
        U = Ring("b_U", [sb(ph, nc, "b_U%d" % i, [128, 515], F32) for i in range(2)])
        Y = Ring("b_Y", [sb(ph, nc, "b_Y%d" % i, [128, 512], F32) for i in range(2)])
        evb = Ring("b_evb", [sb(ph, nc, "b_evb%d" % i, [128, 512], BF16) for i in range(3)])
        evf = Ring("b_evf", [sb(ph, nc, "b_evf%d" % i, [128, 512], F32) for i in range(3)])
        r1 = Ring("b_r1", [sb(ph, nc, "b_r1%d" % i, [128, 512], F32) for i in range(2)])
        tkf = Ring("b_tkf", [sb(ph, nc, "b_tkf%d" % i, [128, 768], F32) for i in range(2)])
        tkb = Ring("b_tkb", [sb(ph, nc, "b_tkb%d" % i, [128, 768], BF16) for i in range(2)])
        ktk = Ring("b_ktk", [sb(ph, nc, "b_ktk%d" % i, [128, 4, 128], F32) for i in range(2)])
        tls = sb(ph, nc, "b_tls", [24, 128], F32)
        P.dma("pool", [(w[:, c, :], T["w_in_l1"][c * 128:(c + 1) * 128, :]) for c in range(8)], [], ["w1"], "w1")
        P.dma("sp", [(gain[:], T["g_mix1"]), (cw[:], T["ssd_cw"]), (cb[:], T["ssd_cb"]), (TL[:], T["ssd_tl"])], [], ["b_gain", "b_small"], "b_small")
        P.op("pool", "memset", [], ["w1r"], ap=wr[:], constant=0.0)
        wq = w[:, :, L1C["qd"]:L1C["qd"] + 1536].rearrange("p c (j e d) -> p c j e d", j=12, e=2)
        wr5 = wr[:].rearrange("p c j (e d) -> p c j e d", e=2)
        for e_ in range(2):
            P.op("dve", "tensor_scalar", ["w1", "w1r"], ["w1r"], out=wr5[:, :, :, e_, 0:8], in0=wq[:, :, :, e_, 8:16], scalar1=-1.0, scalar2=None, op0=ALU.mult)
            P.op("dve", "tensor_copy", ["w1", "w1r"], ["w1r"], out=wr5[:, :, :, e_, 8:16], in_=wq[:, :, :, e_, 0:8])

        def load_group(g):
            b = g % 2
            tk = slice(g * 512, (g + 1) * 512)
            P.dma("sp", [(xT[b][:, c, :], T["X1T"][c * 128:(c + 1) * 128, tk]) for c in range(8)], [], [("b_xT", b)], ("b_xT", b))
            P.dma("sp", [(cs[b][:, i, :], T[n][:, tk]) for i, n in enumerate(("cosD", "sinD", "cosDq", "sinDq"))], [], [("b_cs", b)], ("b_cs", b))

        load_group(0)
        for g in range(G):
            b = g % 2
            tok = slice(g * 512, (g + 1) * 512)
            if g + 1 < G:
                load_group(g + 1)
            x_, c_ = xT[b], cs[b]
            norm_group(P, C, x_, 512, gain, hn, rstd, sqr, psC, [("b_xT", b)] * 8, "b_")

            def fm_group(lhs_fn, M=128):
                pt, ptk = psF.next()
                for c in range(8):
                    P.op("pe", "matmul", [("b_hn", c), "w1", "w1r"], [ptk], out=pt[0:M, :], lhsT=lhs_fn(c), rhs=hn[:, c, :], start=(c == 0), stop=(c == 7))
                return pt, ptk

            for ch in range(8):
                col = L1C["xbc"] + ch * 128
                pt, ptk = fm_group(lambda c, col=col: w[:, c, col:col + 128])
                u_, uk = U.next()
                P.op("act", "copy", [ptk], [(uk, "b")], out=u_[:, 3:515], in_=pt[:])
                P.op("pool", "tensor_copy", [("b_TL", ch), "b_small"], [(uk, "a")], out=u_[:, 0:3], in_=TL[:, :, ch])
                P.op("pool", "tensor_copy", [(uk, "a"), (uk, "b")], [("b_TL", ch)], out=TL[:, :, ch], in_=u_[:, 512:515])
                y_, yk = Y.next()
                ukk = [(uk, "a"), (uk, "b"), "b_small"]
                P.op("dve", "tensor_scalar", ukk, [yk], out=y_[:], in0=u_[:, 3:515], scalar1=cw[:, 3, ch:ch + 1], scalar2=cb[:, ch:ch + 1], op0=ALU.mult, op1=ALU.add)
                for kk in (2, 1, 0):
                    P.op("dve", "scalar_tensor_tensor", ukk + [yk], [yk], out=y_[:], in0=u_[:, kk:kk + 512], scalar=cw[:, kk, ch:ch + 1], in1=y_[:],
                         op0=ALU.mult, op1=ALU.add)
                ev, evk = evb.next()
                P.op("act", "activation", [yk], [evk], out=ev[:], in_=y_[:], func=AF.Silu)
                P.dma("sp", (T["XBC"][ch * 128:(ch + 1) * 128, tok], ev[:]), [evk], [("XBC", g)], evk)
            pt, ptk = fm_group(lambda c: w[:, c, L1C["dt"]:L1C["dt"] + 8], M=8)
            ev, evk = evf.next()
            P.op("act", "copy", [ptk], [evk], out=ev[0:8, :], in_=pt[0:8, :])
            P.dma("sp", (T["DTR"][:, tok], ev[0:8, :]), [evk], [("DTR", g)], evk)
            for j in range(12):
                isq = j < 6
                col = L1C["qd"] + j * 128
                pt, ptk = fm_group(lambda c, col=col: w[:, c, col:col + 128])
                pr, prk = fm_group(lambda c, j=j: wr[:, c, j, :])
                ci = 2 if isq else 0
                a_, ak = r1.next()
                P.op("dve", "tensor_tensor", [ptk, ("b_cs", b)], [ak], out=a_[:], in0=pt[:], in1=c_[:, ci, :], op=ALU.mult)
                ev, evk = evf.next()
                P.op("dve", "tensor_tensor", [prk, ("b_cs", b)], [evk], out=ev[:], in0=pr[:], in1=c_[:, ci + 1, :], op=ALU.mult)
                P.op("pool", "tensor_tensor", [ak, evk], [evk], out=ev[:], in0=ev[:], in1=a_[:], op=ALU.add)
                e2, e2k = evb.next()
                P.op("act", "copy", [evk], [e2k], out=e2[:], in_=ev[:])
                P.dma("sp", (T["QD" if isq else "KD"][j % 6, :, tok], e2[:]), [e2k], [("QKD", g, j)], e2k)
                if not isq:
                    grp = (j - 6) // 2
                    win = SWA[grp][0]
                    for t in range(4):
                        t0 = g * 512 + t * 128
                        if t0 < S - min(win, S):
                            continue
                        r0 = t0 - (S - min(win, S))
                        pt2, pt2k = psT.next()
                        P.op("pe", "transpose", [evk, "identf"], [pt2k], out=pt2[:, 0:128], in_=ev[:, t * 128:(t + 1) * 128], identity=identf[:])
                        kt_, ktk_ = ktk.next()
                        P.op("act", "copy", [pt2k], [ktk_], out=kt_[:, 0, :], in_=pt2[:, 0:128])
                        hh0 = ((j - 6) % 2) * 2
                        P.dma("sp", (T["swa_kv%d" % grp][r0:r0 + 128, 0, hh0:hh0 + 2, :].rearrange("s h d -> s (h d)"), kt_[:, 0, :]),
                              [ktk_], [("swa_k", grp, g, t, j)], ktk_)
            for t in range(4):
                t0 = g * 512 + t * 128
                tsl = slice(t0, t0 + 128)
                pt, ptk = psT.next()
                for c in range(8):
                    P.op("pe", "matmul", [("b_hn", c), "w1"], [ptk], out=pt[:, 0:512], lhsT=hn[:, c, t * 128:(t + 1) * 128], rhs=w[:, c, 0:512], start=(c == 0), stop=(c == 7))
                ev, evk = evf.next()
                P.op("act", "activation", [ptk], [evk], out=ev[:], in_=pt[:], func=AF.Silu)
                P.dma("sp", (T["ZS"][tsl, :], ev[:]), [evk], [("ZS", g)], evk)
                vf, vfk = tkf.next()
                vb_, vbk = tkb.next()
                for half in range(2):
                    pt, ptk = psT.next()
                    c0 = L1C["vd"] + half * 384
                    for c in range(8):
                        P.op("pe", "matmul", [("b_hn", c), "w1"], [ptk], out=pt[:, 0:384], lhsT=hn[:, c, t * 128:(t + 1) * 128], rhs=w[:, c, c0:c0 + 384],
                             start=(c == 0), stop=(c == 7))
                    P.op("act", "copy", [ptk], [(vfk, half)], out=vf[:, half * 384:(half + 1) * 384], in_=pt[:, 0:384])
                    P.op("dve", "tensor_copy", [ptk], [(vbk, half)], out=vb_[:, half * 384:(half + 1) * 384], in_=pt[:, 0:384])
                P.dma("sp", (T["VD"][tsl, :], vb_[:]), [(vbk, 0), (vbk, 1)], [("VD", g)], vbk)
                outs = []
                for grp, (win, dil) in enumerate(SWA):
                    if t0 >= S - min(win, S):
                        r0 = t0 - (S - min(win, S))
                        outs.append((T["swa_kv%d" % grp][r0:r0 + 128, 1, :, :].rearrange("s h d -> s (h d)"), vf[:, grp * 256:(grp + 1) * 256]))
                if outs:
                    P.dma("sp", outs, [(vfk, 0), (vfk, 1)], [("swa_v", g, t)], vfk)
                else:
                    P.readers.setdefault((vfk, 0), [])
        pt, ptk = psF.next()
        P.op("pe", "transpose", [("b_TL", ch) for ch in range(8)] + ["identf"], [ptk], out=pt[0:24, 0:128], in_=TL[:].rearrange("p r c -> p (r c)"), identity=identf[:])
        P.op("act", "copy", [ptk], ["b_tls"], out=tls[:], in_=pt[0:24, 0:128])
        P.dma("sp", [(T["ssd_conv"][r, :].rearrange("(c p) -> c p", p=128), tls[r * 8:(r + 1) * 8, :]) for r in range(3)], ["b_tls"], ["ssd_conv_out"], "b_tls")
        P.barrier()


def phase_B1(nc, P, cfg, T, C):
    S = cfg.S
    NCH = S // 128
    ps = C["ps"]
    psF, psT, psM = Ring("psF", ps[0:4]), Ring("psT", ps[4:6]), Ring("psM", ps[6:8])
    identf, identb, epsc = C["identf"], C["identb"], C["epsc"]
    with ExitStack() as ph:
        dt = sb(ph, nc, "s_dt", [8, S], F32)
        acum = sb(ph, nc, "s_acum", [8, S], F32)
        tmp = sb(ph, nc, "s_tmp", [8, S], F32)
        onesr = sb(ph, nc, "s_ones", [8, 128], F32)
        sm = sb(ph, nc, "s_sm", [8, 4], F32)
        aend = sb(ph, nc, "s_aend", [8, NCH], F32)
        sdec = sb(ph, nc, "s_sdec", [8, NCH], F32)
        P.dma("sp", (dt[:], T["DTR"]), [], ["s_dt"], "s_dt")
        P.dma("sp", [(sm[:, 0:1], T["dt_bias"]), (sm[:, 1:2], T["a_log"])], [], ["s_sm"], "s_sm")
        P.op("pool", "memset", [], ["s_ones"], ap=onesr[:], constant=1.0)
        P.op("pool", "memset", ["s_sm"], ["s_sm"], ap=sm[:, 3:4], constant=1.0)
        P.op("act", "activation", ["s_sm"], ["s_sm"], out=sm[:, 2:3], in_=sm[:, 1:2], func=AF.Exp)
        P.op("dve", "tensor_scalar", ["s_sm"], ["s_sm"], out=sm[:, 2:3], in0=sm[:, 2:3], scalar1=-1.0, scalar2=None, op0=ALU.mult)
        P.op("act", "activation", ["s_dt", "s_sm"], ["s_dt"], out=dt[:], in_=dt[:], func=AF.Exp, bias=sm[:, 0:1], scale=1.0)
        P.op("act", "activation", ["s_dt", "s_sm"], ["s_dt"], out=dt[:], in_=dt[:], func=AF.Ln, bias=sm[:, 3:4], scale=1.0)
        P.op("dve", "tensor_scalar", ["s_dt", "s_sm"], ["s_tmp"], out=tmp[:], in0=dt[:], scalar1=sm[:, 2:3], scalar2=None, op0=ALU.mult)
        for c in range(NCH):
            sl = slice(c * 128, (c + 1) * 128)
            P.op("dve", "tensor_tensor_scan", ["s_tmp", "s_ones"], [("s_acum", c)], out=acum[:, sl], data0=onesr[:], data1=tmp[:, sl], initial=0.0,
                 op0=ALU.mult, op1=ALU.add)
        allc = [("s_acum", c) for c in range(NCH)]
        v3 = lambda t_: t_[:].rearrange("p (c t) -> p c t", t=128)
        bc = lambda t_: t_[:].unsqueeze(2).broadcast_to([8, NCH, 128])
        SQ = T["SQ"]
        P.dma("sp", [(SQ[:, 0, :], acum[:]), (SQ[:, 2, :], dt[:])], allc + ["s_dt"], [("SQ", 0)], "s_q0")
        P.op("act", "activation", allc + ["s_tmp"], ["s_tmp"], out=tmp[:], in_=acum[:], func=AF.Exp)
        P.dma("sp", (SQ[:, 1, :], tmp[:]), ["s_tmp"], [("SQ", 1)], "s_tmp")
        P.op("dve", "tensor_copy", allc, ["s_aend"], out=aend[:], in_=v3(acum)[:, :, 127])
        P.op("dve", "tensor_tensor", allc + ["s_aend", "s_tmp"], ["s_tmp"], out=v3(tmp), in0=bc(aend), in1=v3(acum), op=ALU.subtract)
        P.op("act", "activation", ["s_tmp"], ["s_tmp"], out=tmp[:], in_=tmp[:], func=AF.Exp)
        P.op("dve", "tensor_tensor", ["s_tmp", "s_dt"], ["s_tmp"], out=tmp[:], in0=tmp[:], in1=dt[:], op=ALU.mult)
        P.dma("sp", (SQ[:, 3, :], tmp[:]), ["s_tmp"], [("SQ", 3)], "s_tmp")
        P.op("dve", "tensor_scalar", allc + ["s_tmp"], ["s_tmp"], out=tmp[:], in0=acum[:], scalar1=-1.0, scalar2=None, op0=ALU.mult)
        P.dma("sp", (T["SR"], tmp[:]), ["s_tmp"], ["SR"], "s_tmp")
        P.op("act", "activation", ["s_aend"], ["s_sdec"], out=sdec[:], in_=aend[:], func=AF.Exp)
        P.dma("sp", (T["SDEC"][0, :].rearrange("(h c) -> h c", h=8), sdec[:]), ["s_sdec"], ["SDEC"], "s_sdec")
        P.barrier()
    with ExitStack() as ph:
        cbias = sb(ph, nc, "s_cbias", [128, 128], F32)
        nw = sb(ph, nc, "s_nw", [128, 512], F32)
        dsk = sb(ph, nc, "s_dsk", [128, 8], F32)
        decb = sb(ph, nc, "s_decb", [128, 8 * NCH], F32)
        hT = sb(ph, nc, "s_hT", [128, 8, 64], F32)
        hTb = sb(ph, nc, "s_hTb", [128, 8, 64], BF16)
        NB = 2
        xb = [sb(ph, nc, "s_xb%d" % i, [128, 8, 128], BF16) for i in range(NB)]
        zs = [sb(ph, nc, "s_zs%d" % i, [128, 512], F32) for i in range(NB)]
        sq = [sb(ph, nc, "s_sq%d" % i, [8, 4, 128], F32) for i in range(NB)]
        nB = [sb(ph, nc, "s_nB%d" % i, [128, 8, 128], F32) for i in range(NB)]
        tq = [sb(ph, nc, "s_tq%d" % i, [128, 32], F32) for i in range(NB)]
        xt = [sb(ph, nc, "s_xt%d" % i, [128, 6, 128], BF16) for i in range(NB)]
        CBs = [sb(ph, nc, "s_CB%d" % i, [128, 2, 128], F32) for i in range(NB)]
        YS = [sb(ph, nc, "s_YS%d" % i, [128, 512], F32) for i in range(NB)]
        ycb = [sb(ph, nc, "s_ycb%d" % i, [128, 512], BF16) for i in range(NB)]
        mixs = [sb(ph, nc, "s_mix%d" % i, [128, 4, 128], BF16) for i in range(NB)]
        e1 = Ring("s_e1", [sb(ph, nc, "s_e1%d" % i, [128, 128], F32) for i in range(2)])
        Dm = Ring("s_D", [sb(ph, nc, "s_D%d" % i, [128, 128], F32) for i in range(2)])
        mm = Ring("s_mm", [sb(ph, nc, "s_mm%d" % i, [128, 128], BF16) for i in range(2)])
        mmT = Ring("s_mmT", [sb(ph, nc, "s_mmT%d" % i, [128, 128], BF16) for i in range(2)])
        xd = Ring("s_xd", [sb(ph, nc, "s_xd%d" % i, [128, 64], BF16) for i in range(2)])
        xw = Ring("s_xw", [sb(ph, nc, "s_xw%d" % i, [128, 64], BF16) for i in range(2)])
        ysb = Ring("s_ysb", [sb(ph, nc, "s_ysb%d" % i, [128, 64], F32) for i in range(2)])
        st = Ring("s_st", [sb(ph, nc, "s_st%d" % i, [128, 4], F32) for i in range(2)])
        junk = sb(ph, nc, "s_junk", [128, 256], F32)
        hout = Ring("s_hout", [sb(ph, nc, "s_hout%d" % i, [64, 128], F32) for i in range(2)])
        P.dma("sp", [(cbias[:], T["cbias"]), (nw[:], bcast_rows(T["norm_ssd"])[:, 0, :]), (dsk[:], bcast_rows(T["d_skip"])[:, 0, :]),
                     (decb[:], bcast_rows(T["SDEC"])[:, 0, :])], ["SDEC"], ["s_const"], "s_const")
        P.dma("sp", (hT[:], T["ssd_h0"].rearrange("h p n -> n h p"), dict(allow_slow_non_contiguous=True)), [], [("s_hT", h) for h in range(8)], "s_hT")
        P.op("act", "copy", [("s_hT", h) for h in range(8)], [("s_hTb", h) for h in range(8)], out=hTb[:], in_=hT[:])

        def load_chunk(c):
            b = c % NB
            tok = slice(c * 128, (c + 1) * 128)
            P.dma("sp", (xb[b][:], T["XBC"][:, tok].rearrange("(c p) t -> p c t", p=128)), [], [("s_xb", b)], ("s_xb", b))
            P.dma("sp", (zs[b][:], T["ZS"][tok, :]), [], [("s_zs", b)], ("s_zs", b))
            P.dma("sp", (sq[b][:], T["SQ"][:, :, tok]), [], [("s_sq", b)], ("s_sq", b))
            P.dma("sp", (nB[b][:], bcast_rows(T["SR"][:, tok])), [], [("s_nB", b)], ("s_nB", b))

        load_chunk(0)
        for c in range(NCH):
            b = c % NB
            tok = slice(c * 128, (c + 1) * 128)
            if c + 1 < NCH:
                load_chunk(c + 1)
            pt, ptk = psM.next()
            for j in range(4):
                P.op("pe", "transpose", [("s_sq", b), "identf"], [ptk], out=pt[:, j * 8:(j + 1) * 8], in_=sq[b][:, j, :], identity=identf[0:8, 0:8])
            P.op("dve", "tensor_copy", [ptk], [("s_tq", b)], out=tq[b][:], in_=pt[:, 0:32])
            for j in range(6):
                pX, pXk = psT.next()
                pXb = pX[:].bitcast(BF16)
                P.op("pe", "transpose", [("s_xb", b), "identb"], [pXk], out=pXb[:, 0:128], in_=xb[b][:, j, :], identity=identb[:])
                P.op("act", "copy", [pXk], [("s_xt", b, j)], out=xt[b][:, j, :], in_=pXb[:, 0:128])
            for g2 in range(2):
                pC, pCk = psF.next()
                P.op("pe", "matmul", [("s_xb", b)], [pCk], out=pC[:, 0:128], lhsT=xb[b][:, 6 + g2, :], rhs=xb[b][:, 4 + g2, :], start=True, stop=True)
                P.op("act", "copy", [pCk], [("s_CB", b, g2)], out=CBs[b][:, g2, :], in_=pC[:, 0:128])
            for h in range(8):
                g2 = h // 4
                ac_t = tq[b][:, h:h + 1]
                eac = tq[b][:, 8 + h:9 + h]
                dtc = tq[b][:, 16 + h:17 + h]
                wend = tq[b][:, 24 + h:25 + h]
                xh = xt[b][:, h // 2, (h % 2) * 64:(h % 2) * 64 + 64]
                xkey = ("s_xt", b, h // 2)
                e, ek = e1.next()
                P.op("pool", "tensor_tensor", [("s_nB", b), "s_const"], [ek], out=e[:], in0=nB[b][:, h, :], in1=cbias[:], op=ALU.add)
                Dt, Dk = Dm.next()
                P.op("act", "activation", [ek, ("s_tq", b)], [Dk], out=Dt[:], in_=e[:], func=AF.Exp, bias=ac_t, scale=1.0)
                m_, mk = mm.next()
                P.op("dve", "tensor_tensor", [("s_CB", b, g2), Dk], [mk], out=m_[:], in0=CBs[b][:, g2, :], in1=Dt[:], op=ALU.mult)
                pW, pWk = psT.next()
                pWb = pW[:].bitcast(BF16)
                P.op("pe", "transpose", [mk, "identb"], [pWk], out=pWb[:, 0:128], in_=m_[:], identity=identb[:])
                mT, mTk = mmT.next()
                P.op("act", "copy", [pWk], [mTk], out=mT[:], in_=pWb[:, 0:128])
                x1, x1k = xd.next()
                P.op("pool", "tensor_scalar", [xkey, ("s_tq", b)], [x1k], out=x1[:], in0=xh, scalar1=dtc, scalar2=0.0, op0=ALU.mult, op1=ALU.add)
                pY, pYk = psF.next()
                P.op("pe", "matmul", [mTk, x1k], [pYk], out=pY[:, 0:64], lhsT=mT[:], rhs=x1[:], start=True, stop=True)
                pI, pIk = psF.next()
                P.op("pe", "matmul", [("s_xb", b), ("s_hTb", h)], [pIk], out=pI[:, 0:64], lhsT=xb[b][:, 6 + g2, :], rhs=hTb[:, h, :], start=True, stop=True)
                y1, y1k = ysb.next()
                P.op("act", "copy", [pYk], [y1k], out=y1[:], in_=pY[:, 0:64])
                P.op("dve", "scalar_tensor_tensor", [pIk, y1k, ("s_tq", b)], [y1k], out=y1[:], in0=pI[:, 0:64], scalar=eac, in1=y1[:], op0=ALU.mult, op1=ALU.add)
                P.op("dve", "scalar_tensor_tensor", [xkey, y1k, "s_const"], [("s_YS", b, h)], out=YS[b][:, h * 64:(h + 1) * 64], in0=xh, scalar=dsk[:, h:h + 1],
                     in1=y1[:], op0=ALU.mult, op1=ALU.add)
                x2, x2k = xw.next()
                P.op("pool", "tensor_scalar", [xkey, ("s_tq", b)], [x2k], out=x2[:], in0=xh, scalar1=wend, scalar2=0.0, op0=ALU.mult, op1=ALU.add)
                pU, pUk = psF.next()
                P.op("pe", "matmul", [("s_xt", b, 4 + g2), x2k], [pUk], out=pU[:, 0:64], lhsT=xt[b][:, 4 + g2, :], rhs=x2[:], start=True, stop=True)
                P.op("dve", "scalar_tensor_tensor", [pUk, ("s_hT", h), "s_const"], [("s_hT", h)], out=hT[:, h, :], in0=hT[:, h, :],
                     scalar=decb[:, h * NCH + c:h * NCH + c + 1], in1=pU[:, 0:64], op0=ALU.mult, op1=ALU.add)
                P.op("act", "copy", [("s_hT", h)], [("s_hTb", h)], out=hTb[:, h, :], in_=hT[:, h, :])
            ysk = [("s_YS", b, h) for h in range(8)]
            P.op("dve", "tensor_tensor", ysk + [("s_zs", b)], [("s_YS2", b)], out=YS[b][:], in0=YS[b][:], in1=zs[b][:], op=ALU.mult)
            s_, sk = st.next()
            for g2 in range(2):
                P.op("act", "activation", [("s_YS2", b), sk], ["s_junk", sk], out=junk[:], in_=YS[b][:, g2 * 256:(g2 + 1) * 256], func=AF.Square,
                     accum_out=s_[:, g2:g2 + 1])
            P.op("act", "activation", [sk, "epsc"], [sk], out=s_[:, 2:4], in_=s_[:, 0:2], func=AF.Sqrt, bias=epsc[:], scale=1.0 / 256)
            P.op("dve", "reciprocal", [sk], [sk], out=s_[:, 2:4], in_=s_[:, 2:4])
            for g2 in range(2):
                P.op("dve", "scalar_tensor_tensor", [("s_YS2", b), sk, "s_const"], [("s_ycb", b, g2)], out=ycb[b][:, g2 * 256:(g2 + 1) * 256],
                     in0=YS[b][:, g2 * 256:(g2 + 1) * 256], scalar=s_[:, 2 + g2:3 + g2], in1=nw[:, g2 * 256:(g2 + 1) * 256], op0=ALU.mult, op1=ALU.mult)
            for j in range(4):
                pH, pHk = psT.next()
                pHb = pH[:].bitcast(BF16)
                P.op("pe", "transpose", [("s_ycb", b, j // 2), "identb"], [pHk], out=pHb[:, 0:128], in_=ycb[b][:, j * 128:(j + 1) * 128], identity=identb[:])
                P.op("act", "copy", [pHk], [("s_mix", b, j)], out=mixs[b][:, j, :], in_=pHb[:, 0:128])
            P.dma("sp", (T["MIX1"][0:512, tok].rearrange("(h p) t -> p h t", h=4), mixs[b][:]), [("s_mix", b, j) for j in range(4)], [("MIX1a", c)], ("s_mix", b))
        for h in range(8):
            pt, ptk = psF.next()
            P.op("pe", "transpose", [("s_hT", h), "identf"], [ptk], out=pt[0:64, 0:128], in_=hT[:, h, :], identity=identf[:])
            ho, hok = hout.next()
            P.op("act", "copy", [ptk], [hok], out=ho[:], in_=pt[0:64, 0:128])
            P.dma("sp", (T["ssd_h"][h], ho[:]), [hok], [("ssd_h", h)], hok)
        P.barrier()


def phase_B2(nc, P, cfg, T, C):
    S = cfg.S
    ps = C["ps"]
    psS = Ring("psS", ps[0:4])
    psN = Ring("psN", ps[4:6])
    psD = Ring("psD", ps[6:8])
    with ExitStack() as ph:
        accN = sb(ph, nc, "d_accN", [128, S], F32)
        accD = sb(ph, nc, "d_accD", [128, S], F32)
        qT = sb(ph, nc, "d_qT", [128, S], BF16)
        kT = sb(ph, nc, "d_kT", [128, S], BF16)
        vt = sb(ph, nc, "d_vt", [128, S // 128 if S >= 2048 else 16, 128], BF16)
        tri = sb(ph, nc, "d_tri", [128, 2, 128], BF16)
        onesp = sb(ph, nc, "d_onesp", [128, 2, 128], BF16)
        vp = Ring("d_vp", [sb(ph, nc, "d_vp%d" % i, [128, 2, 128], BF16) for i in range(4)])
        pT = Ring("d_pT", [sb(ph, nc, "d_pT%d" % i, [128, 128], BF16) for i in range(6)])
        ob = Ring("d_ob", [sb(ph, nc, "d_ob%d" % i, [128, 512], BF16) for i in range(2)])
        P.dma("sp", [(tri[:, 0, :], T["triT"]), (tri[:, 1, :], T["triR"])], [], ["d_tri"], "d_tri")
        P.op("pool", "memset", [], ["d_onesp"], ap=onesp[:], constant=0.0)
        P.op("pool", "memset", ["d_onesp"], ["d_onesp"], ap=onesp[:, 0, 0:64], constant=1.0)
        P.op("pool", "memset", ["d_onesp"], ["d_onesp"], ap=onesp[:, 1, 64:128], constant=1.0)
        for i in range(4):
            P.op("pool", "memset", [], [("d_vp", i)], ap=vp.tiles[i][:], constant=0.0)
        for ip in range(2):
            first = True
            for g, (win, dil) in enumerate(SWA):
                j = 2 * g + ip
                ncls = S // dil
                n = min(128, ncls)
                ntile = ncls // n
                P.dma("sp", (qT[:], T["QD"][j]), [], ["d_qT"], "d_qT")
                P.dma("sp", (kT[:], T["KD"][j]), [], ["d_kT"], "d_kT")
                c0 = (g * 4 + 2 * ip) * 64
                vsrc = T["VD"][:, c0:c0 + 128].rearrange("(u p r) d -> r p u d", r=dil, p=n)
                P.dma("sp", [(vt[0:n, r * ntile:(r + 1) * ntile, :], vsrc[r]) for r in range(dil)], [], ["d_vt"], "d_vt")
                qv = qT[:].rearrange("p (u a r) -> p r u a", r=dil, a=n)
                kv = kT[:].rearrange("p (u a r) -> p r u a", r=dil, a=n)
                aN = accN[:].rearrange("p (u a r) -> p r u a", r=dil, a=n)
                aD = accD[:].rearrange("p (u a r) -> p r u a", r=dil, a=n)
                for r in range(dil):
                    for u in range(ntile):
                        pN, pNk = psN.next()
                        pD, pDk = psD.next()
                        kts = [u - 1, u] if u > 0 else [u]
                        nmm = 2 * len(kts)
                        imm = 0
                        vtiles = {}
                        for ku in kts:
                            v_, vk = vp.next()
                            vsrc_t = vt[0:n, r * ntile + ku, :]
                            P.op("pool", "tensor_copy", ["d_vt", vk], [vk], out=v_[0:n, 0, 0:64], in_=vsrc_t[:, 0:64])
                            P.op("pool", "tensor_copy", ["d_vt", vk], [vk], out=v_[0:n, 1, 64:128], in_=vsrc_t[:, 64:128])
                            vtiles[ku] = (v_, vk)
                        items = [(ku, e) for ku in kts for e in range(2)]

                        def qk(ku, e):
                            pS, pSk = psS.next()
                            P.op("pe", "matmul", ["d_kT", "d_qT"], [pSk], out=pS[0:n, 0:n], lhsT=kv[e * 64:(e + 1) * 64, r, ku, :],
                                 rhs=qv[e * 64:(e + 1) * 64, r, u, :], start=True, stop=True)
                            return pS, pSk

                        nxt = qk(*items[0])
                        for imm, (ku, e) in enumerate(items):
                            pS, pSk = nxt
                            if imm + 1 < nmm:
                                nxt = qk(*items[imm + 1])
                            v_, vk = vtiles[ku]
                            p_, pk = pT.next()
                            P.op("act", "activation", [pSk], [pk], out=p_[0:n, 0:n], in_=pS[0:n, 0:n], func=AF.Exp)
                            P.op("dve", "tensor_tensor", [pk, "d_tri"], [pk], out=p_[0:n, 0:n], in0=p_[0:n, 0:n],
                                 in1=tri[0:n, 0 if ku == u else 1, 0:n], op=ALU.mult)
                            P.op("pe", "matmul", [vk, pk], [pNk], out=pN[:, 0:n], lhsT=v_[0:n, e, :], rhs=p_[0:n, 0:n], start=(imm == 0), stop=(imm == nmm - 1))
                            P.op("pe", "matmul", ["d_onesp", pk], [pDk], out=pD[:, 0:n], lhsT=onesp[0:n, e, :], rhs=p_[0:n, 0:n], start=(imm == 0),
                                 stop=(imm == nmm - 1))
                        if first:
                            P.op("act", "copy", [pNk], [("d_acc", r, u, g)], out=aN[:, r, u, :], in_=pN[:, 0:n])
                            P.op("dve", "tensor_copy", [pDk], [("d_accD", r, u, g)], out=aD[:, r, u, :], in_=pD[:, 0:n])
                        else:
                            P.op("dve", "tensor_tensor", [pNk, "d_accall"], [("d_acc", r, u, g)], out=aN[:, r, u, :], in0=pN[:, 0:n], in1=aN[:, r, u, :], op=ALU.add)
                            P.op("dve", "tensor_tensor", [pDk, "d_accall"], [("d_accD", r, u, g)], out=aD[:, r, u, :], in0=pD[:, 0:n], in1=aD[:, r, u, :], op=ALU.add)
                keys = [("d_acc", r, u, g) for r in range(dil) for u in range(ntile)] + [("d_accD", r, u, g) for r in range(dil) for u in range(ntile)]
                P.op("pool", "memset", keys + ["d_accall"], ["d_accall"], ap=onesp[0:1, 0, 64:65], constant=0.0)
                first = False
            for t0 in range(0, S, 512):
                P.op("dve", "reciprocal", ["d_accall"], [("d_rc", t0)], out=accD[:, t0:t0 + 512], in_=accD[:, t0:t0 + 512])
                o_, ok = ob.next()
                P.op("dve", "tensor_tensor", [("d_rc", t0), "d_accall"], [ok], out=o_[:], in0=accN[:, t0:t0 + 512], in1=accD[:, t0:t0 + 512], op=ALU.mult)
                P.dma("sp", (T["MIX1"][512 + ip * 128:512 + (ip + 1) * 128, t0:t0 + 512], o_[:]), [ok], [("MIX1b", ip, t0)], ok)
            P.op("pool", "memset", [("d_rc", t0) for t0 in range(0, S, 512)] + ["d_accall"], ["d_accall"], ap=onesp[0:1, 0, 64:65], constant=0.0)
        P.barrier()


def bh_in(dst, src_rows, H):
    return [(dst[b * H:(b + 1) * H, :], src_rows[b].rearrange("(h d) -> h d", h=H)) for b in range(src_rows.shape[0])]


def bh_out(dst_rows, src, H):
    return [(dst_rows[b].rearrange("(h d) -> h d", h=H), src[b * H:(b + 1) * H, :]) for b in range(dst_rows.shape[0])]


def sample_mlstm(nc, P, cfg, T, C, NS):
    epsc = C["epsc"]
    NP = NS * 4
    with ExitStack() as ph:
        c0 = sb(ph, nc, "zm_c0", [NP, 128, 128], F32)
        prod = sb(ph, nc, "zm_prod", [NP, 128, 128], F32)
        v6 = sb(ph, nc, "zm_v6", [NP, 8, 128], F32)
        sc = sb(ph, nc, "zm_sc", [NP, 24], F32)
        bg = sb(ph, nc, "zm_bg", [NP, 2], F32)
        nw = sb(ph, nc, "zm_nw", [NP, 128], F32)
        q, k, v, o, n0, qc, num, tmp = [v6[:, j, :] for j in range(8)]
        Z = T["ZP0"]
        hv = lambda d_, c0_, w_=512: bh_in(d_, Z[:, c0_:c0_ + w_], 4)
        P.dma("sp", hv(q, 0) + hv(k, 512) + hv(v, 1024) + hv(o, 1536) + hv(sc[:, 0:1], 2048, 4) + hv(sc[:, 1:2], 2052, 4) +
                    [(n0, T["mlstm_n0_s"].rearrange("b h d -> (b h) d")),
                     (sc[:, 2:3], T["mlstm_m0_s"]), (bg[:], T["bg16"]), (nw[:], T["nw16"]),
                     (c0[:], T["mlstm_c0_s"].rearrange("b h k v -> (b h) k v"))], ["ZP0"], ["zm_in"], "zm_in")
        K_ = ["zm_in", "zm_w"]
        col = lambda j: sc[:, j:j + 1]
        def op(eng, name, **kw):
            P.op(eng, name, K_, ["zm_w"], **kw)
        op("dve", "tensor_tensor", out=col(3), in0=col(0), in1=bg[:, 0:1], op=ALU.add)
        op("dve", "tensor_tensor", out=col(4), in0=col(1), in1=bg[:, 1:2], op=ALU.add)
        op("act", "activation", out=col(4), in_=col(4), func=AF.Exp, scale=-1.0)
        op("dve", "tensor_scalar", out=col(4), in0=col(4), scalar1=1.0, scalar2=None, op0=ALU.add)
        op("act", "activation", out=col(4), in_=col(4), func=AF.Ln)
        op("dve", "tensor_tensor", out=col(5), in0=col(2), in1=col(4), op=ALU.subtract)
        op("dve", "tensor_tensor", out=col(6), in0=col(5), in1=col(3), op=ALU.max)
        op("dve", "tensor_tensor", out=col(7), in0=col(5), in1=col(6), op=ALU.subtract)
        op("dve", "tensor_tensor", out=col(8), in0=col(3), in1=col(6), op=ALU.subtract)
        op("act", "activation", out=sc[:, 7:9], in_=sc[:, 7:9], func=AF.Exp)
        op("act", "activation", out=col(9), in_=col(6), func=AF.Exp, scale=-1.0)
        op("dve", "tensor_scalar", out=k, in0=k, scalar1=128.0 ** -0.5, scalar2=None, op0=ALU.mult)
        op("dve", "scalar_tensor_tensor", out=tmp, in0=q, scalar=1.0, in1=k, op0=ALU.mult, op1=ALU.mult, accum_out=col(10))
        op("dve", "scalar_tensor_tensor", out=tmp, in0=q, scalar=1.0, in1=n0, op0=ALU.mult, op1=ALU.mult, accum_out=col(11))
        op("dve", "tensor_tensor", out=prod[:], in0=c0[:], in1=q.unsqueeze(2).broadcast_to([NP, 128, 128]), op=ALU.mult)
        op("dve", "tensor_reduce", out=qc, in_=prod[:].rearrange("p k v -> p v k"), axis=AX.X, op=ALU.add)
        op("dve", "tensor_tensor", out=col(12), in0=col(10), in1=col(8), op=ALU.mult)
        op("dve", "tensor_scalar", out=num, in0=v, scalar1=col(12), scalar2=None, op0=ALU.mult)
        op("dve", "scalar_tensor_tensor", out=num, in0=qc, scalar=col(7), in1=num, op0=ALU.mult, op1=ALU.add)
        op("dve", "scalar_tensor_tensor", out=col(13), in0=col(11), scalar=col(7), in1=col(12), op0=ALU.mult, op1=ALU.add)
        op("dve", "scalar_tensor_tensor", out=col(14), in0=col(13), scalar=-1.0, in1=col(13), op0=ALU.mult, op1=ALU.max)
        op("dve", "tensor_tensor", out=col(14), in0=col(14), in1=col(9), op=ALU.max)
        op("dve", "reciprocal", out=col(14), in_=col(14))
        op("act", "activation", out=o, in_=o, func=AF.Sigmoid)
        op("dve", "scalar_tensor_tensor", out=num, in0=num, scalar=col(14), in1=o, op0=ALU.mult, op1=ALU.mult)
        op("act", "activation", out=tmp, in_=num, func=AF.Square, accum_out=col(15))
        P.op("act", "activation", K_ + ["epsc"], ["zm_w"], out=col(16), in_=col(15), func=AF.Sqrt, bias=epsc[0:NP, :], scale=1.0 / 128)
        op("dve", "reciprocal", out=col(16), in_=col(16))
        op("dve", "scalar_tensor_tensor", out=num, in0=num, scalar=col(16), in1=nw[:], op0=ALU.mult, op1=ALU.mult)
        op("dve", "tensor_tensor", out=prod[:], in0=k.unsqueeze(2).broadcast_to([NP, 128, 128]), in1=v.unsqueeze(1).broadcast_to([NP, 128, 128]), op=ALU.mult)
        op("dve", "tensor_scalar", out=c0[:], in0=c0[:], scalar1=col(7), scalar2=None, op0=ALU.mult)
        op("dve", "scalar_tensor_tensor", out=c0[:], in0=prod[:], scalar=col(8), in1=c0[:], op0=ALU.mult, op1=ALU.add)
        op("dve", "tensor_scalar", out=n0, in0=n0, scalar1=col(7), scalar2=None, op0=ALU.mult)
        op("dve", "scalar_tensor_tensor", out=n0, in0=k, scalar=col(8), in1=n0, op0=ALU.mult, op1=ALU.add)
        P.dma("sp", [(T["mlstm_c_s"].rearrange("b h k v -> (b h) k v"), c0[:]), (T["mlstm_n_s"].rearrange("b h d -> (b h) d"), n0),
                     (T["mlstm_m_s"], col(6))] + bh_out(T["MIXS0"][:, 0:512], num, 4), K_, ["MIXS0a"], "zm_out")
        P.barrier()


def sample_moba(nc, P, cfg, T, C, NS):
    NPG = cfg.NPG
    NBK = NPG // 2
    W = max(NBK, 8)
    ps = C["ps"]
    psN, psSc = Ring("psN", ps[0:2]), Ring("psSc", ps[2:3])
    with ExitStack() as ph:
        ptab = sb(ph, nc, "zb_ptab", [128, NS, NPG], I32)
        idx = sb(ph, nc, "zb_idx", [128, NS, NPG], I32)
        iot = sb(ph, nc, "zb_iota", [128, 1], F32)
        qbc = sb(ph, nc, "zb_qbc", [128, NS, 512], F32)
        onec = sb(ph, nc, "zb_onec", [128, 1], F32)
        Kt = Ring("zb_K", [sb(ph, nc, "zb_K%d" % i, [128, 256], F32) for i in range(4)])
        Vt = Ring("zb_V", [sb(ph, nc, "zb_V%d" % i, [128, 257], F32) for i in range(4)])
        prod = Ring("zb_prod", [sb(ph, nc, "zb_prod%d" % i, [128, 4, 128], F32) for i in range(2)])
        Sk = Ring("zb_Sk", [sb(ph, nc, "zb_Sk%d" % i, [128, 8], F32) for i in range(4)])
        NB = sb(ph, nc, "zb_NB", [4, NBK, 257], F32)
        sc = sb(ph, nc, "zb_sc", [4, W], F32)
        mx = sb(ph, nc, "zb_mx", [4, 8], F32)
        sel = sb(ph, nc, "zb_sel", [4, NBK], F32)
        O = sb(ph, nc, "zb_O", [4, 257], F32)
        P.dma("sp", [(ptab[:], bcast_rows(T["page_table_s"])), (iot[:], T["iota_p"]), (qbc[:], bcast_rows(T["ZQB"]))], ["ZP0"], ["zb_in"], "zb_in")
        P.op("pool", "memset", [], ["zb_onec"], ap=onec[:], constant=1.0)
        for i in range(4):
            P.op("pool", "memset", [], [("zb_V1", i)], ap=Vt.tiles[i][:, 256:257], constant=1.0)
        P.op("dve", "tensor_scalar", ["zb_in"], ["zb_idx"], out=idx[:], in0=ptab[:], scalar1=128.0, scalar2=iot[:, 0:1], op0=ALU.mult, op1=ALU.add)
        for b in range(NS):
            P.op("pool", "memset", [("zb_scw", b - 1)], ["zb_sc"], ap=sc[:], constant=-3.0e4)
            pSc, pSck = psSc.next()
            for pg in range(NPG):
                blk = pg // 2
                kt, ktk = Kt.next()
                vt, vtk = Vt.next()
                off = bass.IndirectOffsetOnAxis(ap=idx[:, b, pg:pg + 1], axis=0)
                P.dma("pool", (kt[:], T["cache_k"], dict(_op="indirect_dma_start", out_offset=None, in_offset=off)), ["zb_idx"], [ktk], ktk)
                P.dma("pool", (vt[:, 0:256], T["cache_v"], dict(_op="indirect_dma_start", out_offset=None, in_offset=off)), ["zb_idx"], [vtk], vtk)
                pr, prk = prod.next()
                kv4 = kt[:].rearrange("p (j d) -> p j d", j=2).unsqueeze(2).broadcast_to([128, 2, 2, 128])
                P.op("dve", "tensor_tensor", [ktk, "zb_in"], [prk], out=pr[:].rearrange("p (j g) d -> p j g d", j=2), in0=kv4,
                     in1=qbc[:, b, :].rearrange("p (j g d) -> p j g d", j=2, g=2), op=ALU.mult)
                s_, sk = Sk.next()
                P.op("dve", "tensor_reduce", [prk], [(sk, 0)], out=s_[:, 0:4], in_=pr[:], axis=AX.X, op=ALU.add)
                P.op("act", "activation", [(sk, 0)], [(sk, 1)], out=s_[:, 4:8], in_=s_[:, 0:4], func=AF.Exp)
                P.op("pe", "matmul", [(sk, 0), "zb_onec"], [pSck], out=pSc[0:4, blk:blk + 1], lhsT=s_[:, 0:4], rhs=onec[:], start=(pg % 2 == 0), stop=(pg % 2 == 1))
                if pg % 2 == 0:
                    pN, pNk = psN.next()
                P.op("pe", "matmul", [(sk, 1), vtk, ("zb_V1", Vt.tiles.index(vt))], [pNk], out=pN[0:4, 0:257], lhsT=s_[:, 4:8], rhs=vt[:], start=(pg % 2 == 0), stop=(pg % 2 == 1))
                if pg % 2 == 1:
                    P.op("act", "copy", [pNk, ("zb_Ow", b - 1)], [("zb_NB", blk)], out=NB[:, blk, :], in_=pN[0:4, 0:257])
            P.op("dve", "tensor_copy", [pSck, "zb_sc"], ["zb_sc"], out=sc[:, 0:NBK], in_=pSc[0:4, 0:NBK])
            P.op("dve", "max", ["zb_sc"], ["zb_mx"], out=mx[:], in_=sc[:])
            P.op("dve", "tensor_scalar", ["zb_mx"], ["zb_mx"], out=mx[:, 2:3], in0=mx[:, 2:3], scalar1=-2.0e4, scalar2=None, op0=ALU.max)
            P.op("dve", "tensor_scalar", ["zb_sc", "zb_mx"], ["zb_sel", ("zb_scw", b)], out=sel[:], in0=sc[:, 0:NBK], scalar1=mx[:, 2:3], scalar2=None, op0=ALU.is_ge)
            nbk = [("zb_NB", blk) for blk in range(NBK)]
            P.op("dve", "tensor_tensor", nbk + ["zb_sel"], ["zb_NBw"], out=NB[:], in0=NB[:], in1=sel[:].unsqueeze(2).broadcast_to([4, NBK, 257]), op=ALU.mult)
            P.op("dve", "tensor_reduce", ["zb_NBw"], ["zb_O", ("zb_Ow", b)], out=O[:], in_=NB[:].rearrange("p b d -> p d b"), axis=AX.X, op=ALU.add)
            P.dma("sp", (T["ZOS"][b], O[:]), ["zb_O"], [("ZOS", b)], "zb_O")
        P.barrier()
    NP = NS * 4
    with ExitStack() as ph:
        O16 = sb(ph, nc, "zc_O", [NP, 257], F32)
        v4 = sb(ph, nc, "zc_v4", [NP, 4, 128], F32)
        s2 = sb(ph, nc, "zc_s2", [NP, 2], F32)
        sc = sb(ph, nc, "zc_sc", [NP, 4], F32)
        q, k, v, tmp = [v4[:, j, :] for j in range(4)]
        vbc = L0_COLS["vb"]
        loads = [(O16[:], T["ZOS"].rearrange("b h d -> (b h) d")), (q, T["ZQB"].rearrange("b (h d) -> (b h) d", h=4)), (s2[:], T["sel2"])]
        for b in range(NS):
            for j in range(2):
                r0 = b * 4 + j * 2
                loads.append((v4[r0:r0 + 2, 1, :], bcast_rows(T["ZKB"][b:b + 1, j * 128:(j + 1) * 128], 2)[:, 0, :]))
                loads.append((v4[r0:r0 + 2, 2, :], bcast_rows(T["ZP0"][b:b + 1, vbc + j * 128:vbc + (j + 1) * 128], 2)[:, 0, :]))
        P.dma("sp", loads, [], ["zc_in"], "zc_in")
        K_ = ["zc_in", "zc_w"]
        def op(eng, name, **kw):
            P.op(eng, name, K_, ["zc_w"], **kw)
        op("dve", "scalar_tensor_tensor", out=tmp, in0=q, scalar=1.0, in1=k, op0=ALU.mult, op1=ALU.mult, accum_out=sc[:, 0:1])
        op("act", "activation", out=sc[:, 1:2], in_=sc[:, 0:1], func=AF.Exp)
        op("dve", "tensor_scalar", out=tmp, in0=O16[:, 0:128], scalar1=s2[:, 0:1], scalar2=None, op0=ALU.mult)
        op("dve", "scalar_tensor_tensor", out=tmp, in0=O16[:, 128:256], scalar=s2[:, 1:2], in1=tmp, op0=ALU.mult, op1=ALU.add)
        op("dve", "scalar_tensor_tensor", out=tmp, in0=v, scalar=sc[:, 1:2], in1=tmp, op0=ALU.mult, op1=ALU.add)
        op("dve", "tensor_tensor", out=sc[:, 2:3], in0=O16[:, 256:257], in1=sc[:, 1:2], op=ALU.add)
        op("dve", "reciprocal", out=sc[:, 2:3], in_=sc[:, 2:3])
        op("dve", "tensor_scalar", out=tmp, in0=tmp, scalar1=sc[:, 2:3], scalar2=None, op0=ALU.mult)
        P.dma("sp", bh_out(T["MIXS0"][:, 512:1024], tmp, 4), K_, ["MIXS0b"], "zc_out")
        P.barrier()


def sample_ssd(nc, P, cfg, T, C, NS):
    epsc = C["epsc"]
    NP = NS * 8
    with ExitStack() as ph:
        h0 = sb(ph, nc, "zs_h0", [NP, 64, 128], F32)
        prod = sb(ph, nc, "zs_prod", [NP, 64, 128], F32)
        Bc = sb(ph, nc, "zs_BC", [NP, 2, 128], F32)
        xv = sb(ph, nc, "zs_x", [NP, 4, 64], F32)
        sc = sb(ph, nc, "zs_sc", [NP, 12], F32)
        hv = sb(ph, nc, "zs_hv", [NP, 3], F32)
        x, hC, y, tmp = [xv[:, j, :] for j in range(4)]
        loads = bh_in(x, T["ZXC"][:, 0:512], 8) + [(sc[:, 0:1], T["ZDT"].rearrange("b (h o) -> (b h) o", o=1)),
                 (hv[:], T["hv32"]), (h0[:], T["ssd_h0_s"].rearrange("b h p n -> (b h) p n"))]
        for b in range(NS):
            for g in range(2):
                r0 = b * 8 + g * 4
                for j in range(2):
                    c0_ = 512 + j * 256 + g * 128
                    loads.append((Bc[r0:r0 + 4, j, :], bcast_rows(T["ZXC"][b:b + 1, c0_:c0_ + 128], 4)[:, 0, :]))
        P.dma("sp", loads, ["ZL1"], ["zs_in"], "zs_in")
        K_ = ["zs_in", "zs_w"]
        col = lambda j: sc[:, j:j + 1]
        def op(eng, name, **kw):
            P.op(eng, name, K_, ["zs_w"], **kw)
        op("dve", "tensor_tensor", out=col(1), in0=col(0), in1=hv[:, 0:1], op=ALU.add)
        op("act", "activation", out=col(1), in_=col(1), func=AF.Exp)
        op("dve", "tensor_scalar", out=col(1), in0=col(1), scalar1=1.0, scalar2=None, op0=ALU.add)
        op("act", "activation", out=col(1), in_=col(1), func=AF.Ln)
        op("act", "activation", out=col(2), in_=hv[:, 1:2], func=AF.Exp)
        op("dve", "scalar_tensor_tensor", out=col(3), in0=col(2), scalar=-1.0, in1=col(1), op0=ALU.mult, op1=ALU.mult)
        op("act", "activation", out=col(3), in_=col(3), func=AF.Exp)
        op("dve", "scalar_tensor_tensor", out=prod[:, 0, :], in0=Bc[:, 0, :], scalar=1.0, in1=Bc[:, 1, :], op0=ALU.mult, op1=ALU.mult, accum_out=col(4))
        op("dve", "tensor_tensor", out=col(5), in0=col(4), in1=col(1), op=ALU.mult)
        op("dve", "tensor_tensor", out=prod[:], in0=h0[:], in1=Bc[:, 1, :].unsqueeze(1).broadcast_to([NP, 64, 128]), op=ALU.mult)
        op("dve", "tensor_reduce", out=hC, in_=prod[:], axis=AX.X, op=ALU.add)
        op("dve", "tensor_scalar", out=y, in0=x, scalar1=col(5), scalar2=None, op0=ALU.mult)
        op("dve", "scalar_tensor_tensor", out=y, in0=hC, scalar=col(3), in1=y, op0=ALU.mult, op1=ALU.add)
        op("dve", "scalar_tensor_tensor", out=y, in0=x, scalar=hv[:, 2:3], in1=y, op0=ALU.mult, op1=ALU.add)
        op("dve", "tensor_tensor", out=prod[:], in0=x.unsqueeze(2).broadcast_to([NP, 64, 128]), in1=Bc[:, 0, :].unsqueeze(1).broadcast_to([NP, 64, 128]), op=ALU.mult)
        op("dve", "tensor_scalar", out=h0[:], in0=h0[:], scalar1=col(3), scalar2=None, op0=ALU.mult)
        op("dve", "scalar_tensor_tensor", out=h0[:], in0=prod[:], scalar=col(1), in1=h0[:], op0=ALU.mult, op1=ALU.add)
        P.dma("sp", [(T["ssd_h_s"].rearrange("b h p n -> (b h) p n"), h0[:]), (T["ZYS"].rearrange("b (h p) -> (b h) p", h=8), y)], K_, ["ZYS"], "zs_out")
        P.barrier()
    with ExitStack() as ph:
        ys = sb(ph, nc, "zt_ys", [NS, 512], F32)
        zs = sb(ph, nc, "zt_zs", [NS, 512], F32)
        nw = sb(ph, nc, "zt_nw", [NS, 512], F32)
        junk = sb(ph, nc, "zt_junk", [NS, 256], F32)
        sc = sb(ph, nc, "zt_sc", [NS, 4], F32)
        P.dma("sp", [(ys[:], T["ZYS"]), (zs[:], T["ZZS"]), (nw[:], bcast_rows(T["norm_ssd"], NS)[:, 0, :])], [], ["zt_in"], "zt_in")
        K_ = ["zt_in", "zt_w"]
        P.op("dve", "tensor_tensor", K_, ["zt_w"], out=ys[:], in0=ys[:], in1=zs[:], op=ALU.mult)
        for g in range(2):
            P.op("act", "activation", K_, ["zt_w"], out=junk[:], in_=ys[:, g * 256:(g + 1) * 256], func=AF.Square, accum_out=sc[:, g:g + 1])
        P.op("act", "activation", K_ + ["epsc"], ["zt_w"], out=sc[:, 2:4], in_=sc[:, 0:2], func=AF.Sqrt, bias=epsc[0:NS, :], scale=1.0 / 256)
        P.op("dve", "reciprocal", K_, ["zt_w"], out=sc[:, 2:4], in_=sc[:, 2:4])
        for g in range(2):
            P.op("dve", "scalar_tensor_tensor", K_, ["zt_w"], out=ys[:, g * 256:(g + 1) * 256], in0=ys[:, g * 256:(g + 1) * 256], scalar=sc[:, 2 + g:3 + g],
                 in1=nw[:, g * 256:(g + 1) * 256], op0=ALU.mult, op1=ALU.mult)
        P.dma("sp", (T["MIXS1"][:, 0:512], ys[:]), K_, ["MIXS1a"], "zt_out")
        P.barrier()


def sample_dil(nc, P, cfg, T, C, NS):
    ps = C["ps"]
    psN, psD = Ring("psN", ps[0:2]), Ring("psD", ps[2:4])
    with ExitStack() as ph:
        qbc = sb(ph, nc, "zd_qbc", [128, NS, 768], F32)
        onec = sb(ph, nc, "zd_onec", [128, 1], F32)
        Kt = Ring("zd_K", [sb(ph, nc, "zd_K%d" % i, [128, 256], F32) for i in range(2)])
        Vt = Ring("zd_V", [sb(ph, nc, "zd_V%d" % i, [128, 256], F32) for i in range(2)])
        prod = Ring("zd_prod", [sb(ph, nc, "zd_prod%d" % i, [128, 4, 64], F32) for i in range(2)])
        Sk = Ring("zd_Sk", [sb(ph, nc, "zd_Sk%d" % i, [128, 8], F32) for i in range(2)])
        ds = Ring("zd_ds", [sb(ph, nc, "zd_ds%d" % i, [1, 260], F32) for i in range(2)])
        P.dma("sp", (qbc[:], bcast_rows(T["ZQD"])), ["ZL1"], ["zd_in"], "zd_in")
        P.op("pool", "memset", [], ["zd_onec"], ap=onec[:], constant=1.0)
        for b in range(NS):
            pN, pNk = psN.next()
            pD, pDk = psD.next()
            for g, (win, dil) in enumerate(SWA):
                kt, ktk = Kt.next()
                vt, vtk = Vt.next()
                src = T["swa_c%d" % g][b].rearrange("(k s) c h d -> s k c (h d)", s=dil)[0]
                P.dma("sp", (kt[:], src[:, 0, :]), [], [ktk], ktk)
                P.dma("sp", (vt[:], src[:, 1, :]), [], [vtk], vtk)
                pr, prk = prod.next()
                P.op("dve", "tensor_tensor", [ktk, "zd_in"], [prk], out=pr[:], in0=kt[:].rearrange("p (h d) -> p h d", h=4),
                     in1=qbc[:, b, g * 256:(g + 1) * 256].rearrange("p (h d) -> p h d", h=4), op=ALU.mult)
                s_, sk = Sk.next()
                P.op("dve", "tensor_reduce", [prk], [(sk, 0)], out=s_[:, 0:4], in_=pr[:], axis=AX.X, op=ALU.add)
                P.op("act", "activation", [(sk, 0)], [(sk, 1)], out=s_[:, 4:8], in_=s_[:, 0:4], func=AF.Exp)
                P.op("dve", "tensor_tensor", [vtk, (sk, 1), prk], [prk], out=pr[:], in0=vt[:].rearrange("p (h d) -> p h d", h=4),
                     in1=s_[:, 4:8].unsqueeze(2).broadcast_to([128, 4, 64]), op=ALU.mult)
                P.op("pe", "matmul", [prk, "zd_onec"], [pNk], out=pN[0:1, 0:256], lhsT=onec[:], rhs=pr[:].rearrange("p h d -> p (h d)"), start=(g == 0), stop=(g == 2))
                P.op("pe", "matmul", [(sk, 1), "zd_onec"], [pDk], out=pD[0:1, 0:4], lhsT=onec[:], rhs=s_[:, 4:8], start=(g == 0), stop=(g == 2))
            d_, dk = ds.next()
            P.op("act", "copy", [pNk], [(dk, 0)], out=d_[:, 0:256], in_=pN[0:1, 0:256])
            P.op("dve", "tensor_copy", [pDk], [(dk, 1)], out=d_[:, 256:260], in_=pD[0:1, 0:4])
            P.dma("sp", (T["ZDS"][b:b + 1, :], d_[:]), [(dk, 0), (dk, 1)], [("ZDS", b)], dk)
        P.barrier()
    with ExitStack() as ph:
        qkv = sb(ph, nc, "ze_qkv", [NS, 3, 768], F32)
        D_ = sb(ph, nc, "ze_D", [NS, 260], F32)
        pn = sb(ph, nc, "ze_pn", [NS, 16], F32)
        acc = sb(ph, nc, "ze_acc", [NS, 260], F32)
        P.dma("sp", [(qkv[:, 0, :], T["ZQD"]), (qkv[:, 1, :], T["ZKD"]), (qkv[:, 2, :], T["ZVD"]), (D_[:], T["ZDS"])], [], ["ze_in"], "ze_in")
        K_ = ["ze_in", "ze_w"]
        def op(eng, name, **kw):
            P.op(eng, name, K_, ["ze_w"], **kw)
        op("dve", "tensor_tensor", out=qkv[:, 0, :], in0=qkv[:, 0, :], in1=qkv[:, 1, :], op=ALU.mult)
        op("dve", "tensor_reduce", out=pn[:, 0:12], in_=qkv[:, 0, :].rearrange("p (h d) -> p h d", h=12), axis=AX.X, op=ALU.add)
        op("act", "activation", out=pn[:, 0:12], in_=pn[:, 0:12], func=AF.Exp)
        op("dve", "tensor_tensor", out=qkv[:, 2, :].rearrange("p (h d) -> p h d", h=12), in0=qkv[:, 2, :].rearrange("p (h d) -> p h d", h=12),
           in1=pn[:, 0:12].unsqueeze(2).broadcast_to([NS, 12, 64]), op=ALU.mult)
        op("dve", "tensor_reduce", out=acc[:, 0:256], in_=qkv[:, 2, :].rearrange("p (g s) -> p s g", g=3), axis=AX.X, op=ALU.add)
        op("dve", "tensor_reduce", out=acc[:, 256:260], in_=pn[:, 0:12].rearrange("p (g s) -> p s g", g=3), axis=AX.X, op=ALU.add)
        op("dve", "tensor_tensor", out=acc[:], in0=acc[:], in1=D_[:], op=ALU.add)
        op("dve", "reciprocal", out=acc[:, 256:260], in_=acc[:, 256:260])
        op("dve", "tensor_tensor", out=acc[:, 0:256].rearrange("p (s d) -> p s d", s=4), in0=acc[:, 0:256].rearrange("p (s d) -> p s d", s=4),
           in1=acc[:, 256:260].unsqueeze(2).broadcast_to([NS, 4, 64]), op=ALU.mult)
        P.dma("sp", (T["MIXS1"][:, 512:768], acc[:, 0:256]), K_, ["MIXS1b"], "ze_out")
        P.barrier()


def phase_sample(nc, P, cfg, T, C):
    NS = 4
    NPG = cfg.NPG
    NBK = NPG // 2
    ps = C["ps"]
    psA, psB, psC = Ring("psA", ps[0:4]), Ring("psB", ps[4:6]), Ring("psC", ps[6:8])
    identf, identb, epsc, onesf = C["identf"], C["identb"], C["epsc"], C["onesf"]
    with ExitStack() as outer:
        xT = sb(outer, nc, "z_xT", [128, 8, NS], F32)
        hn = sb(outer, nc, "z_hn", [128, 8, NS], BF16)
        rstd = sb(outer, nc, "z_rstd", [128, NS], F32)
        sqr = Ring("z_sq", [sb(outer, nc, "z_sq%d" % i, [128, NS], F32) for i in range(2)])
        gains = sb(outer, nc, "z_gains", [128, 5, 8], F32)
        xs_tok = sb(outer, nc, "z_xtok", [NS, 1024], F32)
        rope = sb(outer, nc, "z_rope", [NS, 4, 16], F32)
        P.dma("sp", [(xs_tok[:], T["xs"]), (rope[:], T["rope_s"])] +
              [(gains[:, k, :], T[n]) for k, n in enumerate(("g_mix0", "g_ffn0", "g_mix1", "g_ffn1", "g_fin"))], [], ["z_in"], "z_in")
        pt, ptk = psA.next()
        for c in range(8):
            P.op("pe", "transpose", ["z_in", "identf"], [ptk], out=pt[:, c * NS:(c + 1) * NS], in_=xs_tok[:, c * 128:(c + 1) * 128], identity=identf[0:NS, 0:NS])
        P.op("dve", "tensor_copy", [ptk], ["z_xT"], out=xT[:].rearrange("p c n -> p (c n)"), in_=pt[:, 0:8 * NS])

        def do_norm(gi):
            pss, pssk = psC.next()
            for c in range(8):
                q_, qk = sqr.next()
                P.op("act", "activation", ["z_xT"], [qk], out=q_[:], in_=xT[:, c, :], func=AF.Square)
                P.op("pe", "matmul", [qk, "onesf"], [pssk], out=pss[:, 0:NS], lhsT=onesf[:], rhs=q_[:], start=(c == 0), stop=(c == 7))
            P.op("act", "activation", [pssk, "epsc"], ["z_rstd"], out=rstd[:], in_=pss[:, 0:NS], func=AF.Sqrt, bias=epsc[:], scale=1.0 / D)
            P.op("dve", "reciprocal", ["z_rstd"], ["z_rstd"], out=rstd[:], in_=rstd[:])
            for c in range(8):
                P.op("dve", "scalar_tensor_tensor", ["z_xT", "z_in", "z_rstd"], ["z_hn"], out=hn[:, c, :], in0=xT[:, c, :], scalar=gains[:, gi, c:c + 1],
                     in1=rstd[:], op0=ALU.mult, op1=ALU.mult)

        def proj_tok(w, wkey, N, dst, dkey):
            for n0 in range(0, N, 512):
                n1 = min(N, n0 + 512)
                pp, ppk = psA.next()
                for c in range(8):
                    P.op("pe", "matmul", ["z_hn", wkey], [ppk], out=pp[0:NS, 0:n1 - n0], lhsT=hn[:, c, :], rhs=w[:, c, n0:n1], start=(c == 0), stop=(c == 7))
                P.op("act", "copy", [ppk], [dkey], out=dst[:, n0:n1], in_=pp[0:NS, 0:n1 - n0])

        def rope_tok(src, skey, col0, nh, hd, tcol, scale, dst, dkey):
            half = hd // 8
            sv = src[:, col0:col0 + nh * hd].rearrange("p (h d) -> p h d", h=nh)
            dv = dst[:].rearrange("p (h d) -> p h d", h=nh)
            cosb = rope[:, tcol, 0:half].unsqueeze(1).broadcast_to([NS, nh, half])
            sinb = rope[:, tcol + 1, 0:half].unsqueeze(1).broadcast_to([NS, nh, half])
            P.op("dve", "tensor_scalar", [skey], [dkey], out=dst[:], in0=src[:, col0:col0 + nh * hd], scalar1=scale, scalar2=None, op0=ALU.mult)
            t1 = rtmp[:, 0, 0:nh * half].rearrange("p (h d) -> p h d", h=nh)
            t2 = rtmp[:, 1, 0:nh * half].rearrange("p (h d) -> p h d", h=nh)
            x1, x2 = sv[:, :, 0:half], sv[:, :, half:2 * half]
            P.op("dve", "tensor_tensor", [skey, "z_in"], ["z_rtmp"], out=t1, in0=x1, in1=cosb, op=ALU.mult)
            P.op("dve", "tensor_tensor", [skey, "z_in", "z_rtmp"], ["z_rtmp"], out=t2, in0=x2, in1=sinb, op=ALU.mult)
            P.op("dve", "tensor_tensor", ["z_rtmp"], ["z_rtmp"], out=t1, in0=t1, in1=t2, op=ALU.subtract)
            P.op("dve", "tensor_scalar", ["z_rtmp", dkey], [dkey], out=dv[:, :, 0:half], in0=t1, scalar1=scale, scalar2=None, op0=ALU.mult)
            P.op("dve", "tensor_tensor", [skey, "z_in", "z_rtmp"], ["z_rtmp"], out=t1, in0=x2, in1=cosb, op=ALU.mult)
            P.op("dve", "tensor_tensor", [skey, "z_in", "z_rtmp"], ["z_rtmp"], out=t2, in0=x1, in1=sinb, op=ALU.mult)
            P.op("dve", "tensor_tensor", ["z_rtmp"], ["z_rtmp"], out=t1, in0=t1, in1=t2, op=ALU.add)
            P.op("dve", "tensor_scalar", ["z_rtmp", dkey], [dkey], out=dv[:, :, half:2 * half], in0=t1, scalar1=scale, scalar2=None, op0=ALU.mult)

        rtmp = sb(outer, nc, "z_rtmp", [NS, 2, 128], F32)

        def sample_ffn(L, KC, mixname, final):
            with ExitStack() as ph:
                wo = sb(ph, nc, "zf_wo%d" % L, [128, KC, 1024], BF16)
                wup = sb(ph, nc, "zf_wup%d" % L, [128, 8, 5632], BF16)
                wdn = sb(ph, nc, "zf_wdn%d" % L, [128, 22, 1024], BF16)
                mixt = sb(ph, nc, "zf_mixt%d" % L, [NS, KC * 128], F32)
                mixT = sb(ph, nc, "zf_mixT%d" % L, [128, KC, NS], BF16)
                actT = sb(ph, nc, "zf_actT%d" % L, [128, 22, NS], BF16)
                ytok = sb(ph, nc, "zf_ytok%d" % L, [NS, 1024], F32)
                uC = Ring("zf_uC", [sb(ph, nc, "zf_uC%d_%d" % (L, i), [NS, 2, 256], F32) for i in range(2)])
                stC = Ring("zf_stC", [sb(ph, nc, "zf_stC%d_%d" % (L, i), [NS, 2, 2, 256], F32) for i in range(2)])
                cwC = Ring("zf_cwC", [sb(ph, nc, "zf_cwC%d_%d" % (L, i), [NS, 3, 2, 256], F32) for i in range(2)])
                cbC = Ring("zf_cbC", [sb(ph, nc, "zf_cbC%d_%d" % (L, i), [NS, 2, 256], F32) for i in range(2)])
                yC = Ring("zf_yC", [sb(ph, nc, "zf_yC%d_%d" % (L, i), [NS, 2, 256], F32) for i in range(2)])
                tC = Ring("zf_tC", [sb(ph, nc, "zf_tC%d_%d" % (L, i), [NS, 2, 256], F32) for i in range(2)])
                aC = Ring("zf_aC", [sb(ph, nc, "zf_aC%d_%d" % (L, i), [NS, 256], F32) for i in range(2)])
                P.dma("pool", [(wo[:, k, :], T["w_out%d" % L][k * 128:(k + 1) * 128, :]) for k in range(KC)], [], ["zf_wo"], ("zf_wo", L))
                P.dma("pool", [(wup[:, k, :], T["ffn_up%d" % L][k * 128:(k + 1) * 128, :]) for k in range(8)], [], ["zf_wup"], ("zf_wup", L))
                P.dma("pool", [(wdn[:, k, :], T["ffn_down%d" % L][k * 128:(k + 1) * 128, :]) for k in range(22)], [], ["zf_wdn"], ("zf_wdn", L))
                P.dma("sp", (mixt[:], T[mixname]), [], ["zf_in"], "zf_in")
                pt, ptk = psA.next()
                for c in range(KC):
                    P.op("pe", "transpose", ["zf_in", "identf"], [ptk], out=pt[:, c * NS:(c + 1) * NS], in_=mixt[:, c * 128:(c + 1) * 128], identity=identf[0:NS, 0:NS])
                P.op("dve", "tensor_copy", [ptk], ["zf_mixT"], out=mixT[:].rearrange("p c n -> p (c n)"), in_=pt[:, 0:KC * NS])
                for oc in range(8):
                    pp, ppk = psB.next()
                    for k in range(KC):
                        P.op("pe", "matmul", ["zf_mixT", "zf_wo"], [ppk], out=pp[:, 0:NS], lhsT=wo[:, k, oc * 128:(oc + 1) * 128], rhs=mixT[:, k, :], start=(k == 0), stop=(k == KC - 1))
                    P.op("dve", "tensor_tensor", [ppk, "z_xT"], ["z_xT"], out=xT[:, oc, :], in0=pp[:, 0:NS], in1=xT[:, oc, :], op=ALU.add)
                do_norm(1 + 2 * L)
                pA, pAk = psC.next()
                for jj in range(11):
                    cols = [slice(jj * 256, (jj + 1) * 256), slice(2816 + jj * 256, 2816 + (jj + 1) * 256)]
                    u_, uk = uC.next(); st_, stk = stC.next(); cw_, cwk = cwC.next(); cb_, cbk = cbC.next()
                    y_, yk = yC.next(); t_, tk = tC.next(); a_, ak = aC.next()
                    lds = []
                    for e, cs_ in enumerate(cols):
                        lds += [(st_[:, :, e, :], T["ffn_st%d" % L][:, :, cs_]), (cw_[:, :, e, :], bcast_rows(T["ffn_cwr%d" % L][:, cs_], NS)),
                                (cb_[:, e, :], bcast_rows(T["ffn_cbr%d" % L][:, cs_], NS)[:, 0, :])]
                    P.dma("sp", lds, [], [stk, cwk, cbk], stk)
                    for e, cs_ in enumerate(cols):
                        pp, ppk = psA.next()
                        for c in range(8):
                            P.op("pe", "matmul", ["z_hn", "zf_wup"], [ppk], out=pp[0:NS, 0:256], lhsT=hn[:, c, :], rhs=wup[:, c, cs_], start=(c == 0), stop=(c == 7))
                        P.op("act", "copy", [ppk], [(uk, e)], out=u_[:, e, :], in_=pp[0:NS, 0:256])
                    uks = [(uk, 0), (uk, 1)]
                    P.dma("sp", [(T["ffn_conv_s%d" % L][:, 0, cs_], st_[:, 1, e, :]) for e, cs_ in enumerate(cols)] +
                                [(T["ffn_conv_s%d" % L][:, 1, cs_], u_[:, e, :]) for e, cs_ in enumerate(cols)], uks + [stk], [("ffn_conv_s", jj)], uk)
                    P.op("dve", "tensor_tensor", uks + [cwk], [yk], out=y_[:], in0=u_[:], in1=cw_[:, 2, :, :], op=ALU.mult)
                    P.op("dve", "tensor_tensor", [yk, cbk], [yk], out=y_[:], in0=y_[:], in1=cb_[:], op=ALU.add)
                    for k in range(2):
                        P.op("pool", "tensor_tensor", [stk, cwk, tk], [tk], out=t_[:], in0=st_[:, k, :, :], in1=cw_[:, k, :, :], op=ALU.mult)
                        P.op("dve", "tensor_tensor", [yk, tk], [yk], out=y_[:], in0=y_[:], in1=t_[:], op=ALU.add)
                    P.op("act", "activation", [yk, tk], [tk], out=t_[:, 0, :], in_=y_[:, 0, :], func=AF.Silu)
                    P.op("dve", "tensor_tensor", [yk, tk], [ak], out=a_[:], in0=t_[:, 0, :], in1=y_[:, 1, :], op=ALU.mult)
                    for e in range(2):
                        c = jj * 2 + e
                        P.op("pe", "transpose", [ak, "identf"], [pAk], out=pA[:, c * NS:(c + 1) * NS], in_=a_[:, e * 128:(e + 1) * 128], identity=identf[0:NS, 0:NS])
                P.op("dve", "tensor_copy", [pAk], ["zf_actT"], out=actT[:].rearrange("p c n -> p (c n)"), in_=pA[:, 0:22 * NS])
                for oc in range(8):
                    pp, ppk = psB.next()
                    for k in range(22):
                        P.op("pe", "matmul", ["zf_actT", "zf_wdn"], [ppk], out=pp[:, 0:NS], lhsT=wdn[:, k, oc * 128:(oc + 1) * 128], rhs=actT[:, k, :], start=(k == 0), stop=(k == 21))
                    P.op("dve", "tensor_tensor", [ppk, "z_xT"], ["z_xT"], out=xT[:, oc, :], in0=pp[:, 0:NS], in1=xT[:, oc, :], op=ALU.add)
                if final:
                    do_norm(4)
                    for c in range(8):
                        P.op("dve", "scalar_tensor_tensor", ["z_xT", "z_in", "z_rstd"], ["z_xT"], out=xT[:, c, :], in0=xT[:, c, :], scalar=gains[:, 4, c:c + 1],
                             in1=rstd[:], op0=ALU.mult, op1=ALU.mult)
                    for c4 in range(2):
                        pt, ptk = psA.next()
                        for cc in range(4):
                            c = c4 * 4 + cc
                            P.op("pe", "transpose", ["z_xT", "identf"], [ptk], out=pt[0:NS, cc * 128:(cc + 1) * 128], in_=xT[:, c, :], identity=identf[:])
                        P.op("act", "copy", [ptk], ["zf_ytok"], out=ytok[:, c4 * 512:(c4 + 1) * 512], in_=pt[0:NS, :])
                    P.dma("sp", (T["y_s"], ytok[:]), ["zf_ytok"], ["y_s"], "zf_ytok")
                P.barrier()

        with ExitStack() as ph:
            w = sb(ph, nc, "z_w0", [128, 8, N_L0], BF16)
            P0 = sb(ph, nc, "z_P0", [NS, N_L0], F32)
            qb = sb(ph, nc, "z_qb", [NS, 512], F32)
            kb = sb(ph, nc, "z_kb", [NS, 256], F32)
            P.dma("pool", [(w[:, c, :], T["w_in_l0"][c * 128:(c + 1) * 128, :]) for c in range(8)], [], ["z_w0"], "z_w0")
            do_norm(0)
            proj_tok(w, "z_w0", N_L0, P0, "z_P0")
            rope_tok(P0, "z_P0", L0_COLS["qb"], 4, 128, 0, 128.0 ** -0.5, qb, "z_qb")
            rope_tok(P0, "z_P0", L0_COLS["kb"], 2, 128, 0, 1.0, kb, "z_kb")
            P.dma("sp", [(T["moba_k_s"].rearrange("b h d -> b (h d)"), kb[:]), (T["moba_v_s"].rearrange("b h d -> b (h d)"), P0[:, L0_COLS["vb"]:L0_COLS["vb"] + 256]),
                         (T["ZP0"], P0[:]), (T["ZQB"], qb[:]), (T["ZKB"], kb[:])], ["z_P0", "z_qb", "z_kb"], ["ZP0"], "z_P0o")
            P.barrier()
        sample_mlstm(nc, P, cfg, T, C, NS)
        sample_moba(nc, P, cfg, T, C, NS)
        sample_ffn(0, 8, "MIXS0", False)
        with ExitStack() as ph:
            w = sb(ph, nc, "z_w1", [128, 8, N_L1], BF16)
            P1 = sb(ph, nc, "z_P1", [NS, N_L1], F32)
            qd = sb(ph, nc, "z_qd", [NS, 768], F32)
            kd = sb(ph, nc, "z_kd", [NS, 768], F32)
            cst = sb(ph, nc, "z_cst", [NS, 3, 1024], F32)
            cwb = sb(ph, nc, "z_cwb", [NS, 4, 1024], F32)
            cbb = sb(ph, nc, "z_cbb", [NS, 1024], F32)
            xc = sb(ph, nc, "z_xc", [NS, 1024], F32)
            tmp = sb(ph, nc, "z_tmp1", [NS, 1024], F32)
            zs = sb(ph, nc, "z_zs", [NS, 512], F32)
            P.dma("pool", [(w[:, c, :], T["w_in_l1"][c * 128:(c + 1) * 128, :]) for c in range(8)], [], ["z_w1"], "z_w1")
            P.dma("sp", [(cst[:], T["ssd_conv_st"]), (cwb[:], bcast_rows(T["ssd_cwr"], NS)), (cbb[:], bcast_rows(T["ssd_cbr"], NS)[:, 0, :])], [], ["z_l1in"], "z_l1in")
            do_norm(2)
            proj_tok(w, "z_w1", N_L1, P1, "z_P1")
            rope_tok(P1, "z_P1", L1C["qd"], 12, 64, 2, 0.125, qd, "z_qd")
            rope_tok(P1, "z_P1", L1C["kd"], 12, 64, 2, 1.0, kd, "z_kd")
            xr = P1[:, L1C["xbc"]:L1C["xbc"] + 1024]
            P.op("dve", "tensor_tensor", ["z_P1", "z_l1in"], ["z_xc"], out=xc[:], in0=xr, in1=cwb[:, 3, :], op=ALU.mult)
            P.op("dve", "tensor_tensor", ["z_xc", "z_l1in"], ["z_xc"], out=xc[:], in0=xc[:], in1=cbb[:], op=ALU.add)
            for k in range(3):
                P.op("pool", "tensor_tensor", ["z_l1in", "z_tmp1"], ["z_tmp1"], out=tmp[:], in0=cst[:, k, :], in1=cwb[:, k, :], op=ALU.mult)
                P.op("dve", "tensor_tensor", ["z_xc", "z_tmp1"], ["z_xc"], out=xc[:], in0=xc[:], in1=tmp[:], op=ALU.add)
            P.op("act", "activation", ["z_xc"], ["z_xc"], out=xc[:], in_=xc[:], func=AF.Silu)
            P.op("act", "activation", ["z_P1"], ["z_zs"], out=zs[:], in_=P1[:, 0:512], func=AF.Silu)
            vd = P1[:, L1C["vd"]:L1C["vd"] + 768]
            outs = [(T["ssd_conv_s"][:, 0, :], cst[:, 1, :]), (T["ssd_conv_s"][:, 1, :], cst[:, 2, :]), (T["ssd_conv_s"][:, 2, :], xr),
                    (T["ZXC"], xc[:]), (T["ZZS"], zs[:]), (T["ZDT"], P1[:, L1C["dt"]:L1C["dt"] + 8]), (T["ZQD"], qd[:]), (T["ZKD"], kd[:]), (T["ZVD"], vd)]
            for g in range(3):
                outs.append((T["swa_s%d" % g][:, 0, :, :].rearrange("b h d -> b (h d)"), kd[:, g * 256:(g + 1) * 256]))
                outs.append((T["swa_s%d" % g][:, 1, :, :].rearrange("b h d -> b (h d)"), vd[:, g * 256:(g + 1) * 256]))
            P.dma("sp", outs, ["z_P1", "z_qd", "z_kd", "z_xc", "z_zs", "z_l1in"], ["ZL1"], "z_P1o")
            P.barrier()
        sample_ssd(nc, P, cfg, T, C, NS)
        sample_dil(nc, P, cfg, T, C, NS)
        sample_ffn(1, 6, "MIXS1", True)


OUT_SHAPES = [
    (2, 8192, 1024), (32, 1, 1024), (2, 4, 128, 128), (32, 4, 128, 128), (2, 4, 128), (32, 4, 128), (2, 4), (32, 4),
    (2, 8192, 2, 128), (32, 1, 2, 128), (2, 8192, 2, 128), (32, 1, 2, 128), (2, 8, 64, 128), (32, 8, 64, 128),
    (2, 3, 1024), (32, 3, 1024), (2, 128, 2, 4, 64), (32, 1, 2, 4, 64), (2, 512, 2, 4, 64), (32, 1, 2, 4, 64),
    (2, 2048, 2, 4, 64), (32, 1, 2, 4, 64), (2, 2, 2, 5632), (2, 32, 2, 5632)]


def host_consts(S):
    pos = np.arange(S)
    cos, sin = rope_tables(pos, 128)
    sc = np.float32(128.0 ** -0.5)
    tt = np.arange(128)
    cb = np.where(tt[None, :] <= tt[:, None], 0.0, NEG).astype(np.float32)
    c16, s16 = rope_tables(pos, 64)
    cosD = np.ones((128, S), np.float32)
    sinD = np.zeros((128, S), np.float32)
    for e_ in (0, 64):
        cosD[e_:e_ + 16] = c16
        sinD[e_:e_ + 16] = s16
    return dict(ident_f=np.eye(128, dtype=np.float32), ones_f=np.ones((128, 128), np.float32), cbias=cb,
                cosD=cosD, sinD=sinD, cosDq=(cosD * np.float32(0.125)).astype(np.float32), sinDq=(sinD * np.float32(0.125)).astype(np.float32),
                ident_b=np.eye(128, dtype=np.float32).astype(ml_dtypes.bfloat16),
                Eall=np.ascontiguousarray(np.broadcast_to(np.eye(32, dtype=np.float32)[:, :, None], (32, 32, 128))).astype(ml_dtypes.bfloat16),
                triT=(tt[:, None] <= tt[None, :]).astype(np.float32).astype(ml_dtypes.bfloat16),
                triR=(tt[:, None] >= tt[None, :]).astype(np.float32).astype(ml_dtypes.bfloat16),
                cos32=cos, sin32=sin, cos32q=(cos * sc).astype(np.float32), sin32q=(sin * sc).astype(np.float32))


def fm(v):
    v = np.asarray(v, np.float32)
    return np.ascontiguousarray(v.reshape(-1, 128).T)


def make_inmap(inp, x, S, state=None):
    m = dict(host_consts(S))
    m["x"] = np.ascontiguousarray(x)
    m["w_in_l0"] = np.ascontiguousarray(inp["w_in_l0"])
    m["g_mix0"] = fm(inp["norm_mix"][0])
    bg = np.asarray(inp["b_gates_l0"], np.float32)
    m["b_gates0"] = np.ascontiguousarray(bg.reshape(2, 4).T)
    m["mlstm_m0"] = np.zeros((4, 1), np.float32)
    m["mlstm_c0"] = np.zeros((4, 128, 128), np.float32)
    m["mlstm_n0"] = np.zeros((4, 128), np.float32)
    m["norm_mlstm"] = np.asarray(inp["norm_mlstm_l0"], np.float32).reshape(1, 512)
    m["w_out0"] = np.ascontiguousarray(inp["w_out_l0"])
    m["w_out1"] = np.ascontiguousarray(inp["w_out_l1"])
    for L in range(2):
        m["ffn_up%d" % L] = np.ascontiguousarray(inp["ffn_up"][L])
        m["ffn_down%d" % L] = np.ascontiguousarray(inp["ffn_down"][L])
        m["ffn_cw%d" % L] = np.ascontiguousarray(np.asarray(inp["ffn_conv_w"][L], np.float32).reshape(3, 44, 128).transpose(2, 0, 1))
        m["ffn_cb%d" % L] = fm(inp["ffn_conv_b"][L])
        m["g_ffn%d" % L] = fm(inp["norm_ffn"][L])
        m["ffn_tl%d" % L] = np.zeros((128, 2, 44), np.float32)
    m["g_fin"] = fm(inp["norm_final"])
    m["w_in_l1"] = np.ascontiguousarray(inp["w_in_l1"])
    m["g_mix1"] = fm(inp["norm_mix"][1])
    m["ssd_cw"] = np.ascontiguousarray(np.asarray(inp["conv_w_l1"], np.float32).reshape(4, 8, 128).transpose(2, 0, 1))
    m["ssd_cb"] = fm(inp["conv_b_l1"])
    m["ssd_tl"] = np.zeros((128, 3, 8), np.float32)
    m["dt_bias"] = np.asarray(inp["dt_bias_l1"], np.float32).reshape(8, 1)
    m["a_log"] = np.asarray(inp["a_log_l1"], np.float32).reshape(8, 1)
    m["d_skip"] = np.asarray(inp["d_skip_l1"], np.float32).reshape(1, 8)
    m["norm_ssd"] = np.asarray(inp["norm_ssd_l1"], np.float32).reshape(1, 512)
    m["ssd_h0"] = np.zeros((8, 64, 128), np.float32)
    return m


def make_sample_inmap(inp, sl, past_len):
    f = lambda a: np.ascontiguousarray(np.asarray(a, np.float32))
    m = {}
    m["xs"] = f(inp["x_sample"][sl, 0])
    pos = np.array([past_len])
    c128, s128 = rope_tables(pos, 128)
    c64, s64 = rope_tables(pos, 64)
    r = np.zeros((4, 16), np.float32)
    r[0, :16] = c128[:16, 0]; r[1, :16] = s128[:16, 0]; r[2, :8] = c64[:8, 0]; r[3, :8] = s64[:8, 0]
    m["rope_s"] = np.ascontiguousarray(np.broadcast_to(r[None], (4, 4, 16)))
    m["mlstm_c0_s"] = f(inp["state_l0_mlstm_c"][sl]); m["mlstm_n0_s"] = f(inp["state_l0_mlstm_n"][sl])
    m["mlstm_m0_s"] = f(inp["state_l0_mlstm_m"][sl]).reshape(16, 1)
    bg = np.asarray(inp["b_gates_l0"], np.float32).reshape(2, 4).T
    m["bg16"] = np.ascontiguousarray(np.tile(bg, (4, 1)))
    m["nw16"] = np.ascontiguousarray(np.tile(np.asarray(inp["norm_mlstm_l0"], np.float32).reshape(4, 128), (4, 1)))
    m["sel2"] = np.ascontiguousarray(np.tile(np.array([[1, 0], [1, 0], [0, 1], [0, 1]], np.float32), (4, 1)))
    hv = np.stack([np.asarray(inp["dt_bias_l1"], np.float32), np.asarray(inp["a_log_l1"], np.float32), np.asarray(inp["d_skip_l1"], np.float32)], 1)
    m["hv32"] = np.ascontiguousarray(np.tile(hv, (4, 1)))
    ck = np.asarray(inp["cache_l0_moba_k"]); cv = np.asarray(inp["cache_l0_moba_v"])
    m["cache_k"] = ck.reshape(ck.shape[0] * 128, 256); m["cache_v"] = cv.reshape(cv.shape[0] * 128, 256)
    m["page_table_s"] = np.ascontiguousarray(np.asarray(inp["page_table"], np.int32)[sl])
    m["iota_p"] = np.arange(128, dtype=np.float32).reshape(128, 1)
    for L in range(2):
        m["ffn_st%d" % L] = f(inp["state_ffn_conv"][L][sl])
        m["ffn_cwr%d" % L] = f(inp["ffn_conv_w"][L]); m["ffn_cbr%d" % L] = f(inp["ffn_conv_b"][L]).reshape(1, 5632)
    m["ssd_conv_st"] = f(inp["state_l1_ssd_conv"][sl]); m["ssd_cwr"] = f(inp["conv_w_l1"]); m["ssd_cbr"] = f(inp["conv_b_l1"]).reshape(1, 1024)
    m["ssd_h0_s"] = f(inp["state_l1_ssd_h"][sl])
    for g in range(3):
        m["swa_c%d" % g] = f(inp["cache_l1_swa_kv%d" % g][sl])
    return m


_NC_CACHE = {}


ALL_PHASES = ("A0", "A1", "A2", "A3", "B0", "B1", "B2", "B3")


def kernel(**inp):
    S = 8192
    inp = {k: np.asarray(v) for k, v in inp.items()}
    n_pool = inp["cache_l0_moba_k"].shape[0]
    n_pages = inp["page_table"].shape[1]
    cfg = Cfg(S=S, debug=False, phases=ALL_PHASES + ("SMP",), NPG=n_pages, NPOOL=n_pool)
    if "nc" not in _NC_CACHE:
        _NC_CACHE["nc"] = build(cfg)
    nc = _NC_CACHE["nc"]
    base = make_inmap(inp, inp["x_prompt"][0], S)
    in_maps = []
    for c in range(8):
        m = dict(base)
        m["x"] = np.ascontiguousarray(inp["x_prompt"][c % 2])
        m.update(make_sample_inmap(inp, slice(4 * c, 4 * c + 4), n_pages * 128))
        in_maps.append(m)
    res = run_bass_kernel_spmd(nc, in_maps, core_ids=list(range(8)))
    R = res.results
    outs = [None] * 24
    two = lambda name: np.stack([np.asarray(R[0][name]), np.asarray(R[1][name])])
    cat = lambda name: np.concatenate([np.asarray(R[c][name]) for c in range(8)], 0)
    outs[0] = two("y")
    outs[1] = cat("y_s").reshape(32, 1, 1024)
    outs[2] = two("mlstm_c")
    outs[3] = cat("mlstm_c_s")
    outs[4] = two("mlstm_n")
    outs[5] = cat("mlstm_n_s")
    outs[6] = two("mlstm_m")[:, :, 0]
    outs[7] = cat("mlstm_m_s").reshape(32, 4)
    outs[8] = two("moba_k")
    outs[9] = cat("moba_k_s").reshape(32, 1, 2, 128)
    outs[10] = two("moba_v")
    outs[11] = cat("moba_v_s").reshape(32, 1, 2, 128)
    outs[12] = two("ssd_h")
    outs[13] = cat("ssd_h_s")
    outs[14] = two("ssd_conv")
    outs[15] = cat("ssd_conv_s")
    for g in range(3):
        outs[16 + 2 * g] = two("swa_kv%d" % g)
        outs[17 + 2 * g] = cat("swa_s%d" % g).reshape(32, 1, 2, 4, 64)
    outs[22] = np.stack([two("ffn_conv0"), two("ffn_conv1")])
    outs[23] = np.stack([cat("ffn_conv_s0"), cat("ffn_conv_s1")])
    return tuple(np.ascontiguousarray(o, dtype=np.float32) for o in outs)
```
